# Optimizing a Trainium2 kernel written in Bass

```python
import math
import jax
import jax.numpy as jnp
from jax import lax
import numpy as np

D_MODEL = 2048
BATCH = 4
SEQ = 2048
DEPTH = 4
DEC_BATCH = 8
DEC_SEQ = 64
PAST_LEN = 4096

CHUNK = 64
Q_BLOCK = 128
N_MIXERS = 3
NA_LAYERS = (DEPTH + 2) // 3
NB_LAYERS = (DEPTH + 1) // 3
NC_LAYERS = DEPTH // 3
D_MIX = D_MODEL
ROPE_THETA = 10000.0
RMS_EPS = 1e-6
DH_A = 64
H_A = D_MIX // (2 * DH_A)
SUBLN_EPS = 1e-5
HD_B = 128
H_B = D_MIX // HD_B
KV_B = 4
H_IDX = 16
D_IDX = 64
TOPK_MAX = 256
B_SIZES = (H_B * HD_B, KV_B * HD_B, KV_B * HD_B, H_IDX * D_IDX, D_IDX, H_IDX, D_MIX)
B_IN = sum(B_SIZES)
HS_C = 64
H_C = D_MIX // HS_C
R_DECAY = 96
R_ICLR = 96
GN_EPS = 64e-5

kernel_name = 'hybrid_chunk_stream_diffattn_dsa_rwkv7'


def _rmsnorm(x, g, eps=RMS_EPS):
    xf = x.astype(jnp.float32)
    y = xf * lax.rsqrt(jnp.mean(xf * xf, axis=-1, keepdims=True) + eps)
    return (y * g.astype(jnp.float32)).astype(x.dtype)


def _rope(x, pos):
    dh = x.shape[-1]
    half = dh // 2
    inv = jnp.power(ROPE_THETA, -jnp.arange(half, dtype=jnp.float32) * (2.0 / dh))
    ang = pos.astype(jnp.float32)[:, None] * inv[None, :]
    shape = (ang.shape[0],) + (1,) * (x.ndim - 3) + (half,)
    cos = jnp.cos(ang).reshape(shape)
    sin = jnp.sin(ang).reshape(shape)
    xf = x.astype(jnp.float32)
    x1, x2 = xf[..., :half], xf[..., half:]
    return jnp.concatenate([x1 * cos - x2 * sin, x2 * cos + x1 * sin], axis=-1).astype(x.dtype)


def _over_query_blocks(fn, *qs):
    t = qs[0].shape[1]
    if t <= Q_BLOCK or t % Q_BLOCK:
        return fn(*qs)
    nb = t // Q_BLOCK
    blocks = tuple(jnp.moveaxis(a.reshape((a.shape[0], nb, Q_BLOCK) + a.shape[2:]), 1, 0) for a in qs)
    out = lax.map(lambda args: fn(*args), blocks)
    out = jnp.moveaxis(out, 0, 1)
    return out.reshape((out.shape[0], t) + out.shape[3:])


def _diff_attn(h, pos, k_past, v_past, w_in, lam_p, subln_g, lam_init):
    b, t, _ = h.shape
    q, k, v, g = jnp.split(h @ w_in, 4, axis=-1)
    q = _rope(q.reshape(b, t, H_A, 2, DH_A), pos)
    k = _rope(k.reshape(b, t, H_A, 2, DH_A), pos)
    k_rows = k.reshape(b, t, H_A, 2 * DH_A)
    v_rows = v.reshape(b, t, H_A, 2 * DH_A)
    if k_past is None:
        k_all, v_all = k_rows, v_rows
    else:
        k_all = jnp.concatenate([k_past.astype(k_rows.dtype), k_rows], axis=1)
        v_all = jnp.concatenate([v_past.astype(v_rows.dtype), v_rows], axis=1)
    s = k_all.shape[1]
    kpos = jnp.arange(s)
    k_all = k_all.reshape(b, s, H_A, 2, DH_A)
    k1, k2 = k_all[..., 0, :], k_all[..., 1, :]
    lp = lam_p.astype(jnp.float32)
    lam = jnp.exp(jnp.sum(lp[0] * lp[1])) - jnp.exp(jnp.sum(lp[2] * lp[3])) + lam_init

    def core(q1, q2, qp):
        mask = (kpos // CHUNK)[None, :] <= (qp[0] // CHUNK)[:, None]
        def attn_map(qq, kk):
            sc = jnp.einsum('bqhd,bkhd->bhqk', qq.astype(jnp.float32), kk.astype(jnp.float32)) * DH_A ** -0.5
            return jax.nn.softmax(jnp.where(mask, sc, -jnp.inf), axis=-1)
        p = attn_map(q1, k1) - lam * attn_map(q2, k2)
        return jnp.einsum('bhqk,bkhe->bqhe', p, v_all.astype(jnp.float32))

    o = _over_query_blocks(core, q[..., 0, :], q[..., 1, :], pos[None])
    o = o * lax.rsqrt(jnp.mean(o * o, axis=-1, keepdims=True) + SUBLN_EPS) * subln_g.astype(jnp.float32)
    o = (o * (1.0 - lam_init)).reshape(b, t, D_MIX)
    return (o * jax.nn.silu(g.astype(jnp.float32))).astype(h.dtype), k_rows, v_rows


def _dsa_attn(h, pos, k_past, v_past, ki_past, w_in):
    b, t, _ = h.shape
    offs = np.cumsum(B_SIZES)[:-1].tolist()
    q, k, v, qi, ki, wi, g = jnp.split(h @ w_in, offs, axis=-1)
    q = _rope(q.reshape(b, t, H_B, HD_B), pos)
    k_rows = _rope(k.reshape(b, t, KV_B, HD_B), pos)
    v_rows = v.reshape(b, t, KV_B, HD_B)
    qi = _rope(qi.reshape(b, t, H_IDX, D_IDX), pos)
    ki_rows = _rope(ki, pos)
    wi = wi * H_IDX ** -0.5
    if k_past is None:
        k_all, v_all, ki_all = k_rows, v_rows, ki_rows
    else:
        k_all = jnp.concatenate([k_past.astype(k_rows.dtype), k_rows], axis=1)
        v_all = jnp.concatenate([v_past.astype(v_rows.dtype), v_rows], axis=1)
        ki_all = jnp.concatenate([ki_past.astype(ki_rows.dtype), ki_rows], axis=1)
    s = k_all.shape[1]
    kpos = jnp.arange(s)
    n_sel = min(TOPK_MAX, s // 4)
    take = jax.vmap(lambda rows, ix: rows[ix])

    def core(q_, qi_, wi_, qp):
        qp = qp[0]
        tq = q_.shape[1]
        adm = (kpos // CHUNK)[None, :] <= (qp // CHUNK)[:, None]
        logits = jnp.einsum('bqhd,bsd->bqhs', qi_.astype(jnp.float32), ki_all.astype(jnp.float32)) * D_IDX ** -0.5
        score = jnp.einsum('bqh,bqhs->bqs', wi_.astype(jnp.float32), jax.nn.relu(logits))
        score = jnp.where(adm[None], score, -jnp.inf)
        _, idx = lax.top_k(score, n_sel)
        valid = (idx // CHUNK) <= (qp // CHUNK)[None, :, None]
        kg = take(k_all, idx).astype(jnp.float32)
        vg = take(v_all, idx).astype(jnp.float32)
        qg = q_.astype(jnp.float32).reshape(b, tq, KV_B, H_B // KV_B, HD_B)
        sc = jnp.einsum('bqgrd,bqngd->bqgrn', qg, kg) * HD_B ** -0.5
        p = jax.nn.softmax(jnp.where(valid[:, :, None, None, :], sc, -jnp.inf), axis=-1)
        return jnp.einsum('bqgrn,bqngd->bqgrd', p, vg).reshape(b, tq, D_MIX)

    o = _over_query_blocks(core, q, qi, wi, pos[None])
    return (o * jax.nn.silu(g.astype(jnp.float32))).astype(h.dtype), k_rows, v_rows, ki_rows


def _rwkv_scan(r, w, k, v, kk, a, s0):
    def step(st, inp):
        r_t, w_t, k_t, v_t, kk_t, a_t = inp
        sa = jnp.einsum('bhvk,bhk->bhv', st, -kk_t)
        st = st * w_t[:, :, None, :] + sa[..., None] * (kk_t * a_t)[:, :, None, :] + v_t[..., None] * k_t[:, :, None, :]
        return st, jnp.einsum('bhvk,bhk->bhv', st, r_t)
    xs = tuple(jnp.moveaxis(u, 1, 0) for u in (r, w, k, v, kk, a))
    s_new, o = lax.scan(step, s0, xs)
    return jnp.moveaxis(o, 0, 1), s_new


def _rwkv_mix(h, shift0, s0, mu, w_rkvg, w0, w_la, w_lb, a0, a_la, a_lb, k_k, k_a, r_k, ln_w, ln_b):
    b, t, d = h.shape
    hf = h.astype(jnp.float32)
    prev = jnp.concatenate([shift0.astype(jnp.float32)[:, None], hf[:, :-1]], axis=1)
    lerp = hf[None] + (prev - hf)[None] * mu.astype(jnp.float32)[:, None, None, :]
    r, k, v, g = jnp.einsum('nbtd,nde->nbte', lerp[:4], w_rkvg.astype(jnp.float32))
    w_log = -jax.nn.softplus(-(w0 + jnp.tanh(lerp[4] @ w_la) @ w_lb)) - 0.5
    decay = jnp.exp(-jnp.exp(w_log))
    a = jax.nn.sigmoid(a0 + (lerp[5] @ a_la) @ a_lb)
    heads = lambda u: u.reshape(b, t, H_C, HS_C)
    kk = heads(k * k_k)
    kk = kk / jnp.maximum(jnp.sqrt(jnp.sum(kk * kk, axis=-1, keepdims=True)), 1e-12)
    k = k * (1.0 + (a - 1.0) * k_a)
    r_h, k_h, v_h, w_h, a_h = heads(r), heads(k), heads(v), heads(decay), heads(a)
    o, s_new = _rwkv_scan(r_h, w_h, k_h, v_h, kk, a_h, s0.astype(jnp.float32))
    mean = jnp.mean(o, axis=-1, keepdims=True)
    var = jnp.mean(jnp.square(o - mean), axis=-1, keepdims=True)
    o = ((o - mean) * lax.rsqrt(var + GN_EPS)).reshape(b, t, d) * ln_w + ln_b
    bonus = jnp.sum(r_h * k_h * r_k, axis=-1, keepdims=True) * v_h
    o = o + bonus.reshape(b, t, d)
    return (o * jax.nn.silu(g)).astype(h.dtype), s_new, h[:, -1]


def setup_inputs(seed: int = 0) -> dict:
    key = jax.random.key(seed)
    ks = iter(jax.random.split(key, 40))
    f32 = jnp.float32
    d = D_MODEL
    nrm = lambda shape, scale: jax.random.normal(next(ks), shape, f32) * scale
    return {
        'x_prompt': nrm((BATCH, SEQ, d), 1.0),
        'x_sample': nrm((DEC_BATCH, DEC_SEQ, d), 1.0),
        'cache_a_k': nrm((NA_LAYERS, DEC_BATCH, PAST_LEN, H_A, 2 * DH_A), 1.0),
        'cache_a_v': nrm((NA_LAYERS, DEC_BATCH, PAST_LEN, H_A, 2 * DH_A), 1.0),
        'cache_b_k': nrm((NB_LAYERS, DEC_BATCH, PAST_LEN, KV_B, HD_B), 1.0),
        'cache_b_v': nrm((NB_LAYERS, DEC_BATCH, PAST_LEN, KV_B, HD_B), 1.0),
        'cache_b_kidx': nrm((NB_LAYERS, DEC_BATCH, PAST_LEN, D_IDX), 1.0),
        'state_c_wkv': nrm((NC_LAYERS, DEC_BATCH, H_C, HS_C, HS_C), 0.3),
        'state_c_shift': nrm((NC_LAYERS, DEC_BATCH, d), 1.0),
        'norm_g': 1.0 + nrm((DEPTH, d), 0.02),
        'final_g': 1.0 + nrm((d,), 0.02),
        'w_out': nrm((DEPTH, D_MIX, d), D_MIX ** -0.5),
        'a_w_in': nrm((NA_LAYERS, d, 4 * D_MIX), d ** -0.5),
        'a_lam': nrm((NA_LAYERS, 4, DH_A), 0.1),
        'a_subln_g': 1.0 + nrm((NA_LAYERS, 2 * DH_A), 0.02),
        'b_w_in': nrm((NB_LAYERS, d, B_IN), d ** -0.5),
        'c_mu': jax.random.uniform(next(ks), (NC_LAYERS, 6, d), f32),
        'c_w_rkvg': nrm((NC_LAYERS, 4, d, D_MIX), d ** -0.5),
        'c_w0': -1.0 + nrm((NC_LAYERS, D_MIX), 0.3),
        'c_w_la': nrm((NC_LAYERS, d, R_DECAY), d ** -0.5),
        'c_w_lb': nrm((NC_LAYERS, R_DECAY, D_MIX), 0.1 * R_DECAY ** -0.5),
        'c_a0': nrm((NC_LAYERS, D_MIX), 0.1),
        'c_a_la': nrm((NC_LAYERS, d, R_ICLR), d ** -0.5),
        'c_a_lb': nrm((NC_LAYERS, R_ICLR, D_MIX), 0.1 * R_ICLR ** -0.5),
        'c_k_k': 0.85 + nrm((NC_LAYERS, D_MIX), 0.02),
        'c_k_a': 1.0 + nrm((NC_LAYERS, D_MIX), 0.02),
        'c_r_k': nrm((NC_LAYERS, H_C, HS_C), 0.1),
        'c_ln_w': 1.0 + nrm((NC_LAYERS, D_MIX), 0.02),
        'c_ln_b': nrm((NC_LAYERS, D_MIX), 0.02),
    }


def reference(x_prompt, x_sample, cache_a_k, cache_a_v, cache_b_k, cache_b_v, cache_b_kidx, state_c_wkv, state_c_shift, norm_g, final_g, w_out, a_w_in, a_lam, a_subln_g, b_w_in, c_mu, c_w_rkvg, c_w0, c_w_la, c_w_lb, c_a0, c_a_la, c_a_lb, c_k_k, c_k_a, c_r_k, c_ln_w, c_ln_b):
    n_prompt = x_prompt.shape[1]
    past = cache_a_k.shape[2]
    n_new = x_sample.shape[1]
    pos_p = jnp.arange(n_prompt)
    pos_s = past + jnp.arange(n_new)
    xp, xs = x_prompt, x_sample
    a_kp, a_vp, a_ks, a_vs = [], [], [], []
    b_kp, b_vp, b_ip, b_ks, b_vs, b_is = [], [], [], [], [], []
    c_wp, c_hp, c_ws, c_hs = [], [], [], []
    for i in range(DEPTH):
        kind, j = i % N_MIXERS, i // N_MIXERS
        hp = _rmsnorm(xp, norm_g[i])
        hs = _rmsnorm(xs, norm_g[i])
        if kind == 0:
            lam_init = 0.8 - 0.6 * math.exp(-0.3 * i)
            op, kr, vr = _diff_attn(hp, pos_p, None, None, a_w_in[j], a_lam[j], a_subln_g[j], lam_init)
            a_kp.append(kr)
            a_vp.append(vr)
            os_, kr, vr = _diff_attn(hs, pos_s, cache_a_k[j], cache_a_v[j], a_w_in[j], a_lam[j], a_subln_g[j], lam_init)
            a_ks.append(kr)
            a_vs.append(vr)
        elif kind == 1:
            op, kr, vr, ir = _dsa_attn(hp, pos_p, None, None, None, b_w_in[j])
            b_kp.append(kr)
            b_vp.append(vr)
            b_ip.append(ir)
            os_, kr, vr, ir = _dsa_attn(hs, pos_s, cache_b_k[j], cache_b_v[j], cache_b_kidx[j], b_w_in[j])
            b_ks.append(kr)
            b_vs.append(vr)
            b_is.append(ir)
        else:
            cp = (c_mu[j], c_w_rkvg[j], c_w0[j], c_w_la[j], c_w_lb[j], c_a0[j], c_a_la[j], c_a_lb[j], c_k_k[j], c_k_a[j], c_r_k[j], c_ln_w[j], c_ln_b[j])
            bp = xp.shape[0]
            op, sw, sh = _rwkv_mix(hp, jnp.zeros((bp, D_MODEL), hp.dtype), jnp.zeros((bp, H_C, HS_C, HS_C), jnp.float32), *cp)
            c_wp.append(sw)
            c_hp.append(sh)
            os_, sw, sh = _rwkv_mix(hs, state_c_shift[j], state_c_wkv[j], *cp)
            c_ws.append(sw)
            c_hs.append(sh)
        xp = xp + op @ w_out[i]
        xs = xs + os_ @ w_out[i]
    y_prompt = _rmsnorm(xp, final_g)
    y_sample = _rmsnorm(xs, final_g)
    return (y_prompt, y_sample,
            jnp.stack(a_kp), jnp.stack(a_vp), jnp.stack(a_ks), jnp.stack(a_vs),
            jnp.stack(b_kp), jnp.stack(b_vp), jnp.stack(b_ip), jnp.stack(b_ks), jnp.stack(b_vs), jnp.stack(b_is),
            jnp.stack(c_wp), jnp.stack(c_hp), jnp.stack(c_ws), jnp.stack(c_hs))
```

```python
import os
import math
import numpy as np
from contextlib import ExitStack
import concourse.bass as bass
import concourse.mybir as mybir
from concourse.bass_utils import run_bass_kernel_spmd

F32 = mybir.dt.float32
BF16 = mybir.dt.bfloat16
AF = mybir.ActivationFunctionType
ALU = mybir.AluOpType
AX = mybir.AxisListType

D = 2048
NT = 17
NTOK = 2112
PAST = 4096
DEPTH = 4


def tn(t):
    return 128 if t < 16 else 64


class Buf:
    __slots__ = ("name", "lw", "rd")

    def __init__(self, name=""):
        self.name = name
        self.lw = None
        self.rd = []


class T:
    def __init__(self, t, nb=1, name=""):
        self.t = t
        self.bs = [Buf(name + str(i)) for i in range(nb)]

    def __getitem__(self, k):
        return self.t[k]


class Prog:
    ENGS = ("pe", "act", "dve", "pool", "sp")
    NDMA = {"sp": 12, "act": 4, "pool": 8}

    def __init__(self, nc, stack):
        self.nc = nc
        self.stack = stack
        self.q = {e: [] for e in self.ENGS}
        self.sem = {e: stack.enter_context(nc.semaphore("s_" + e)) for e in self.ENGS}
        self.cnt = {e: 0 for e in self.ENGS}
        self.seen = {e: {} for e in self.ENGS}
        self.pend = {e: {} for e in self.ENGS}
        self.dsem = {}
        self.dcnt = {}
        self.di = {}
        for e, n in self.NDMA.items():
            self.dsem[e] = [stack.enter_context(nc.semaphore("d_%s%d" % (e, i))) for i in range(n)]
            self.dcnt[e] = [0] * n
            self.di[e] = 0
        self.n_ins = 0
        self.uid = 0

    def sb(self, shape, dt, nb=1, name=None, stack=None):
        self.uid += 1
        name = name or "t%d" % self.uid
        t = (stack or self.stack).enter_context(self.nc.sbuf_tensor(name, list(shape), dt))
        return T(t, nb, name)

    def ps(self, shape, dt=F32, nb=1, name=None, stack=None):
        self.uid += 1
        name = name or "p%d" % self.uid
        t = (stack or self.stack).enter_context(self.nc.psum_tensor(name, list(shape), dt))
        return T(t, nb, name)

    def _bufs(self, xs):
        out = []
        for x in xs:
            if isinstance(x, T):
                out.extend(x.bs)
            elif isinstance(x, Buf):
                out.append(x)
            elif x is None:
                pass
            else:
                out.extend(self._bufs(x))
        return out

    def fence(self):
        snap = {}
        for e in self.ENGS:
            if self.cnt[e] > 0:
                snap[("c", e)] = self.cnt[e]
        for e in self.NDMA:
            for i, c in enumerate(self.dcnt[e]):
                if c > 0:
                    snap[("d", e, i)] = c
        for e in self.ENGS:
            for k, v in snap.items():
                if e == "pe" and k == ("c", "pe"):
                    continue
                if self.pend[e].get(k, 0) < v:
                    self.pend[e][k] = v

    def op(self, eng, fn, reads=(), writes=(), dma=False):
        reads = self._bufs(reads)
        writes = self._bufs(writes)
        deps = dict(self.pend[eng])
        self.pend[eng] = {}

        def add(ev):
            if ev is None:
                return
            k, v = ev
            if eng == "pe" and k == ("c", "pe"):
                return
            if deps.get(k, 0) < v:
                deps[k] = v

        for b in reads:
            add(b.lw)
        for b in writes:
            add(b.lw)
            for r in b.rd:
                add(r)
        if dma:
            i = self.di[eng] % len(self.dsem[eng])
            self.di[eng] += 1
            if self.dcnt[eng][i] > 0:
                add((("d", eng, i), self.dcnt[eng][i]))
            self.dcnt[eng][i] += 16
            ev = (("d", eng, i), self.dcnt[eng][i])
        else:
            self.cnt[eng] += 1
            ev = (("c", eng), self.cnt[eng])
        waits = []
        seen = self.seen[eng]
        for k, v in deps.items():
            if seen.get(k, 0) < v:
                seen[k] = v
                waits.append((k, v))
        self.q[eng].append((waits, fn, ev))
        for b in reads:
            b.rd.append(ev)
            if len(b.rd) > 64:
                m = {}
                for k, v in b.rd:
                    if m.get(k, 0) < v:
                        m[k] = v
                b.rd = list(m.items())
        for b in writes:
            b.lw = ev
            b.rd = []
        self.n_ins += 1
        return ev

    def _semof(self, k):
        if k[0] == "c":
            return self.sem[k[1]]
        return self.dsem[k[1]][k[2]]

    def emit(self):
        nc = self.nc
        fin = []
        for e in self.NDMA:
            for i, c in enumerate(self.dcnt[e]):
                if c > 0:
                    fin.append((("d", e, i), c))
        for e in self.ENGS:
            if e != "sp" and self.cnt[e] > 0:
                fin.append((("c", e), self.cnt[e]))
        engobj = {"pe": "tensor", "act": "scalar", "dve": "vector", "pool": "gpsimd", "sp": "sync"}
        with nc.Block() as block:
            for e in self.ENGS:
                def body(engine, e=e):
                    for waits, fn, ev in self.q[e]:
                        for k, v in waits:
                            engine.wait_ge(self._semof(k), v)
                        ins = fn(engine)
                        ins.then_inc(self._semof(ev[0]), 16 if ev[0][0] == "d" else 1)
                    if e == "sp":
                        for k, v in fin:
                            engine.wait_ge(self._semof(k), v)
                getattr(block, engobj[e])(body)

    def dma(self, out, in_, reads=(), writes=(), eng="sp", **kw):
        return self.op(eng, lambda E: E.dma_start(out=out, in_=in_, **kw), reads, writes, dma=True)

    def mm(self, out, lhsT, rhs, start, stop, reads=(), writes=()):
        return self.op("pe", lambda E: E.matmul(out, lhsT, rhs, start=start, stop=stop), reads, writes)

    def tr(self, out, in_, ident, reads=(), writes=()):
        return self.op("pe", lambda E: E.transpose(out, in_, ident), reads, writes)

    def act(self, out, in_, func, reads=(), writes=(), **kw):
        return self.op("act", lambda E: E.activation(out=out, in_=in_, func=func, **kw), reads, writes)

    def tt(self, eng, out, in0, in1, op, reads=(), writes=()):
        return self.op(eng, lambda E: E.tensor_tensor(out, in0, in1, op), reads, writes)

    def ts(self, eng, out, in0, s1, s2, op0, op1=None, reads=(), writes=(), **kw):
        if op1 is None:
            return self.op(eng, lambda E: E.tensor_scalar(out, in0, s1, s2, op0, **kw), reads, writes)
        return self.op(eng, lambda E: E.tensor_scalar(out, in0, s1, s2, op0, op1, **kw), reads, writes)

    def stt(self, eng, out, in0, scalar, in1, op0, op1, reads=(), writes=()):
        return self.op(eng, lambda E: E.scalar_tensor_tensor(out, in0, scalar, in1, op0, op1), reads, writes)

    def cp(self, eng, out, in_, reads=(), writes=()):
        if eng == "act":
            return self.op(eng, lambda E: E.copy(out, in_), reads, writes)
        return self.op(eng, lambda E: E.tensor_copy(out, in_), reads, writes)


class Ctx:
    pass


def build_nc(stage):
    nc = bass.Bass("TRN2", target_bir_lowering=False)
    C = Ctx()
    C.nc = nc
    dt_in = lambda n, s, d=F32: nc.dram_tensor(n, list(s), d, kind="ExternalInput").ap()
    dt_out = lambda n, s, d=F32: nc.dram_tensor(n, list(s), d, kind="ExternalOutput").ap()
    dt_tmp = lambda n, s, d=F32: nc.dram_tensor(n, list(s), d, kind="Internal").ap()
    C.xin = dt_in("xin", [NTOK, D])
    C.cak = dt_in("cak", [2, PAST, 16 * 128])
    C.cav = dt_in("cav", [2, PAST, 16 * 128])
    C.cbk = dt_in("cbk", [PAST, 4 * 128])
    C.cbv = dt_in("cbv", [PAST, 4 * 128])
    C.cbi = dt_in("cbi", [PAST, 64])
    C.swkv = dt_in("swkv", [32 * 64, 64])
    C.sshift = dt_in("sshift", [1, D])
    C.norm_g = dt_in("norm_g", [DEPTH, D])
    C.final_g = dt_in("final_g", [1, D])
    C.w_out = dt_in("w_out", [DEPTH, D, D])
    C.a_w_in = dt_in("a_w_in", [2, D, 4 * D])
    C.a_lam = dt_in("a_lam", [2, 4 * 64])
    C.a_subln_g = dt_in("a_subln_g", [2, 128])
    C.b_w_in = dt_in("b_w_in", [D, 6224])
    C.c_mu = dt_in("c_mu", [6, D])
    C.c_w_rkvg = dt_in("c_w_rkvg", [4, D, D])
    C.c_w0 = dt_in("c_w0", [1, D])
    C.c_w_la = dt_in("c_w_la", [D, 96])
    C.c_w_lb = dt_in("c_w_lb", [96, D])
    C.c_a0 = dt_in("c_a0", [1, D])
    C.c_a_la = dt_in("c_a_la", [D, 96])
    C.c_a_lb = dt_in("c_a_lb", [96, D])
    C.c_k_k = dt_in("c_k_k", [1, D])
    C.c_k_a = dt_in("c_k_a", [1, D])
    C.c_r_k = dt_in("c_r_k", [1, D])
    C.c_ln_w = dt_in("c_ln_w", [1, D])
    C.c_ln_b = dt_in("c_ln_b", [1, D])
    C.rope64 = dt_in("rope64", [NTOK, 2, 64])
    C.rope128 = dt_in("rope128", [NTOK, 2, 128])
    C.ident_d = dt_in("ident", [128, 128])
    C.y = dt_out("y", [NTOK, D])
    C.o_ak = dt_out("o_ak", [2, NTOK, D])
    C.o_av = dt_out("o_av", [2, NTOK, D])
    C.o_bk = dt_out("o_bk", [NTOK, 512])
    C.o_bv = dt_out("o_bv", [NTOK, 512])
    C.o_bi = dt_out("o_bi", [NTOK, 64])
    C.o_cw = dt_out("o_cw", [2, 32 * 64, 64])
    C.o_cs = dt_out("o_cs", [2, D])
    C.xs = dt_tmp("xs", [NTOK, D])
    C.og = dt_tmp("og", [NTOK, D], BF16)
    C.qid = dt_tmp("qid", [NTOK, 1024], BF16)
    C.qd = dt_tmp("qd", [NTOK, D], BF16)
    C.sgd = dt_tmp("sgd", [NTOK, D], BF16)
    C.maskd = dt_tmp("maskd", [NTOK, 4160], BF16)
    for nm in ("c_r", "c_k", "c_v", "c_sg", "c_bv", "c_At", "c_Bt", "c_Kt", "c_Rt"):
        setattr(C, nm, dt_tmp(nm, [NTOK, D]))
    C.cmask = dt_in("cmask", [5, 128, 128])
    C.selc = dt_in("selc", [128, 2])

    with ExitStack() as st:
        P = Prog(nc, st)
        C.P = P
        C.ident = P.sb([128, 128], F32, name="identf")
        C.identb = P.sb([128, 128], BF16, name="identb")
        P.dma(C.ident[:], C.ident_d[:, :], writes=[C.ident])
        P.cp("dve", C.identb[:], C.ident[:], [C.ident], [C.identb])
        C.gb = P.sb([128, D], F32, name="gb")
        C.eps6 = P.sb([128, 1], F32, name="eps6")
        P.op("dve", lambda E: E.memset(C.eps6[:], 1e-6), [], [C.eps6])
        C.eps5 = P.sb([128, 1], F32, name="eps5")
        P.op("dve", lambda E: E.memset(C.eps5[:], 1e-5), [], [C.eps5])
        C.pp = [P.ps([128, 512], F32, name="pp%d" % i) for i in range(2)]
        C.ptb = P.ps([128, 1024], BF16, name="ptb")
        C.psc = [P.ps([128, 512], F32, name="psc%d" % i) for i in range(2)]
        C.po = [P.ps([128, 512], F32, name="po%d" % i) for i in range(2)]
        C.ptf = P.ps([128, 512], F32, name="ptf")

        x_src = C.xin
        for li in range(DEPTH):
            if li >= stage:
                break
            kind, j = li % 3, li // 3
            with ExitStack() as cst:
                if kind == 2:
                    C.tT = P.sb([128, NTOK], BF16, stack=cst, name='c_tT')
                    C.aT = P.sb([128, NTOK], BF16, stack=cst, name='c_aT')
                    C.gC = P.sb([128, 16, 34], F32, stack=cst, name='c_gC')
                done = False
                with ExitStack() as hs:
                    C.hT = P.sb([128, 16, NTOK], BF16, stack=hs, name="hT%d" % li)
                    with ExitStack() as lst:
                        C.lst = lst
                        phase_norm(C, x_src, C.norm_g[li:li + 1, :], li, want_last=(kind == 2))
                        P.fence()
                    with ExitStack() as lst:
                        C.lst = lst
                        if kind == 0:
                            layer_a(C, li, j)
                            done = True
                        elif kind == 1:
                            done = layer_b(C, li)
                        else:
                            done = layer_c1(C, li)
                        P.fence()
                if kind == 2 and done:
                    with ExitStack() as lst:
                        C.lst = lst
                        layer_c2(C, li)
                        P.fence()
                    with ExitStack() as lst:
                        C.lst = lst
                        layer_c3(C, li)
                        P.fence()
            if not done:
                continue
            with ExitStack() as lst:
                C.lst = lst
                phase_out(C, x_src, li)
                P.fence()
            x_src = C.xs
        with ExitStack() as lst:
            C.lst = lst
            phase_final(C, x_src)
        P.emit()
    return nc


def phase_norm(C, x_src, g_row, li, want_last=False):
    P = C.P
    lst = C.lst
    P.dma(C.gb[:], g_row.partition_broadcast(128), writes=[C.gb])
    xb = [P.sb([128, D], F32, stack=lst) for _ in range(2)]
    junk = P.sb([128, D], BF16, stack=lst)
    hb = [P.sb([128, D], BF16, stack=lst) for _ in range(2)]
    ss = [P.sb([128, 1], F32, stack=lst) for _ in range(2)]
    rs = [P.sb([128, 1], F32, stack=lst) for _ in range(2)]
    for t in range(NT):
        n = tn(t)
        xt, h, s, r = xb[t % 2], hb[t % 2], ss[t % 2], rs[t % 2]
        P.dma(xt[:n], x_src[t * 128:t * 128 + n, :], writes=[xt])
        P.act(junk[:n], xt[:n], AF.Square, [xt], [junk, s], accum_out=s[:n])
        P.act(r[:n], s[:n], AF.Sqrt, [s, C.eps6], [r], bias=C.eps6[:n], scale=1.0 / D)
        P.op("dve", lambda E, r=r, n=n: E.reciprocal(r[:n], r[:n]), [r], [r])
        P.stt("dve", h[:n], xt[:n], r[:n, 0:1], C.gb[:n], ALU.mult, ALU.mult, [xt, r, C.gb], [h])
        if want_last and t >= 15:
            hl = P.sb([128, D], F32, stack=lst)
            P.stt("dve", hl[:n], xt[:n], r[:n, 0:1], C.gb[:n], ALU.mult, ALU.mult, [xt, r, C.gb], [hl])
            P.dma(C.o_cs[t - 15:t - 14, :], hl[n - 1:n, :], reads=[hl])
        for gi in range(2):
            for jj in range(8):
                c = gi * 8 + jj
                P.tr(C.ptb[:, jj * 128:jj * 128 + n], h[:n, c * 128:(c + 1) * 128], C.identb[:n, :n],
                     [h, C.identb], [C.ptb])
            src = C.ptb[:].rearrange("p (a b) -> p a b", a=8)[:, :, :n]
            P.cp("act" if gi == 0 else "dve", C.hT[:, gi * 8:(gi + 1) * 8, t * 128:t * 128 + n], src, [C.ptb], [C.hT])


def load_w_block(C, wf_bufs, wb, pieces, ncols, ctr):
    P = C.P
    for qtr in range(4):
        wf = wf_bufs[(ctr[0]) % 2]
        for ap, off in pieces:
            w = ap.shape[1]
            P.dma(wf[:, :, off:off + w], ap[qtr * 512:(qtr + 1) * 512, :].rearrange("(c p) n -> p c n", p=128),
                  writes=[wf])
        eng = ("pool", "act", "dve")[ctr[0] % 2]
        P.cp(eng, wb[:, qtr * 4:(qtr + 1) * 4, :ncols], wf[:, :, :ncols], [wf], [wb])
        ctr[0] += 1


def layer_a(C, li, j):
    P = C.P
    lst = C.lst
    lam_init = 0.8 - 0.6 * math.exp(-0.3 * li)
    rope = P.sb([128, NT, 2, 64], F32, stack=lst, name="rope_a%d" % li)
    P.dma(rope[:, 0:16], C.rope64[0:2048].rearrange("(t p) a b -> p t a b", p=128), writes=[rope])
    P.dma(rope[:64, 16], C.rope64[2048:2112], writes=[rope])
    lamt = P.sb([128, 4, 64], F32, stack=lst)
    P.dma(lamt[:].rearrange("p a b -> p (a b)"), C.a_lam[j:j + 1, :].partition_broadcast(128), writes=[lamt])
    lprod = P.sb([128, 2, 64], F32, stack=lst)
    lsum = P.sb([128, 2], F32, stack=lst)
    neglam = P.sb([128, 1], F32, stack=lst)
    lv = lamt[:].rearrange("p (a b) c -> p a b c", b=2)
    P.tt("dve", lprod[:], lv[:, :, 0, :], lv[:, :, 1, :], ALU.mult, [lamt], [lprod])
    P.op("dve", lambda E: E.reduce_sum(lsum[:], lprod[:], AX.X), [lprod], [lsum])
    P.act(lsum[:], lsum[:], AF.Exp, [lsum], [lsum])
    P.tt("dve", neglam[:], lsum[:, 1:2], lsum[:, 0:1], ALU.subtract, [lsum], [neglam])
    P.ts("dve", neglam[:], neglam[:], -lam_init, None, ALU.add, reads=[neglam], writes=[neglam])
    gsub = P.sb([128, 128], F32, stack=lst)
    P.dma(gsub[:], C.a_subln_g[j:j + 1, :].partition_broadcast(128), writes=[gsub])
    P.ts("dve", gsub[:], gsub[:], 1.0 - lam_init, None, ALU.mult, reads=[gsub], writes=[gsub])

    wf_bufs = [P.sb([128, 4, 512], F32, stack=lst) for _ in range(2)]
    wbs = [P.sb([128, 16, 512], BF16, stack=lst) for _ in range(1)]
    qkT = [P.sb([128, 2, NTOK], BF16, stack=lst) for _ in range(2)]
    vaug = [P.sb([128, NT, 132], BF16, stack=lst) for _ in range(2)]
    sg = [P.sb([128, NT, 128], BF16, stack=lst) for _ in range(2)]
    for v in vaug:
        P.op("pool", lambda E, v=v: E.memset(v[:, :, 128:129], 1.0), [], [v])
    ra = [P.sb([128, 256], F32, stack=lst) for _ in range(2)]
    rt = [P.sb([128, 256], F32, stack=lst) for _ in range(2)]
    kv = [P.sb([128, 2, 128], F32, stack=lst) for _ in range(2)]
    qkb = [P.sb([128, 256], BF16, stack=lst) for _ in range(2)]
    PT = [P.sb([128, 2048], BF16, stack=lst) for _ in range(2)]
    ot = [P.sb([128, 128], F32, stack=lst) for _ in range(2)]
    ogt = [P.sb([128, 128], BF16, stack=lst) for _ in range(2)]
    rr = [P.sb([128, 2], F32, stack=lst) for _ in range(2)]
    s2 = [P.sb([128, 1], F32, stack=lst) for _ in range(2)]
    junk = P.sb([128, 128], F32, stack=lst)
    kcs = [P.sb([128, 8, 128], F32, stack=lst) for _ in range(2)]
    vcs = [P.sb([128, 8, 128], F32, stack=lst) for _ in range(2)]
    kTc = P.sb([128, PAST], BF16, stack=lst)
    vca = P.sb([128, 32, 132], BF16, stack=lst)
    P.op("pool", lambda E: E.memset(vca[:, :, 128:129], 1.0), [], [vca])
    PTs = [P.sb([128, 512], BF16, stack=lst) for _ in range(2)]

    ctr = [0]
    cnt = [0]
    W = C.a_w_in[j]
    scale = 64 ** -0.5

    def epilogue(h, t, n, po, k):
        o, og_, r_, s_ = ot[k % 2], ogt[k % 2], rr[k % 2], s2[k % 2]
        pov = po[:].rearrange("p (a b) -> p a b", a=2)
        P.op("dve", lambda E: E.reciprocal(r_[:n], pov[:n, :, 128]), [po], [r_])
        P.tt("dve", r_[:n, 1:2], r_[:n, 1:2], neglam[:n], ALU.mult, [r_, neglam], [r_])
        P.ts("dve", o[:n], po[:n, 0:128], r_[:n, 0:1], None, ALU.mult, reads=[po, r_], writes=[o])
        P.stt("dve", o[:n], po[:n, 256:384], r_[:n, 1:2], o[:n], ALU.mult, ALU.add, [po, r_, o], [o])
        P.act(junk[:n], o[:n], AF.Square, [o], [junk, s_], accum_out=s_[:n])
        P.act(s_[:n], s_[:n], AF.Sqrt, [s_, C.eps5], [s_], bias=C.eps5[:n], scale=1.0 / 128)
        P.op("dve", lambda E: E.reciprocal(s_[:n], s_[:n]), [s_], [s_])
        P.stt("dve", o[:n], o[:n], s_[:n, 0:1], gsub[:n], ALU.mult, ALU.mult, [o, s_, gsub], [o])
        P.tt("pool", og_[:n], o[:n], sg[h % 2][:n, t, :], ALU.mult, [o, sg[h % 2]], [og_])
        P.dma(C.og[t * 128:t * 128 + n, h * 128:(h + 1) * 128], og_[:n], reads=[og_])

    NH = int(os.environ.get('MK_HEADS', '16'))
    NOATT = int(os.environ.get('MK_NOATT', '0'))
    NOSAMP = int(os.environ.get('MK_NOSAMP', '0'))
    for h in range(NH):
        wb = wbs[0]
        pieces = [(W[:, blk * D + h * 128: blk * D + (h + 1) * 128], blk * 128) for blk in range(4)]
        load_w_block(C, wf_bufs, wb, pieces, 512, ctr)
        QK, V, SG = qkT[h % 2], vaug[h % 2], sg[h % 2]
        for t in range(NT):
            n = tn(t)
            ps = C.pp[t % 2]
            for c in range(16):
                P.mm(ps[:n, :], C.hT[:, c, t * 128:t * 128 + n], wb[:, c, :], c == 0, c == 15, [C.hT, wb], [ps])
            k = cnt[0]
            cnt[0] += 1
            a_, t_, kv_, qb = ra[k % 2], rt[k % 2], kv[k % 2], qkb[k % 2]
            xv = ps[:n, 0:256].rearrange("p (a b c) -> p a b c", a=4, b=2)
            cosb = rope[:n, t, 0, :].rearrange("p (b c) -> p b c", b=2).unsqueeze(1).to_broadcast([n, 4, 2, 32])
            sinb = rope[:n, t, 1, :].rearrange("p (b c) -> p b c", b=2).unsqueeze(1).to_broadcast([n, 4, 2, 32])
            av = a_[:n].rearrange("p (a b c) -> p a b c", a=4, b=2)
            tv = t_[:n].rearrange("p (a b c) -> p a b c", a=4, b=2)
            P.tt("dve", av, xv, cosb, ALU.mult, [ps, rope], [a_])
            P.tt("dve", tv[:, :, 0, :], xv[:, :, 1, :], sinb[:, :, 0, :], ALU.mult, [ps, rope], [t_])
            P.tt("dve", tv[:, :, 1, :], xv[:, :, 0, :], sinb[:, :, 1, :], ALU.mult, [ps, rope], [t_])
            P.tt("pool", qb[:n, 0:128], a_[:n, 0:128], t_[:n, 0:128], ALU.add, [a_, t_], [qb])
            P.tt("pool", kv_[:n, 0, :], a_[:n, 128:256], t_[:n, 128:256], ALU.add, [a_, t_], [kv_])
            P.cp("pool", qb[:n, 128:256], kv_[:n, 0, :], [kv_], [qb])
            P.cp("act", kv_[:n, 1, :], ps[:n, 256:384], [ps], [kv_])
            P.cp("act", V[:n, t, 0:128], ps[:n, 256:384], [ps], [V])
            P.act(SG[:n, t, :], ps[:n, 384:512], AF.Silu, [ps], [SG])
            P.dma(C.o_ak[j, t * 128:t * 128 + n, h * 128:(h + 1) * 128], kv_[:n, 0, :], reads=[kv_])
            P.dma(C.o_av[j, t * 128:t * 128 + n, h * 128:(h + 1) * 128], kv_[:n, 1, :], reads=[kv_])
            for m in range(2):
                P.tr(C.ptb[:, m * 128:m * 128 + n], qb[:n, m * 128:(m + 1) * 128], C.identb[:n, :n],
                     [qb, C.identb], [C.ptb])
            P.cp("act", QK[:, :, t * 128:t * 128 + n],
                 C.ptb[:, 0:256].rearrange("p (a b) -> p a b", a=2)[:, :, :n], [C.ptb], [QK])
        for i in range(0 if not NOATT else 16, 16):
            k = cnt[0]
            cnt[0] += 1
            po = C.po[k % 2]
            for m in range(2):
                pt = PT[m]
                nb = i + 1
                for g0 in range(0, nb, 4):
                    g1 = min(nb, g0 + 4)
                    sc = C.psc[(g0 // 4 + m) % 2]
                    for jb in range(g0, g1):
                        P.mm(sc[:, (jb - g0) * 128:(jb - g0 + 1) * 128],
                             QK[m * 64:(m + 1) * 64, 1, jb * 128:(jb + 1) * 128],
                             QK[m * 64:(m + 1) * 64, 0, i * 128:(i + 1) * 128], True, True, [QK], [sc])
                    P.act(pt[:, g0 * 128:g1 * 128], sc[:, 0:(g1 - g0) * 128], AF.Exp, [sc], [pt], scale=scale)
                P.op("pool", lambda E, pt=pt, i=i: E.memset(pt[64:128, i * 128:i * 128 + 64], 0.0), [], [pt])
                for jb in range(nb):
                    P.mm(po[:, m * 256:m * 256 + 129], pt[:, jb * 128:(jb + 1) * 128], V[:, jb, 0:129],
                         jb == 0, jb == nb - 1, [pt, V], [po])
            epilogue(h, i, 128, po, k)
        if NOSAMP:
            continue
        for half in range(4):
            kc, vc = kcs[half % 2], vcs[half % 2]
            P.dma(kc[:], C.cak[j, half * 1024:(half + 1) * 1024, h * 128:(h + 1) * 128].rearrange(
                "(b p) d -> p b d", p=128), writes=[kc])
            P.dma(vc[:], C.cav[j, half * 1024:(half + 1) * 1024, h * 128:(h + 1) * 128].rearrange(
                "(b p) d -> p b d", p=128), writes=[vc])
            P.cp("pool", vca[:, half * 8:(half + 1) * 8, 0:128], vc[:], [vc], [vca])
            for g in range(2):
                for b in range(4):
                    P.tr(C.ptf[:, b * 128:(b + 1) * 128], kc[:, g * 4 + b, :], C.ident[:], [kc, C.ident], [C.ptf])
                P.cp("dve" if g % 2 else "act", kTc[:, half * 1024 + g * 512: half * 1024 + (g + 1) * 512], C.ptf[:],
                     [C.ptf], [kTc])
        k = cnt[0]
        cnt[0] += 1
        po = C.po[k % 2]
        for m in range(2):
            qs = QK[m * 64:(m + 1) * 64, 0, 2048:2112]
            for g in range(5):
                sc = C.psc[(g + m) % 2]
                pts = PTs[(g + m) % 2]
                nblk = 8 if g < 4 else 1
                for b in range(nblk):
                    jb = g * 8 + b
                    if jb < 32:
                        P.mm(sc[:, b * 64:(b + 1) * 64], kTc[m * 64:(m + 1) * 64, jb * 128:(jb + 1) * 128], qs,
                             True, True, [kTc, QK], [sc])
                    else:
                        P.mm(sc[:64, b * 64:(b + 1) * 64], QK[m * 64:(m + 1) * 64, 1, 2048:2112], qs,
                             True, True, [QK], [sc])
                rows = 128 if g < 4 else 64
                P.act(pts[:rows, 0:nblk * 64], sc[:rows, 0:nblk * 64], AF.Exp, [sc], [pts], scale=scale)
                for b in range(nblk):
                    jb = g * 8 + b
                    if jb < 32:
                        P.mm(po[:64, m * 256:m * 256 + 129], pts[:, b * 64:(b + 1) * 64], vca[:, jb, 0:129],
                             jb == 0, False, [pts, vca], [po])
                    else:
                        P.mm(po[:64, m * 256:m * 256 + 129], pts[:64, b * 64:(b + 1) * 64], V[:64, 16, 0:129],
                             False, True, [pts, V], [po])
        epilogue(h, 16, 64, po, k)


def rope_ops(C, n, xin, H, half, rope, t, a_, t_, rd):
    P = C.P
    Wd = H * 2 * half
    xv = xin.rearrange("p (a b c) -> p a b c", a=H, b=2)
    cosb = rope[:n, t, 0, :].rearrange("p (b c) -> p b c", b=2).unsqueeze(1).to_broadcast([n, H, 2, half])
    sinb = rope[:n, t, 1, :].rearrange("p (b c) -> p b c", b=2).unsqueeze(1).to_broadcast([n, H, 2, half])
    av = a_[:n, :Wd].rearrange("p (a b c) -> p a b c", a=H, b=2)
    tv = t_[:n, :Wd].rearrange("p (a b c) -> p a b c", a=H, b=2)
    P.tt("dve", av, xv, cosb, ALU.mult, rd + [rope], [a_])
    P.tt("dve", tv[:, :, 0, :], xv[:, :, 1, :], sinb[:, :, 0, :], ALU.mult, rd + [rope], [t_])
    P.tt("dve", tv[:, :, 1, :], xv[:, :, 0, :], sinb[:, :, 1, :], ALU.mult, rd + [rope], [t_])


def layer_b(C, li):
    P = C.P
    W = C.b_w_in
    OQ, OK_, OV, OQI, OKI, OG = 0, 2048, 2560, 3072, 4096, 4176
    NEG = -1.0e30
    with ExitStack() as bst:
        kT = P.sb([128, 4, NTOK], BF16, stack=bst, name="b_kT")
        vaug = P.sb([128, NT, 4, 132], BF16, stack=bst, name="b_vaug")
        kiT2 = P.sb([128, NTOK], BF16, stack=bst, name="b_kiT2")
        wis = P.sb([128, NT, 16], F32, stack=bst, name="b_wis")
        for g in range(4):
            P.op("pool", lambda E, g=g: E.memset(vaug[:, :, g, 128:129], 1.0), [], [vaug])
        with ExitStack() as lst:
            rope128 = P.sb([128, NT, 2, 128], F32, stack=lst)
            P.dma(rope128[:, 0:16], C.rope128[0:2048].rearrange("(t p) a b -> p t a b", p=128), writes=[rope128])
            P.dma(rope128[:64, 16], C.rope128[2048:2112], writes=[rope128])
            rope64 = P.sb([128, NT, 2, 64], F32, stack=lst)
            P.dma(rope64[:, 0:16], C.rope64[0:2048].rearrange("(t p) a b -> p t a b", p=128), writes=[rope64])
            P.dma(rope64[:64, 16], C.rope64[2048:2112], writes=[rope64])
            wf_bufs = [P.sb([128, 4, 512], F32, stack=lst) for _ in range(2)]
            wbs = [P.sb([128, 16, 512], BF16, stack=lst) for _ in range(2)]
            ra = [P.sb([128, 512], F32, stack=lst) for _ in range(2)]
            rt = [P.sb([128, 512], F32, stack=lst) for _ in range(2)]
            of = [P.sb([128, 512], F32, stack=lst) for _ in range(2)]
            ob = [P.sb([128, 512], BF16, stack=lst) for _ in range(2)]
            ctr = [0]
            blocks = [("k", OK_, 512), ("v", OV, 512), ("ki", OKI, 80), ("qi", OQI, 512), ("qi", OQI + 512, 512)]
            blocks += [("q", OQ + g * 512, 512) for g in range(4)] + [("g", OG + g * 512, 512) for g in range(4)]
            kk = 0
            blocks = blocks[:int(os.environ.get('MK_BK', '99'))]
            for bi, (kind, off, ncols) in enumerate(blocks):
                wb = wbs[bi % 2]
                load_w_block(C, wf_bufs, wb, [(W[:, off:off + ncols], 0)], ncols, ctr)
                for t in range(NT):
                    n = tn(t)
                    r0 = t * 128
                    ps = C.pp[t % 2]
                    for c in range(16):
                        P.mm(ps[:n, :ncols], C.hT[:, c, r0:r0 + n], wb[:, c, :ncols], c == 0, c == 15, [C.hT, wb], [ps])
                    a_, t_, f_, b_ = ra[kk % 2], rt[kk % 2], of[kk % 2], ob[kk % 2]
                    kk += 1
                    if kind == "k":
                        rope_ops(C, n, ps[:n, 0:512], 4, 64, rope128, t, a_, t_, [ps])
                        P.tt("pool", f_[:n], a_[:n], t_[:n], ALU.add, [a_, t_], [f_])
                        P.dma(C.o_bk[r0:r0 + n, :], f_[:n], reads=[f_])
                        P.cp("act", b_[:n], f_[:n], [f_], [b_])
                        for g in range(4):
                            P.tr(C.ptb[:, g * 128:g * 128 + n], b_[:n, g * 128:(g + 1) * 128], C.identb[:n, :n],
                                 [b_, C.identb], [C.ptb])
                        P.cp("act", kT[:, :, r0:r0 + n], C.ptb[:, 0:512].rearrange("p (a b) -> p a b", a=4)[:, :, :n],
                             [C.ptb], [kT])
                    elif kind == "v":
                        P.cp("act", f_[:n], ps[:n, :], [ps], [f_])
                        P.dma(C.o_bv[r0:r0 + n, :], f_[:n], reads=[f_])
                        for g in range(4):
                            P.cp("dve" if g % 2 else "pool", vaug[:n, t, g, 0:128], f_[:n, g * 128:(g + 1) * 128], [f_], [vaug])
                    elif kind == "ki":
                        rope_ops(C, n, ps[:n, 0:64], 1, 32, rope64, t, a_, t_, [ps])
                        P.tt("pool", f_[:n, 0:64], a_[:n, 0:64], t_[:n, 0:64], ALU.add, [a_, t_], [f_])
                        P.dma(C.o_bi[r0:r0 + n, :], f_[:n, 0:64], reads=[f_])
                        P.cp("act", b_[:n, 0:64], f_[:n, 0:64], [f_], [b_])
                        P.cp("act", b_[:n, 64:128], f_[:n, 0:64], [f_], [b_])
                        P.tr(C.ptb[:, 0:n], b_[:n, 0:128], C.identb[:n, :n], [b_, C.identb], [C.ptb])
                        P.cp("act", kiT2[:, r0:r0 + n], C.ptb[:, 0:n], [C.ptb], [kiT2])
                        P.ts("dve", wis[:n, t, :], ps[:n, 64:80], 0.25, None, ALU.mult, reads=[ps], writes=[wis])
                    elif kind == "qi":
                        rope_ops(C, n, ps[:n, 0:512], 8, 32, rope64, t, a_, t_, [ps])
                        P.tt("pool", b_[:n], a_[:n], t_[:n], ALU.add, [a_, t_], [b_])
                        P.dma(C.qid[r0:r0 + n, off - OQI:off - OQI + 512], b_[:n], reads=[b_])
                    elif kind == "q":
                        rope_ops(C, n, ps[:n, 0:512], 4, 64, rope128, t, a_, t_, [ps])
                        P.tt("pool", b_[:n], a_[:n], t_[:n], ALU.add, [a_, t_], [b_])
                        P.dma(C.qd[r0:r0 + n, off - OQ:off - OQ + 512], b_[:n], reads=[b_])
                    else:
                        P.act(b_[:n], ps[:n, :], AF.Silu, [ps], [b_])
                        P.dma(C.sgd[r0:r0 + n, off - OG:off - OG + 512], b_[:n], reads=[b_])
            P.fence()
        if int(os.environ.get('MK_B', '3')) < 2:
            return False
        with ExitStack() as lst:
            acc = P.sb([128, 4160], F32, stack=lst)
            work = P.sb([128, 4160], F32, stack=lst)
            maskb = P.sb([128, 4160], BF16, stack=lst)
            kiS = P.sb([128, 4160], BF16, stack=lst)
            qi_t = [P.sb([128, 1024], BF16, stack=lst) for _ in range(2)]
            qiT = [P.sb([128, 8, 128], BF16, stack=lst) for _ in range(2)]
            rl = [P.sb([128, 512], F32, stack=lst) for _ in range(2)]
            m8 = P.sb([128, 8], F32, stack=lst)
            thr = P.sb([128, 1], F32, stack=lst)
            cst = [P.sb([128, 8, 128], F32, stack=lst) for _ in range(2)]
            for qd_ in range(4):
                cs = cst[qd_ % 2]
                src = C.cbi[qd_ * 1024:(qd_ + 1) * 1024, :].rearrange("(b p) d -> p b d", p=128)
                P.dma(cs[:, :, 0:64], src, writes=[cs])
                P.dma(cs[:, :, 64:128], src, writes=[cs])
                for g in range(2):
                    for b in range(4):
                        P.tr(C.ptf[:, b * 128:(b + 1) * 128], cs[:, g * 4 + b, :], C.ident[:], [cs, C.ident], [C.ptf])
                    P.cp("dve" if g % 2 else "act", kiS[:, qd_ * 1024 + g * 512: qd_ * 1024 + (g + 1) * 512], C.ptf[:],
                         [C.ptf], [kiS])
            P.cp("pool", kiS[:, 4096:4160], kiT2[:, 2048:2112], [kiT2], [kiS])
            for t in range(NT):
                n = tn(t)
                r0 = t * 128
                S = 128 * (t + 1) if t < 16 else 4160
                keys = kiT2 if t < 16 else kiS
                qt, qT_ = qi_t[t % 2], qiT[t % 2]
                P.dma(qt[:n], C.qid[r0:r0 + n, :], writes=[qt])
                for hp in range(8):
                    P.tr(C.ptb[:, hp * 128:hp * 128 + n], qt[:n, hp * 128:(hp + 1) * 128], C.identb[:n, :n],
                         [qt, C.identb], [C.ptb])
                P.cp("act", qT_[:, :, :n], C.ptb[:].rearrange("p (a b) -> p a b", a=8)[:, :, :n], [C.ptb], [qT_])
                kq = 0
                for hd in range(16):
                    hp, par = hd // 2, hd % 2
                    for c0 in range(0, S, 512):
                        w = min(512, S - c0)
                        sc = C.psc[kq % 2]
                        r_ = rl[kq % 2]
                        kq += 1
                        P.mm(sc[:n, :w], qT_[par * 64:(par + 1) * 64, hp, :n], keys[par * 64:(par + 1) * 64, c0:c0 + w],
                             True, True, [qT_, keys], [sc])
                        P.act(r_[:n, :w], sc[:n, :w], AF.Relu, [sc], [r_], scale=0.125)
                        if hd == 0:
                            P.ts("dve", acc[:n, c0:c0 + w], r_[:n, :w], wis[:n, t, 0:1], None, ALU.mult,
                                 reads=[r_, wis], writes=[acc])
                        else:
                            P.stt("dve", acc[:n, c0:c0 + w], r_[:n, :w], wis[:n, t, hd:hd + 1], acc[:n, c0:c0 + w],
                                  ALU.mult, ALU.add, [r_, wis, acc], [acc])
                if t < 16:
                    P.op("dve", lambda E, t=t: E.memset(acc[0:64, t * 128 + 64:(t + 1) * 128], NEG), [], [acc])
                if t >= 2:
                    for rnd in range(32):
                        srcw = acc if rnd == 0 else work
                        P.op("dve", lambda E, srcw=srcw, n=n, S=S: E.max(out=m8[:n], in_=srcw[:n, :S]), [srcw], [m8])
                        if rnd < 31:
                            P.op("dve", lambda E, srcw=srcw, n=n, S=S: E.match_replace(
                                out=work[:n, :S], in_to_replace=m8[:n], in_values=srcw[:n, :S], imm_value=NEG),
                                [srcw, m8], [work])
                    P.cp("dve", thr[:n], m8[:n, 7:8], [m8], [thr])
                else:
                    P.op("dve", lambda E, n=n: E.memset(thr[:n], -1.0e29), [], [thr])
                P.ts("dve", maskb[:n, :S], acc[:n, :S], thr[:n, 0:1], None, ALU.is_ge, reads=[acc, thr], writes=[maskb])
                P.dma(C.maskd[r0:r0 + n, 0:S], maskb[:n, :S], reads=[maskb])
            P.fence()
        if int(os.environ.get('MK_B', '3')) < 3:
            return False
        with ExitStack() as lst:
            q_t = [P.sb([128, D], BF16, stack=lst) for _ in range(2)]
            sg_t = [P.sb([128, D], BF16, stack=lst) for _ in range(2)]
            mk = [P.sb([128, 4160], BF16, stack=lst) for _ in range(1)]
            maskT = P.sb([128, 2112], BF16, stack=lst)
            qT = P.sb([128, 16, 128], BF16, stack=lst)
            PTt = P.sb([128, 8448], BF16, stack=lst)
            kTc = P.sb([128, PAST], BF16, stack=lst)
            vca = P.sb([128, 32, 132], BF16, stack=lst)
            P.op("pool", lambda E: E.memset(vca[:, :, 128:129], 1.0), [], [vca])
            kcs = [P.sb([128, 8, 128], F32, stack=lst) for _ in range(2)]
            vcs = [P.sb([128, 8, 128], F32, stack=lst) for _ in range(2)]
            rr = P.sb([128, 4], F32, stack=lst)
            of = P.sb([128, 2, 128], F32, stack=lst)
            ogt = [P.sb([128, D], BF16, stack=lst) for _ in range(2)]
            scale = 128 ** -0.5
            kq = 0
            for t in range(NT):
                n = tn(t)
                r0 = t * 128
                S = 128 * (t + 1) if t < 16 else 4160
                nb = (S + 127) // 128
                q_, s_, m_, og_ = q_t[t % 2], sg_t[t % 2], mk[0], ogt[t % 2]
                P.dma(q_[:n], C.qd[r0:r0 + n, :], writes=[q_])
                P.dma(s_[:n], C.sgd[r0:r0 + n, :], writes=[s_])
                P.dma(m_[:n, :S], C.maskd[r0:r0 + n, 0:S], writes=[m_])
                for j0 in range(0, nb, 8):
                    j1 = min(nb, j0 + 8)
                    rmax = 0
                    for jb in range(j0, j1):
                        rows = min(128, S - jb * 128)
                        rmax = max(rmax, rows)
                        P.tr(C.ptb[:rows, (jb - j0) * 128:(jb - j0) * 128 + n], m_[:n, jb * 128:jb * 128 + rows],
                             C.identb[:n, :n], [m_, C.identb], [C.ptb])
                    full = [jb for jb in range(j0, j1) if min(128, S - jb * 128) == 128]
                    if full:
                        P.cp("act", maskT[:, j0 * n:(j0 + len(full)) * n].rearrange("p (a b) -> p a b", b=n),
                             C.ptb[:].rearrange("p (a b) -> p a b", a=8)[:, 0:len(full), :n], [C.ptb], [maskT])
                    if len(full) < j1 - j0:
                        jb = j1 - 1
                        P.cp("act", maskT[:64, jb * n:(jb + 1) * n], C.ptb[:64, (jb - j0) * 128:(jb - j0) * 128 + n],
                             [C.ptb], [maskT])
                for hp in range(2):
                    for hh in range(8):
                        hd = hp * 8 + hh
                        P.tr(C.ptb[:, hh * 128:hh * 128 + n], q_[:n, hd * 128:(hd + 1) * 128], C.identb[:n, :n],
                             [q_, C.identb], [C.ptb])
                    P.cp("dve", qT[:, hp * 8:(hp + 1) * 8, :n], C.ptb[:].rearrange("p (a b) -> p a b", a=8)[:, :, :n],
                         [C.ptb], [qT])
                for g in range(4):
                    if t == 16:
                        for qd_ in range(4):
                            kc, vc = kcs[qd_ % 2], vcs[qd_ % 2]
                            P.dma(kc[:], C.cbk[qd_ * 1024:(qd_ + 1) * 1024, g * 128:(g + 1) * 128].rearrange(
                                "(b p) d -> p b d", p=128), writes=[kc])
                            P.dma(vc[:], C.cbv[qd_ * 1024:(qd_ + 1) * 1024, g * 128:(g + 1) * 128].rearrange(
                                "(b p) d -> p b d", p=128), writes=[vc])
                            P.cp("pool", vca[:, qd_ * 8:(qd_ + 1) * 8, 0:128], vc[:], [vc], [vca])
                            for gg in range(2):
                                for b in range(4):
                                    P.tr(C.ptf[:, b * 128:(b + 1) * 128], kc[:, gg * 4 + b, :], C.ident[:],
                                         [kc, C.ident], [C.ptf])
                                P.cp("dve" if gg % 2 else "act",
                                     kTc[:, qd_ * 1024 + gg * 512: qd_ * 1024 + (gg + 1) * 512], C.ptf[:], [C.ptf], [kTc])

                    def kblk(jb):
                        if t < 16:
                            return kT[:, g, jb * 128:(jb + 1) * 128], vaug[:, jb, g, 0:129], 128, [kT], [vaug]
                        if jb < 32:
                            return kTc[:, jb * 128:(jb + 1) * 128], vca[:, jb, 0:129], 128, [kTc], [vca]
                        return kT[:, g, 2048:2112], vaug[:64, 16, g, 0:129], 64, [kT], [vaug]

                    for jb in range(nb):
                        ka, va, rows, kr, vr = kblk(jb)
                        sc = C.psc[kq % 2]
                        kq += 1
                        P.mm(sc[:rows, 0:4 * n], ka, qT[:, g * 4:(g + 1) * 4, :n], True, True, kr + [qT], [sc])
                        pt = PTt[:rows, jb * 4 * n:(jb + 1) * 4 * n]
                        P.act(pt, sc[:rows, 0:4 * n], AF.Exp, [sc], [PTt], scale=scale)
                        pt3 = pt.rearrange("p (a b) -> p a b", a=4)
                        mT = maskT[:rows, jb * n:(jb + 1) * n].unsqueeze(1).to_broadcast([rows, 4, n])
                        P.tt("pool" if jb % 2 else "dve", pt3, pt3, mT, ALU.mult, [PTt, maskT], [PTt])
                    for r in range(4):
                        po = C.po[r // 2]
                        for jb in range(nb):
                            ka, va, rows, kr, vr = kblk(jb)
                            P.mm(po[:n, (r % 2) * 256:(r % 2) * 256 + 129],
                                 PTt[:rows, jb * 4 * n + r * n: jb * 4 * n + (r + 1) * n], va,
                                 jb == 0, jb == nb - 1, [PTt] + vr, [po])
                    for b in range(2):
                        po = C.po[b]
                        pov = po[:].rearrange("p (a b) -> p a b", a=2)
                        P.op("dve", lambda E, pov=pov, b=b, n=n: E.reciprocal(rr[:n, b * 2:b * 2 + 2], pov[:n, :, 128]),
                             [po], [rr])
                        P.tt("dve", of[:n], pov[:n, :, 0:128],
                             rr[:n, b * 2:b * 2 + 2].unsqueeze(2).to_broadcast([n, 2, 128]), ALU.mult, [po, rr], [of])
                        c0 = g * 512 + b * 256
                        P.tt("pool", og_[:n, c0:c0 + 256], of[:n].rearrange("p a b -> p (a b)"), s_[:n, c0:c0 + 256],
                             ALU.mult, [of, s_], [og_])
                P.dma(C.og[r0:r0 + n, :], og_[:n], reads=[og_])
            P.fence()
    return True


def layer_c1(C, li):
    P = C.P
    lst = C.lst
    hT = C.hT
    ms = P.sb([128, 128], F32, stack=lst)
    P.dma(ms[0:96, :], C.c_mu.rearrange("n (c p) -> (n c) p", p=128), writes=[ms])
    P.dma(ms[96:112, :], C.sshift.rearrange("o (c p) -> (o c) p", p=128), writes=[ms])
    P.tr(C.ptf[:, 0:112], ms[0:112, :], C.ident[:112, :112], [ms, C.ident], [C.ptf])
    mu = P.sb([128, 112], F32, stack=lst)
    om = P.sb([128, 96], F32, stack=lst)
    P.cp("act", mu[:], C.ptf[:, 0:112], [C.ptf], [mu])
    P.ts("dve", om[:], mu[:, 0:96], -1.0, 1.0, ALU.mult, ALU.add, reads=[mu], writes=[om])
    lerpT = P.sb([128, 16, NTOK], BF16, stack=lst)
    tmpb = [P.sb([128, NTOK], BF16, stack=lst) for _ in range(2)]
    wf_bufs = [P.sb([128, 4, 512], F32, stack=lst) for _ in range(2)]
    wbs = [P.sb([128, 16, 512], BF16, stack=lst) for _ in range(1)]
    ev = [P.sb([128, 512], F32, stack=lst) for _ in range(3)]
    ctr = [0]
    kk = 0
    dsts = [C.c_r, C.c_k, C.c_v, C.c_sg]
    for nidx in range(6):
        for c in range(16):
            tm = tmpb[c % 2]
            col = nidx * 16 + c
            P.ts("pool", tm[:], hT[:, c, :], om[:, col:col + 1], None, ALU.mult, reads=[hT, om], writes=[tm])
            P.stt("dve", lerpT[:, c, 1:NTOK], hT[:, c, 0:NTOK - 1], mu[:, col:col + 1], tm[:, 1:NTOK],
                  ALU.mult, ALU.add, [hT, mu, tm], [lerpT])
            P.cp("dve", lerpT[:, c, 0:1], tm[:, 0:1], [tm], [lerpT])
            P.stt("dve", lerpT[:, c, 2048:2049], mu[:, 96 + c:97 + c], mu[:, col:col + 1], tm[:, 2048:2049],
                  ALU.mult, ALU.add, [mu, tm], [lerpT])
        if nidx < 4:
            for blk in range(4):
                wb = wbs[0]
                load_w_block(C, wf_bufs, wb, [(C.c_w_rkvg[nidx][:, blk * 512:(blk + 1) * 512], 0)], 512, ctr)
                for t in range(NT):
                    n = tn(t)
                    r0 = t * 128
                    ps = C.pp[t % 2]
                    for c in range(16):
                        P.mm(ps[:n, :], lerpT[:, c, r0:r0 + n], wb[:, c, :], c == 0, c == 15, [lerpT, wb], [ps])
                    e_ = ev[kk % 3]
                    kk += 1
                    if nidx < 3:
                        P.cp("act", e_[:n], ps[:n, :], [ps], [e_])
                    else:
                        P.act(e_[:n], ps[:n, :], AF.Silu, [ps], [e_])
                    P.dma(dsts[nidx][r0:r0 + n, blk * 512:(blk + 1) * 512], e_[:n], reads=[e_])
        else:
            wsrc = C.c_w_la if nidx == 4 else C.c_a_la
            dstT = C.tT if nidx == 4 else C.aT
            wf = wf_bufs[0]
            wb = wbs[0]
            for qtr in range(4):
                P.dma(wf[:, :, 0:96], wsrc[qtr * 512:(qtr + 1) * 512, :].rearrange("(c p) n -> p c n", p=128), writes=[wf])
                P.cp("dve", wb[:, qtr * 4:(qtr + 1) * 4, 0:96], wf[:, :, 0:96], [wf], [wb])
            for tb in range(0, NTOK, 512):
                w = min(512, NTOK - tb)
                ps = C.pp[(tb // 512) % 2]
                for c in range(16):
                    P.mm(ps[:96, :w], wb[:, c, 0:96], lerpT[:, c, tb:tb + w], c == 0, c == 15, [lerpT, wb], [ps])
                if nidx == 4:
                    P.act(dstT[:96, tb:tb + w], ps[:96, :w], AF.Tanh, [ps], [dstT])
                else:
                    P.cp("act", dstT[:96, tb:tb + w], ps[:96, :w], [ps], [dstT])
    return True


def layer_c2(C, li):
    P = C.P
    lst = C.lst
    cb = {}
    for nm in ("c_w0", "c_a0", "c_k_k", "c_k_a", "c_r_k"):
        cb[nm] = P.sb([128, D], F32, stack=lst, name="cb_" + nm)
        P.dma(cb[nm][:], getattr(C, nm)[0:1, :].partition_broadcast(128), writes=[cb[nm]])
    lbf = P.sb([128, D], F32, stack=lst)
    wlb = P.sb([128, D], BF16, stack=lst)
    alb = P.sb([128, D], BF16, stack=lst)
    P.dma(lbf[:96], C.c_w_lb[:, :], writes=[lbf])
    P.cp("dve", wlb[:96], lbf[:96], [lbf], [wlb])
    P.dma(lbf[:96], C.c_a_lb[:, :], writes=[lbf])
    P.cp("dve", alb[:96], lbf[:96], [lbf], [alb])
    tri = P.sb([128, 128], F32, stack=lst)
    P.dma(tri[:], C.cmask[4], writes=[tri])
    selc = P.sb([128, 2], F32, stack=lst)
    P.dma(selc[:], C.selc[:, :], writes=[selc])
    eps12 = P.sb([128, 1], F32, stack=lst)
    B = {nm: P.sb([128, D], F32, stack=lst, name="c2_" + nm) for nm in
         ("R", "K", "V", "A", "W", "KK", "T1", "T2", "CUM", "G")}
    ssq = P.sb([128, 32], F32, stack=lst)
    bs = P.sb([128, 32], F32, stack=lst)
    v3 = lambda tl, n: tl[:n].rearrange("p (a b) -> p a b", a=32)
    for t in range(NT):
        n = tn(t)
        r0 = t * 128
        nc_ = 2 if t < 16 else 1
        R, K, V, A, W, KK, T1, T2, CUM, G = (B[x] for x in ("R", "K", "V", "A", "W", "KK", "T1", "T2", "CUM", "G"))
        P.dma(R[:n], C.c_r[r0:r0 + n, :], writes=[R])
        P.dma(K[:n], C.c_k[r0:r0 + n, :], writes=[K])
        P.dma(V[:n], C.c_v[r0:r0 + n, :], writes=[V])
        for blk in range(4):
            cs = slice(blk * 512, (blk + 1) * 512)
            ps = C.pp[blk % 2]
            P.mm(ps[:n, :], C.tT[:96, r0:r0 + n], wlb[:96, cs], True, True, [C.tT, wlb], [ps])
            P.tt("dve", W[:n, cs], ps[:n, :], cb["c_w0"][:n, cs], ALU.add, [ps, cb["c_w0"]], [W])
            ps2 = C.psc[blk % 2]
            P.mm(ps2[:n, :], C.aT[:96, r0:r0 + n], alb[:96, cs], True, True, [C.aT, alb], [ps2])
            P.tt("dve", A[:n, cs], ps2[:n, :], cb["c_a0"][:n, cs], ALU.add, [ps2, cb["c_a0"]], [A])
        P.act(W[:n], W[:n], AF.Sigmoid, [W], [W])
        P.ts("pool", W[:n], W[:n], -math.exp(-0.5), None, ALU.mult, reads=[W], writes=[W])
        P.act(A[:n], A[:n], AF.Sigmoid, [A], [A])
        P.tt("pool", KK[:n], K[:n], cb["c_k_k"][:n], ALU.mult, [K, cb["c_k_k"]], [KK])
        P.tt("pool", T1[:n], KK[:n], KK[:n], ALU.mult, [KK], [T1])
        P.op("dve", lambda E, n=n, T1=T1: E.reduce_sum(ssq[:n], v3(T1, n), AX.X), [T1], [ssq])
        P.act(ssq[:n], ssq[:n], AF.Sqrt, [ssq], [ssq])
        P.ts("dve", ssq[:n], ssq[:n], 1e-12, None, ALU.max, reads=[ssq], writes=[ssq])
        P.op("dve", lambda E, n=n: E.reciprocal(ssq[:n], ssq[:n]), [ssq], [ssq])
        P.tt("dve", v3(KK, n), v3(KK, n), ssq[:n].unsqueeze(2).to_broadcast([n, 32, 64]), ALU.mult, [KK, ssq], [KK])
        P.stt("dve", T1[:n], A[:n], -1.0, cb["c_k_a"][:n], ALU.add, ALU.mult, [A, cb["c_k_a"]], [T1])
        P.stt("dve", T2[:n], T1[:n], 1.0, K[:n], ALU.add, ALU.mult, [T1, K], [T2])
        P.tt("pool", T1[:n], R[:n], T2[:n], ALU.mult, [R, T2], [T1])
        P.tt("pool", T1[:n], T1[:n], cb["c_r_k"][:n], ALU.mult, [T1, cb["c_r_k"]], [T1])
        P.op("dve", lambda E, n=n, T1=T1: E.reduce_sum(bs[:n], v3(T1, n), AX.X), [T1], [bs])
        P.tt("dve", v3(T1, n), v3(V, n), bs[:n].unsqueeze(2).to_broadcast([n, 32, 64]), ALU.mult, [V, bs], [T1])
        P.dma(C.c_bv[r0:r0 + n, :], T1[:n], reads=[T1])
        for blk in range(4):
            cs = slice(blk * 512, (blk + 1) * 512)
            ps = C.po[blk % 2]
            P.mm(ps[:n, :], tri[:n, :n], W[:n, cs], True, True, [tri, W], [ps])
            P.cp("act", CUM[:n, cs], ps[:n, :], [ps], [CUM])
        for fc in range(16):
            P.mm(C.ptf[:, fc * 2:fc * 2 + nc_], W[:n, fc * 128:(fc + 1) * 128], selc[:n, 0:nc_], True, True,
                 [W, selc], [C.ptf])
        P.act(C.gC[:, :, t * 2:t * 2 + nc_], C.ptf[:, 0:32].rearrange("p (a b) -> p a b", b=2)[:, :, 0:nc_], AF.Exp,
              [C.ptf], [C.gC])
        P.act(G[:n], CUM[:n], AF.Exp, [CUM], [G])
        P.tt("pool", R[:n], R[:n], G[:n], ALU.mult, [R, G], [R])
        P.dma(C.c_Rt[r0:r0 + n, :], R[:n], reads=[R])
        P.act(G[:n], CUM[:n], AF.Exp, [CUM], [G], scale=-1.0)
        P.tt("pool", T2[:n], T2[:n], G[:n], ALU.mult, [T2, G], [T2])
        P.dma(C.c_Kt[r0:r0 + n, :], T2[:n], reads=[T2])
        P.tt("dve", A[:n], A[:n], KK[:n], ALU.mult, [A, KK], [A])
        P.tt("dve", A[:n], A[:n], G[:n], ALU.mult, [A, G], [A])
        P.dma(C.c_Bt[r0:r0 + n, :], A[:n], reads=[A])
        P.tt("dve", CUM[:n], CUM[:n], W[:n], ALU.subtract, [CUM, W], [CUM])
        P.act(G[:n], CUM[:n], AF.Exp, [CUM], [G])
        P.stt("dve", KK[:n], KK[:n], -1.0, G[:n], ALU.mult, ALU.mult, [KK, G], [KK])
        P.dma(C.c_At[r0:r0 + n, :], KK[:n], reads=[KK])


def layer_c3(C, li):
    P = C.P
    lst = C.lst
    mk = [P.sb([128, 128], F32, stack=lst, name="c3m%d" % i) for i in range(5)]
    for i in range(5):
        P.dma(mk[i][:], C.cmask[i], writes=[mk[i]])
    SEL2, BDM, MUS, MLS, MUI = mk
    lnw2 = P.sb([128, 16, 64], F32, stack=lst)
    lnb2 = P.sb([128, 16, 64], F32, stack=lst)
    for h in range(2):
        for dst, src in ((lnw2, C.c_ln_w), (lnb2, C.c_ln_b)):
            sv = src[0:1, :].rearrange("o (a h v) -> o a h v", h=2, v=64)[:, :, h, :]
            P.dma(dst[h * 64:(h + 1) * 64], sv.partition_broadcast(64), writes=[dst])
    epsg = P.sb([128, 1], F32, stack=lst)
    P.op("dve", lambda E: E.memset(epsg[:], 64e-5), [], [epsg])
    banks = [C.pp[0], C.pp[1], C.psc[0], C.psc[1], C.po[0], C.po[1], C.ptf]
    bk = [0]

    def bank():
        b = banks[bk[0] % len(banks)]
        bk[0] += 1
        return b

    tok = {nm: [P.sb([128, NT, 128], F32, stack=lst, name="c3_%s%d" % (nm, i)) for i in range(2)]
           for nm in ("At", "Bt", "Kt", "Rt")}
    hv = {nm: [P.sb([128, 33, 64], F32, stack=lst, name="c3_%s%d" % (nm, i)) for i in range(2)]
          for nm in ("V2", "BV2", "SG2")}
    srcs = {"At": C.c_At, "Bt": C.c_Bt, "Kt": C.c_Kt, "Rt": C.c_Rt, "V2": C.c_v, "BV2": C.c_bv, "SG2": C.c_sg}
    sq = lambda nm: [[P.sb([128, 128], F32, stack=lst, name="c3q_%s%d" % (nm, i)) for i in range(2)]]
    Q = {nm: [P.sb([128, 128], F32, stack=lst, name="c3q_%s%d" % (nm, i)) for i in range(2)]
         for nm in ("BD_A", "BD_B", "BD_K", "BD_R", "BDT_B", "BDT_K", "N", "NT", "AakT", "WbT", "WkT", "Pm",
                    "Ma", "MTa", "Mb", "MTb")}
    S2 = P.sb([128, 64], F32, stack=lst)
    S2g = P.sb([128, 64], F32, stack=lst)
    Xs = P.sb([128, 64], F32, stack=lst)
    Us = P.sb([128, 64], F32, stack=lst)
    cen = [P.sb([128, 64], F32, stack=lst) for _ in range(2)]
    junk = P.sb([128, 64], F32, stack=lst)
    st1 = [P.sb([128, 1], F32, stack=lst) for _ in range(2)]
    st2 = [P.sb([128, 1], F32, stack=lst) for _ in range(2)]
    ogt = [P.sb([128, 64], BF16, stack=lst) for _ in range(2)]
    sw = P.sb([128, 128], F32, stack=lst)
    sto = P.sb([128, 128], F32, stack=lst)
    NHP = int(os.environ.get("MK_NHP", "16"))

    def save_state(hp, which):
        b = bank()
        P.tr(b[:64, 0:128], S2[:, :], C.ident[:, :], [S2, C.ident], [b])
        P.cp("act", sto[:64, :], b[:64, 0:128], [b], [sto])
        P.dma(C.o_cw[which, hp * 128:(hp + 1) * 128, :].rearrange("(h v) k -> v h k", h=2),
              sto[:64, :].rearrange("v (h k) -> v h k", h=2), reads=[sto])

    for hp in range(NHP):
        sx = hp % 2
        fs = slice(hp * 128, (hp + 1) * 128)
        for nm in ("At", "Bt", "Kt", "Rt"):
            tl = tok[nm][sx]
            P.dma(tl[:, 0:16, :], srcs[nm][0:2048, fs].rearrange("(t p) f -> p t f", p=128), writes=[tl])
            P.dma(tl[:64, 16, :], srcs[nm][2048:2112, fs], writes=[tl])
        for nm in ("V2", "BV2", "SG2"):
            tl = hv[nm][sx]
            for h in range(2):
                hs_ = slice(hp * 128 + h * 64, hp * 128 + (h + 1) * 64)
                P.dma(tl[h * 64:(h + 1) * 64, 0:32, :], srcs[nm][0:2048, hs_].rearrange("(c j) v -> j c v", j=64),
                      writes=[tl])
                P.dma(tl[h * 64:(h + 1) * 64, 32, :], srcs[nm][2048:2112, hs_], writes=[tl])
        At, Bt, Kt, Rt = (tok[x][sx] for x in ("At", "Bt", "Kt", "Rt"))
        V2, BV2, SG2 = (hv[x][sx] for x in ("V2", "BV2", "SG2"))
        P.op("dve", lambda E: E.memset(S2[:], 0.0), [], [S2])
        for c in range(33):
            k = c % 2
            if c == 32:
                save_state(hp, 0)
                P.dma(sw[:64, :].rearrange("v (h k) -> v h k", h=2),
                      C.swkv[hp * 128:(hp + 1) * 128, :].rearrange("(h v) k -> v h k", h=2), writes=[sw])
                b = bank()
                P.tr(b[:, 0:64], sw[:64, :], C.ident[:64, :64], [sw, C.ident], [b])
                P.cp("act", S2[:], b[:, 0:64], [b], [S2])
            t, cp = (c // 2, c % 2) if c < 32 else (16, 0)
            rows = slice(cp * 64, cp * 64 + 64)
            q = {nm: Q[nm][k] for nm in Q}

            def prod(dst, lhsT, rhs, mask, rd):
                b = bank()
                P.mm(b[:, 0:128], lhsT, rhs, True, True, rd, [b])
                P.tt("dve", dst[:], b[:, 0:128], mask[:], ALU.mult, [b, mask], [dst])

            prod(q["BD_A"], At[rows, t, :], SEL2[rows, :], BDM, [At, SEL2])
            prod(q["BD_B"], Bt[rows, t, :], SEL2[rows, :], BDM, [Bt, SEL2])
            prod(q["BD_K"], Kt[rows, t, :], SEL2[rows, :], BDM, [Kt, SEL2])
            prod(q["BD_R"], Rt[rows, t, :], SEL2[rows, :], BDM, [Rt, SEL2])
            prod(q["BDT_B"], SEL2[rows, :], Bt[rows, t, :], BDM, [Bt, SEL2])
            prod(q["BDT_K"], SEL2[rows, :], Kt[rows, t, :], BDM, [Kt, SEL2])
            prod(q["N"], q["BD_B"][:], q["BD_A"][:], MUS, [q["BD_B"], q["BD_A"]])
            prod(q["NT"], q["BD_A"][:], q["BD_B"][:], MLS, [q["BD_B"], q["BD_A"]])
            prod(q["AakT"], q["BD_K"][:], q["BD_A"][:], MUS, [q["BD_K"], q["BD_A"]])
            prod(q["WbT"], q["BD_B"][:], q["BD_R"][:], MUI, [q["BD_B"], q["BD_R"]])
            prod(q["WkT"], q["BD_K"][:], q["BD_R"][:], MUI, [q["BD_K"], q["BD_R"]])
            Pm = q["Pm"]
            P.tt("dve", Pm[:], q["N"][:], C.ident[:], ALU.add, [q["N"], C.ident], [Pm])
            M, MT = q["N"], q["NT"]
            alt = [(q["Ma"], q["MTa"]), (q["Mb"], q["MTb"])]
            for lvl in range(5):
                M2, M2T = alt[lvl % 2]
                if lvl < 4:
                    b = bank()
                    P.mm(b[:, 0:128], MT[:], M[:], True, True, [MT, M], [b])
                    P.cp("act", M2[:], b[:, 0:128], [b], [M2])
                b = bank()
                P.mm(b[:, 0:128], M[:], MT[:], True, True, [MT, M], [b])
                P.cp("act", M2T[:], b[:, 0:128], [b], [M2T])
                b = bank()
                P.mm(b[:, 0:128], M2T[:], Pm[:], True, True, [M2T, Pm], [b])
                P.tt("dve", Pm[:], b[:, 0:128], Pm[:], ALU.add, [b, Pm], [Pm])
                M, MT = M2, M2T
            Vc = V2[:, c, :]
            b = bank()
            P.mm(b[:, 0:64], q["BD_A"][:], S2[:], True, False, [q["BD_A"], S2], [b])
            P.mm(b[:, 0:64], q["AakT"][:], Vc, False, True, [q["AakT"], V2], [b])
            P.cp("act", Xs[:], b[:, 0:64], [b], [Xs])
            b = bank()
            P.mm(b[:, 0:64], Pm[:], Xs[:], True, True, [Pm, Xs], [b])
            P.cp("act", Us[:], b[:, 0:64], [b], [Us])
            bo = bank()
            P.mm(bo[:, 0:64], q["BD_R"][:], S2[:], True, False, [q["BD_R"], S2], [bo])
            P.mm(bo[:, 0:64], q["WbT"][:], Us[:], False, False, [q["WbT"], Us], [bo])
            P.mm(bo[:, 0:64], q["WkT"][:], Vc, False, True, [q["WkT"], V2], [bo])
            bd = bank()
            P.mm(bd[:, 0:64], q["BDT_B"][:], Us[:], True, False, [q["BDT_B"], Us], [bd])
            P.mm(bd[:, 0:64], q["BDT_K"][:], Vc, False, True, [q["BDT_K"], V2], [bd])
            gcol = C.gC[:, hp, c:c + 1]
            P.ts("dve", S2g[:], S2[:], gcol, None, ALU.mult, reads=[S2, C.gC], writes=[S2g])
            P.stt("dve", S2[:], bd[:, 0:64], gcol, S2g[:], ALU.mult, ALU.add, [bd, C.gC, S2g], [S2])
            ce, s1, s2_, og_ = cen[k], st1[k], st2[k], ogt[k]
            P.op("dve", lambda E, bo=bo, s1=s1: E.reduce_sum(s1[:], bo[:, 0:64], AX.X), [bo], [s1])
            P.ts("dve", s1[:], s1[:], -1.0 / 64, None, ALU.mult, reads=[s1], writes=[s1])
            P.ts("dve", ce[:], bo[:, 0:64], s1[:, 0:1], None, ALU.add, reads=[bo, s1], writes=[ce])
            P.act(junk[:], ce[:], AF.Square, [ce], [junk, s2_], accum_out=s2_[:])
            P.act(s2_[:], s2_[:], AF.Sqrt, [s2_, epsg], [s2_], bias=epsg[:], scale=1.0 / 64)
            P.op("dve", lambda E, s2_=s2_: E.reciprocal(s2_[:], s2_[:]), [s2_], [s2_])
            P.stt("dve", ce[:], ce[:], s2_[:, 0:1], lnw2[:, hp, :], ALU.mult, ALU.mult, [ce, s2_, lnw2], [ce])
            P.tt("pool", ce[:], ce[:], lnb2[:, hp, :], ALU.add, [ce, lnb2], [ce])
            P.tt("pool", ce[:], ce[:], BV2[:, c, :], ALU.add, [ce, BV2], [ce])
            P.tt("pool", og_[:], ce[:], SG2[:, c, :], ALU.mult, [ce, SG2], [og_])
            for h in range(2):
                P.dma(C.og[c * 64:(c + 1) * 64, hp * 128 + h * 64: hp * 128 + (h + 1) * 64], og_[h * 64:(h + 1) * 64, :],
                      reads=[og_])
        save_state(hp, 1)
    return True


def phase_out(C, x_src, li):
    P = C.P
    lst = C.lst
    ogT = P.sb([128, 16, NTOK], BF16, stack=lst, name="ogT%d" % li)
    ob = [P.sb([128, D], BF16, stack=lst) for _ in range(2)]
    for t in range(NT):
        n = tn(t)
        o = ob[t % 2]
        P.dma(o[:n], C.og[t * 128:t * 128 + n, :], writes=[o])
        for gi in range(2):
            for jj in range(8):
                c = gi * 8 + jj
                P.tr(C.ptb[:, jj * 128:jj * 128 + n], o[:n, c * 128:(c + 1) * 128], C.identb[:n, :n],
                     [o, C.identb], [C.ptb])
            src = C.ptb[:].rearrange("p (a b) -> p a b", a=8)[:, :, :n]
            P.cp("act" if gi == 0 else "dve", ogT[:, gi * 8:(gi + 1) * 8, t * 128:t * 128 + n], src, [C.ptb], [ogT])
    wf_bufs = [P.sb([128, 4, 512], F32, stack=lst) for _ in range(2)]
    wbs = [P.sb([128, 16, 512], BF16, stack=lst) for _ in range(2)]
    xb = [P.sb([128, 512], F32, stack=lst) for _ in range(3)]
    ctr = [0]
    k = 0
    for blk in range(4):
        wb = wbs[blk % 2]
        load_w_block(C, wf_bufs, wb, [(C.w_out[li][:, blk * 512:(blk + 1) * 512], 0)], 512, ctr)
        for t in range(NT):
            n = tn(t)
            ps = C.pp[t % 2]
            xt = xb[k % 3]
            k += 1
            P.dma(xt[:n], x_src[t * 128:t * 128 + n, blk * 512:(blk + 1) * 512], writes=[xt])
            for c in range(16):
                P.mm(ps[:n, :], ogT[:, c, t * 128:t * 128 + n], wb[:, c, :], c == 0, c == 15, [ogT, wb], [ps])
            P.tt("dve", xt[:n], xt[:n], ps[:n, :], ALU.add, [xt, ps], [xt])
            P.dma(C.xs[t * 128:t * 128 + n, blk * 512:(blk + 1) * 512], xt[:n], reads=[xt])


def phase_final(C, x_src):
    P = C.P
    lst = C.lst
    P.dma(C.gb[:], C.final_g[0:1, :].partition_broadcast(128), writes=[C.gb])
    xb = [P.sb([128, D], F32, stack=lst) for _ in range(2)]
    junk = P.sb([128, D], BF16, stack=lst)
    ss = [P.sb([128, 1], F32, stack=lst) for _ in range(2)]
    rs = [P.sb([128, 1], F32, stack=lst) for _ in range(2)]
    for t in range(NT):
        n = tn(t)
        xt, s, r = xb[t % 2], ss[t % 2], rs[t % 2]
        P.dma(xt[:n], x_src[t * 128:t * 128 + n, :], writes=[xt])
        P.act(junk[:n], xt[:n], AF.Square, [xt], [junk, s], accum_out=s[:n])
        P.act(r[:n], s[:n], AF.Sqrt, [s, C.eps6], [r], bias=C.eps6[:n], scale=1.0 / D)
        P.op("dve", lambda E, r=r, n=n: E.reciprocal(r[:n], r[:n]), [r], [r])
        P.stt("dve", xt[:n], xt[:n], r[:n, 0:1], C.gb[:n], ALU.mult, ALU.mult, [xt, r, C.gb], [xt])
        P.dma(C.y[t * 128:t * 128 + n, :], xt[:n], reads=[xt])


def const_masks():
    i = np.arange(128)
    same = (i[:, None] // 64) == (i[None, :] // 64)
    s_, t_ = i[:, None] % 64, i[None, :] % 64
    sel2 = (s_ == t_)
    m = np.stack([sel2, same, same & (s_ < t_), same & (s_ > t_), same & (s_ <= t_)]).astype(np.float32)
    return np.ascontiguousarray(m)


def const_selc():
    i = np.arange(128)
    return np.ascontiguousarray(((i[:, None] // 64) == np.arange(2)[None, :]).astype(np.float32))


def rope_table(dh):
    half = dh // 2
    pos = np.concatenate([np.arange(2048), PAST + np.arange(64)]).astype(np.float32)
    inv = np.power(np.float32(10000.0), -np.arange(half, dtype=np.float32) * np.float32(2.0 / dh)).astype(np.float32)
    ang = pos[:, None] * inv[None, :]
    cos = np.cos(ang).astype(np.float32)
    sin = np.sin(ang).astype(np.float32)
    tab = np.stack([np.concatenate([cos, cos], 1), np.concatenate([-sin, sin], 1)], 1)
    return np.ascontiguousarray(tab.astype(np.float32))


_NC_CACHE = {}


def kernel(x_prompt, x_sample, cache_a_k, cache_a_v, cache_b_k, cache_b_v, cache_b_kidx, state_c_wkv, state_c_shift,
           norm_g, final_g, w_out, a_w_in, a_lam, a_subln_g, b_w_in, c_mu, c_w_rkvg, c_w0, c_w_la, c_w_lb, c_a0,
           c_a_la, c_a_lb, c_k_k, c_k_a, c_r_k, c_ln_w, c_ln_b):
    stage = int(os.environ.get("MK_STAGE", "4"))
    skey = (stage, os.environ.get("MK_HEADS"), os.environ.get("MK_NOATT"), os.environ.get("MK_NOSAMP"), os.environ.get("MK_B"), os.environ.get("MK_BK"), os.environ.get("MK_NHP"))
    ncores = int(os.environ.get("MK_CORES", "8"))
    f = lambda a: np.ascontiguousarray(np.asarray(a, dtype=np.float32))
    if skey not in _NC_CACHE:
        _NC_CACHE[skey] = build_nc(stage)
    nc = _NC_CACHE[skey]
    shared = {
        "norm_g": f(norm_g), "final_g": f(final_g).reshape(1, D), "w_out": f(w_out), "a_w_in": f(a_w_in),
        "a_lam": f(a_lam).reshape(2, 256), "a_subln_g": f(a_subln_g), "b_w_in": f(b_w_in)[0],
        "c_mu": f(c_mu)[0], "c_w_rkvg": f(c_w_rkvg)[0], "c_w0": f(c_w0), "c_w_la": f(c_w_la)[0],
        "c_w_lb": f(c_w_lb)[0], "c_a0": f(c_a0), "c_a_la": f(c_a_la)[0], "c_a_lb": f(c_a_lb)[0],
        "c_k_k": f(c_k_k), "c_k_a": f(c_k_a), "c_r_k": f(c_r_k).reshape(1, D), "c_ln_w": f(c_ln_w),
        "c_ln_b": f(c_ln_b), "rope64": rope_table(64), "rope128": rope_table(128),
        "ident": np.eye(128, dtype=np.float32), "cmask": const_masks(), "selc": const_selc(),
    }
    in_maps = []
    for c in range(ncores):
        m = dict(shared)
        m["xin"] = np.concatenate([f(x_prompt[c // 2]), f(x_sample[c])], 0)
        m["cak"] = f(cache_a_k[:, c]).reshape(2, PAST, D)
        m["cav"] = f(cache_a_v[:, c]).reshape(2, PAST, D)
        m["cbk"] = f(cache_b_k[0, c]).reshape(PAST, 512)
        m["cbv"] = f(cache_b_v[0, c]).reshape(PAST, 512)
        m["cbi"] = f(cache_b_kidx[0, c])
        m["swkv"] = f(state_c_wkv[0, c]).reshape(2048, 64)
        m["sshift"] = f(state_c_shift[0, c]).reshape(1, D)
        in_maps.append(m)
    tr = bool(int(os.environ.get('MK_TRACE', '0')))
    res = run_bass_kernel_spmd(nc, in_maps, core_ids=list(range(ncores)), **({'trace': True} if tr else {}))
    if tr:
        print('EXEC_NS', res.exec_time_ns)
    R = res.results
    nb = 4
    y_p = np.zeros((4, 2048, D), np.float32)
    y_s = np.zeros((8, 64, D), np.float32)
    akp = np.zeros((2, 4, 2048, 16, 128), np.float32)
    avp = np.zeros_like(akp)
    aks = np.zeros((2, 8, 64, 16, 128), np.float32)
    avs = np.zeros_like(aks)
    bkp = np.zeros((1, 4, 2048, 4, 128), np.float32)
    bvp = np.zeros_like(bkp)
    bip = np.zeros((1, 4, 2048, 64), np.float32)
    bks = np.zeros((1, 8, 64, 4, 128), np.float32)
    bvs = np.zeros_like(bks)
    bis = np.zeros((1, 8, 64, 64), np.float32)
    cwp = np.zeros((1, 4, 32, 64, 64), np.float32)
    csp = np.zeros((1, 4, D), np.float32)
    cws = np.zeros((1, 8, 32, 64, 64), np.float32)
    css = np.zeros((1, 8, D), np.float32)
    for c in range(ncores):
        r = R[c]
        p = c // 2
        if c % 2 == 0:
            y_p[p] = r["y"][:2048]
            akp[:, p] = r["o_ak"][:, :2048].reshape(2, 2048, 16, 128)
            avp[:, p] = r["o_av"][:, :2048].reshape(2, 2048, 16, 128)
            bkp[0, p] = r["o_bk"][:2048].reshape(2048, 4, 128)
            bvp[0, p] = r["o_bv"][:2048].reshape(2048, 4, 128)
            bip[0, p] = r["o_bi"][:2048]
            cwp[0, p] = r["o_cw"][0].reshape(32, 64, 64)
            csp[0, p] = r["o_cs"][0]
        y_s[c] = r["y"][2048:]
        aks[:, c] = r["o_ak"][:, 2048:].reshape(2, 64, 16, 128)
        avs[:, c] = r["o_av"][:, 2048:].reshape(2, 64, 16, 128)
        bks[0, c] = r["o_bk"][2048:].reshape(64, 4, 128)
        bvs[0, c] = r["o_bv"][2048:].reshape(64, 4, 128)
        bis[0, c] = r["o_bi"][2048:]
        cws[0, c] = r["o_cw"][1].reshape(32, 64, 64)
        css[0, c] = r["o_cs"][1]
    return (y_p, y_s, akp, avp, aks, avs, bkp, bvp, bip, bks, bvs, bis, cwp, csp, cws, css)
```

```python
import os
import math
import numpy as np
from contextlib import ExitStack
import concourse.bass as bass
import concourse.mybir as mybir
from concourse.bass_utils import run_bass_kernel_spmd

F32 = mybir.dt.float32
BF16 = mybir.dt.bfloat16
AF = mybir.ActivationFunctionType
ALU = mybir.AluOpType
AX = mybir.AxisListType

D = 2048
NT = 17
NTOK = 2112
PAST = 4096
DEPTH = 4


def tn(t):
    return 128 if t < 16 else 64


class Buf:
    __slots__ = ("name", "lw", "rd")

    def __init__(self, name=""):
        self.name = name
        self.lw = None
        self.rd = []


class T:
    def __init__(self, t, nb=1, name=""):
        self.t = t
        self.bs = [Buf(name + str(i)) for i in range(nb)]

    def __getitem__(self, k):
        return self.t[k]


class Prog:
    ENGS = ("pe", "act", "dve", "pool", "sp")
    NDMA = {"sp": 12, "act": 4, "pool": 8}

    def __init__(self, nc, stack):
        self.nc = nc
        self.stack = stack
        self.q = {e: [] for e in self.ENGS}
        self.sem = {e: stack.enter_context(nc.semaphore("s_" + e)) for e in self.ENGS}
        self.cnt = {e: 0 for e in self.ENGS}
        self.seen = {e: {} for e in self.ENGS}
        self.pend = {e: {} for e in self.ENGS}
        self.dsem = {}
        self.dcnt = {}
        self.di = {}
        for e, n in self.NDMA.items():
            self.dsem[e] = [stack.enter_context(nc.semaphore("d_%s%d" % (e, i))) for i in range(n)]
            self.dcnt[e] = [0] * n
            self.di[e] = 0
        self.n_ins = 0
        self.uid = 0

    def sb(self, shape, dt, nb=1, name=None, stack=None):
        self.uid += 1
        name = name or "t%d" % self.uid
        t = (stack or self.stack).enter_context(self.nc.sbuf_tensor(name, list(shape), dt))
        return T(t, nb, name)

    def ps(self, shape, dt=F32, nb=1, name=None, stack=None):
        self.uid += 1
        name = name or "p%d" % self.uid
        t = (stack or self.stack).enter_context(self.nc.psum_tensor(name, list(shape), dt))
        return T(t, nb, name)

    def _bufs(self, xs):
        out = []
        for x in xs:
            if isinstance(x, T):
                out.extend(x.bs)
            elif isinstance(x, Buf):
                out.append(x)
            elif x is None:
                pass
            else:
                out.extend(self._bufs(x))
        return out

    def fence(self):
        snap = {}
        for e in self.ENGS:
            if self.cnt[e] > 0:
                snap[("c", e)] = self.cnt[e]
        for e in self.NDMA:
            for i, c in enumerate(self.dcnt[e]):
                if c > 0:
                    snap[("d", e, i)] = c
        for e in self.ENGS:
            for k, v in snap.items():
                if e == "pe" and k == ("c", "pe"):
                    continue
                if self.pend[e].get(k, 0) < v:
                    self.pend[e][k] = v

    def op(self, eng, fn, reads=(), writes=(), dma=False):
        reads = self._bufs(reads)
        writes = self._bufs(writes)
        deps = dict(self.pend[eng])
        self.pend[eng] = {}

        def add(ev):
            if ev is None:
                return
            k, v = ev
            if eng == "pe" and k == ("c", "pe"):
                return
            if deps.get(k, 0) < v:
                deps[k] = v

        for b in reads:
            add(b.lw)
        for b in writes:
            add(b.lw)
            for r in b.rd:
                add(r)
        if dma:
            i = self.di[eng] % len(self.dsem[eng])
            self.di[eng] += 1
            if self.dcnt[eng][i] > 0:
                add((("d", eng, i), self.dcnt[eng][i]))
            self.dcnt[eng][i] += 16
            ev = (("d", eng, i), self.dcnt[eng][i])
        else:
            self.cnt[eng] += 1
            ev = (("c", eng), self.cnt[eng])
        waits = []
        seen = self.seen[eng]
        for k, v in deps.items():
            if seen.get(k, 0) < v:
                seen[k] = v
                waits.append((k, v))
        self.q[eng].append((waits, fn, ev))
        for b in reads:
            b.rd.append(ev)
            if len(b.rd) > 64:
                m = {}
                for k, v in b.rd:
                    if m.get(k, 0) < v:
                        m[k] = v
                b.rd = list(m.items())
        for b in writes:
            b.lw = ev
            b.rd = []
        self.n_ins += 1
        return ev

    def _semof(self, k):
        if k[0] == "c":
            return self.sem[k[1]]
        return self.dsem[k[1]][k[2]]

    def emit(self):
        nc = self.nc
        fin = []
        for e in self.NDMA:
            for i, c in enumerate(self.dcnt[e]):
                if c > 0:
                    fin.append((("d", e, i), c))
        for e in self.ENGS:
            if e != "sp" and self.cnt[e] > 0:
                fin.append((("c", e), self.cnt[e]))
        engobj = {"pe": "tensor", "act": "scalar", "dve": "vector", "pool": "gpsimd", "sp": "sync"}
        with nc.Block() as block:
            for e in self.ENGS:
                def body(engine, e=e):
                    for waits, fn, ev in self.q[e]:
                        for k, v in waits:
                            engine.wait_ge(self._semof(k), v)
                        ins = fn(engine)
                        ins.then_inc(self._semof(ev[0]), 16 if ev[0][0] == "d" else 1)
                    if e == "sp":
                        for k, v in fin:
                            engine.wait_ge(self._semof(k), v)
                getattr(block, engobj[e])(body)

    def dma(self, out, in_, reads=(), writes=(), eng="sp", **kw):
        return self.op(eng, lambda E: E.dma_start(out=out, in_=in_, **kw), reads, writes, dma=True)

    def mm(self, out, lhsT, rhs, start, stop, reads=(), writes=()):
        return self.op("pe", lambda E: E.matmul(out, lhsT, rhs, start=start, stop=stop), reads, writes)

    def tr(self, out, in_, ident, reads=(), writes=()):
        return self.op("pe", lambda E: E.transpose(out, in_, ident), reads, writes)

    def act(self, out, in_, func, reads=(), writes=(), **kw):
        return self.op("act", lambda E: E.activation(out=out, in_=in_, func=func, **kw), reads, writes)

    def tt(self, eng, out, in0, in1, op, reads=(), writes=()):
        return self.op(eng, lambda E: E.tensor_tensor(out, in0, in1, op), reads, writes)

    def ts(self, eng, out, in0, s1, s2, op0, op1=None, reads=(), writes=(), **kw):
        if op1 is None:
            return self.op(eng, lambda E: E.tensor_scalar(out, in0, s1, s2, op0, **kw), reads, writes)
        return self.op(eng, lambda E: E.tensor_scalar(out, in0, s1, s2, op0, op1, **kw), reads, writes)

    def stt(self, eng, out, in0, scalar, in1, op0, op1, reads=(), writes=()):
        return self.op(eng, lambda E: E.scalar_tensor_tensor(out, in0, scalar, in1, op0, op1), reads, writes)

    def cp(self, eng, out, in_, reads=(), writes=()):
        if eng == "act":
            return self.op(eng, lambda E: E.copy(out, in_), reads, writes)
        return self.op(eng, lambda E: E.tensor_copy(out, in_), reads, writes)


class Ctx:
    pass


def build_nc(stage):
    nc = bass.Bass("TRN2", target_bir_lowering=False)
    C = Ctx()
    C.nc = nc
    dt_in = lambda n, s, d=F32: nc.dram_tensor(n, list(s), d, kind="ExternalInput").ap()
    dt_out = lambda n, s, d=F32: nc.dram_tensor(n, list(s), d, kind="ExternalOutput").ap()
    dt_tmp = lambda n, s, d=F32: nc.dram_tensor(n, list(s), d, kind="Internal").ap()
    C.xin = dt_in("xin", [NTOK, D])
    C.cak = dt_in("cak", [2, PAST, 16 * 128])
    C.cav = dt_in("cav", [2, PAST, 16 * 128])
    C.cbk = dt_in("cbk", [PAST, 4 * 128])
    C.cbv = dt_in("cbv", [PAST, 4 * 128])
    C.cbi = dt_in("cbi", [PAST, 64])
    C.swkv = dt_in("swkv", [32 * 64, 64])
    C.sshift = dt_in("sshift", [1, D])
    C.norm_g = dt_in("norm_g", [DEPTH, D])
    C.final_g = dt_in("final_g", [1, D])
    C.w_out = dt_in("w_out", [DEPTH, D, D])
    C.a_w_in = dt_in("a_w_in", [2, D, 4 * D])
    C.a_lam = dt_in("a_lam", [2, 4 * 64])
    C.a_subln_g = dt_in("a_subln_g", [2, 128])
    C.b_w_in = dt_in("b_w_in", [D, 6224])
    C.c_mu = dt_in("c_mu", [6, D])
    C.c_w_rkvg = dt_in("c_w_rkvg", [4, D, D])
    C.c_w0 = dt_in("c_w0", [1, D])
    C.c_w_la = dt_in("c_w_la", [D, 96])
    C.c_w_lb = dt_in("c_w_lb", [96, D])
    C.c_a0 = dt_in("c_a0", [1, D])
    C.c_a_la = dt_in("c_a_la", [D, 96])
    C.c_a_lb = dt_in("c_a_lb", [96, D])
    C.c_k_k = dt_in("c_k_k", [1, D])
    C.c_k_a = dt_in("c_k_a", [1, D])
    C.c_r_k = dt_in("c_r_k", [1, D])
    C.c_ln_w = dt_in("c_ln_w", [1, D])
    C.c_ln_b = dt_in("c_ln_b", [1, D])
    C.rope64 = dt_in("rope64", [NTOK, 2, 64])
    C.rope128 = dt_in("rope128", [NTOK, 2, 128])
    C.ident_d = dt_in("ident", [128, 128])
    C.y = dt_out("y", [NTOK, D])
    C.o_ak = dt_out("o_ak", [2, NTOK, D])
    C.o_av = dt_out("o_av", [2, NTOK, D])
    C.o_bk = dt_out("o_bk", [NTOK, 512])
    C.o_bv = dt_out("o_bv", [NTOK, 512])
    C.o_bi = dt_out("o_bi", [NTOK, 64])
    C.o_cw = dt_out("o_cw", [2, 32 * 64, 64])
    C.o_cs = dt_out("o_cs", [2, D])
    C.xs = dt_tmp("xs", [NTOK, D])
    C.og = dt_tmp("og", [NTOK, D], BF16)
    C.qid = dt_tmp("qid", [NTOK, 1024], BF16)
    C.qd = dt_tmp("qd", [NTOK, D], BF16)
    C.sgd = dt_tmp("sgd", [NTOK, D], BF16)
    C.maskd = dt_tmp("maskd", [NTOK, 4160], BF16)
    for nm in ("c_r", "c_k", "c_v", "c_sg", "c_bv"):
        setattr(C, nm, dt_tmp(nm, [NTOK, D]))
    for nm in ("c_At", "c_Bt", "c_Kt", "c_Rt"):
        setattr(C, nm, dt_tmp(nm, [NTOK, D], BF16))
    C.cmask = dt_in("cmask", [5, 128, 128])
    C.selc = dt_in("selc", [128, 2])

    with ExitStack() as st:
        P = Prog(nc, st)
        C.P = P
        C.ident = P.sb([128, 128], F32, name="identf")
        C.identb = P.sb([128, 128], BF16, name="identb")
        P.dma(C.ident[:], C.ident_d[:, :], writes=[C.ident])
        P.cp("dve", C.identb[:], C.ident[:], [C.ident], [C.identb])
        C.gb = P.sb([128, D], F32, name="gb")
        C.eps6 = P.sb([128, 1], F32, name="eps6")
        P.op("dve", lambda E: E.memset(C.eps6[:], 1e-6), [], [C.eps6])
        C.eps5 = P.sb([128, 1], F32, name="eps5")
        P.op("dve", lambda E: E.memset(C.eps5[:], 1e-5), [], [C.eps5])
        C.pp = [P.ps([128, 512], F32, name="pp%d" % i) for i in range(2)]
        C.ptb = P.ps([128, 1024], BF16, name="ptb")
        C.psc = [P.ps([128, 512], F32, name="psc%d" % i) for i in range(2)]
        C.po = [P.ps([128, 512], F32, name="po%d" % i) for i in range(2)]
        C.ptf = P.ps([128, 512], F32, name="ptf")

        x_src = C.xin
        for li in range(DEPTH):
            if li >= stage:
                break
            kind, j = li % 3, li // 3
            with ExitStack() as cst:
                if kind == 2:
                    C.tT = P.sb([128, NTOK], BF16, stack=cst, name='c_tT')
                    C.aT = P.sb([128, NTOK], BF16, stack=cst, name='c_aT')
                    C.gC = P.sb([128, 16, 34], F32, stack=cst, name='c_gC')
                done = False
                with ExitStack() as hs:
                    C.hT = P.sb([128, 16, NTOK], BF16, stack=hs, name="hT%d" % li)
                    with ExitStack() as lst:
                        C.lst = lst
                        phase_norm(C, x_src, C.norm_g[li:li + 1, :], li, want_last=(kind == 2))
                        P.fence()
                    with ExitStack() as lst:
                        C.lst = lst
                        if kind == 0:
                            layer_a(C, li, j)
                            done = True
                        elif kind == 1:
                            done = layer_b(C, li)
                        else:
                            done = layer_c1(C, li)
                        P.fence()
                if kind == 2 and done:
                    with ExitStack() as lst:
                        C.lst = lst
                        layer_c2(C, li)
                        P.fence()
                    with ExitStack() as lst:
                        C.lst = lst
                        layer_c3(C, li)
                        P.fence()
            if not done:
                continue
            with ExitStack() as lst:
                C.lst = lst
                phase_out(C, x_src, li)
                P.fence()
            x_src = C.xs
        with ExitStack() as lst:
            C.lst = lst
            phase_final(C, x_src)
        P.emit()
    return nc


def phase_norm(C, x_src, g_row, li, want_last=False):
    P = C.P
    lst = C.lst
    P.dma(C.gb[:], g_row.partition_broadcast(128), writes=[C.gb])
    xb = [P.sb([128, D], F32, stack=lst) for _ in range(2)]
    junk = P.sb([128, D], BF16, stack=lst)
    hb = [P.sb([128, D], BF16, stack=lst) for _ in range(2)]
    ss = [P.sb([128, 1], F32, stack=lst) for _ in range(2)]
    rs = [P.sb([128, 1], F32, stack=lst) for _ in range(2)]
    for t in range(NT):
        n = tn(t)
        xt, h, s, r = xb[t % 2], hb[t % 2], ss[t % 2], rs[t % 2]
        P.dma(xt[:n], x_src[t * 128:t * 128 + n, :], writes=[xt])
        P.act(junk[:n], xt[:n], AF.Square, [xt], [junk, s], accum_out=s[:n])
        P.act(r[:n], s[:n], AF.Sqrt, [s, C.eps6], [r], bias=C.eps6[:n], scale=1.0 / D)
        P.op("dve", lambda E, r=r, n=n: E.reciprocal(r[:n], r[:n]), [r], [r])
        P.stt("dve", h[:n], xt[:n], r[:n, 0:1], C.gb[:n], ALU.mult, ALU.mult, [xt, r, C.gb], [h])
        if want_last and t >= 15:
            hl = P.sb([128, D], F32, stack=lst)
            P.stt("dve", hl[:n], xt[:n], r[:n, 0:1], C.gb[:n], ALU.mult, ALU.mult, [xt, r, C.gb], [hl])
            P.dma(C.o_cs[t - 15:t - 14, :], hl[n - 1:n, :], reads=[hl])
        for gi in range(2):
            for jj in range(8):
                c = gi * 8 + jj
                P.tr(C.ptb[:, jj * 128:jj * 128 + n], h[:n, c * 128:(c + 1) * 128], C.identb[:n, :n],
                     [h, C.identb], [C.ptb])
            src = C.ptb[:].rearrange("p (a b) -> p a b", a=8)[:, :, :n]
            P.cp("act" if gi == 0 else "dve", C.hT[:, gi * 8:(gi + 1) * 8, t * 128:t * 128 + n], src, [C.ptb], [C.hT])


def load_w_block(C, wf_bufs, wb, pieces, ncols, ctr):
    P = C.P
    for qtr in range(4):
        wf = wf_bufs[(ctr[0]) % 2]
        for ap, off in pieces:
            w = ap.shape[1]
            P.dma(wf[:, :, off:off + w], ap[qtr * 512:(qtr + 1) * 512, :].rearrange("(c p) n -> p c n", p=128),
                  writes=[wf])
        eng = ("pool", "act", "dve")[ctr[0] % 2]
        P.cp(eng, wb[:, qtr * 4:(qtr + 1) * 4, :ncols], wf[:, :, :ncols], [wf], [wb])
        ctr[0] += 1


def layer_a(C, li, j):
    P = C.P
    lst = C.lst
    lam_init = 0.8 - 0.6 * math.exp(-0.3 * li)
    rope = P.sb([128, NT, 2, 64], F32, stack=lst, name="rope_a%d" % li)
    P.dma(rope[:, 0:16], C.rope64[0:2048].rearrange("(t p) a b -> p t a b", p=128), writes=[rope])
    P.dma(rope[:64, 16], C.rope64[2048:2112], writes=[rope])
    lamt = P.sb([128, 4, 64], F32, stack=lst)
    P.dma(lamt[:].rearrange("p a b -> p (a b)"), C.a_lam[j:j + 1, :].partition_broadcast(128), writes=[lamt])
    lprod = P.sb([128, 2, 64], F32, stack=lst)
    lsum = P.sb([128, 2], F32, stack=lst)
    neglam = P.sb([128, 1], F32, stack=lst)
    lv = lamt[:].rearrange("p (a b) c -> p a b c", b=2)
    P.tt("dve", lprod[:], lv[:, :, 0, :], lv[:, :, 1, :], ALU.mult, [lamt], [lprod])
    P.op("dve", lambda E: E.reduce_sum(lsum[:], lprod[:], AX.X), [lprod], [lsum])
    P.act(lsum[:], lsum[:], AF.Exp, [lsum], [lsum])
    P.tt("dve", neglam[:], lsum[:, 1:2], lsum[:, 0:1], ALU.subtract, [lsum], [neglam])
    P.ts("dve", neglam[:], neglam[:], -lam_init, None, ALU.add, reads=[neglam], writes=[neglam])
    gsub = P.sb([128, 128], F32, stack=lst)
    P.dma(gsub[:], C.a_subln_g[j:j + 1, :].partition_broadcast(128), writes=[gsub])
    P.ts("dve", gsub[:], gsub[:], 1.0 - lam_init, None, ALU.mult, reads=[gsub], writes=[gsub])

    wf_bufs = [P.sb([128, 4, 512], F32, stack=lst) for _ in range(2)]
    wbs = [P.sb([128, 16, 512], BF16, stack=lst) for _ in range(1)]
    qkT = [P.sb([128, 2, NTOK], BF16, stack=lst) for _ in range(2)]
    vaug = [P.sb([128, NT, 132], BF16, stack=lst) for _ in range(2)]
    sg = [P.sb([128, NT, 128], BF16, stack=lst) for _ in range(2)]
    for v in vaug:
        P.op("pool", lambda E, v=v: E.memset(v[:, :, 128:129], 1.0), [], [v])
    ra = [P.sb([128, 256], F32, stack=lst) for _ in range(2)]
    rt = [P.sb([128, 256], F32, stack=lst) for _ in range(2)]
    kv = [P.sb([128, 2, 128], F32, stack=lst) for _ in range(2)]
    qkb = [P.sb([128, 256], BF16, stack=lst) for _ in range(2)]
    PT = [P.sb([128, 2048], BF16, stack=lst) for _ in range(4)]
    ot = [P.sb([128, 128], F32, stack=lst) for _ in range(2)]
    ogt = [P.sb([128, 128], BF16, stack=lst) for _ in range(2)]
    rr = [P.sb([128, 2], F32, stack=lst) for _ in range(2)]
    s2 = [P.sb([128, 1], F32, stack=lst) for _ in range(2)]
    junk = P.sb([128, 128], F32, stack=lst)
    kcs = [P.sb([128, 8, 128], F32, stack=lst) for _ in range(1)]
    vcs = [P.sb([128, 8, 128], F32, stack=lst) for _ in range(1)]
    kTc = P.sb([128, PAST], BF16, stack=lst)
    vca = P.sb([128, 32, 132], BF16, stack=lst)
    P.op("pool", lambda E: E.memset(vca[:, :, 128:129], 1.0), [], [vca])
    PTs = [P.sb([128, 512], BF16, stack=lst) for _ in range(2)]

    ctr = [0]
    cnt = [0]
    W = C.a_w_in[j]
    scale = 64 ** -0.5

    def epilogue(h, t, n, po, k):
        o, og_, r_, s_ = ot[k % 2], ogt[k % 2], rr[k % 2], s2[k % 2]
        pov = po[:].rearrange("p (a b) -> p a b", a=2)
        P.op("dve", lambda E: E.reciprocal(r_[:n], pov[:n, :, 128]), [po], [r_])
        P.tt("dve", r_[:n, 1:2], r_[:n, 1:2], neglam[:n], ALU.mult, [r_, neglam], [r_])
        P.ts("dve", o[:n], po[:n, 0:128], r_[:n, 0:1], None, ALU.mult, reads=[po, r_], writes=[o])
        P.stt("dve", o[:n], po[:n, 256:384], r_[:n, 1:2], o[:n], ALU.mult, ALU.add, [po, r_, o], [o])
        P.act(junk[:n], o[:n], AF.Square, [o], [junk, s_], accum_out=s_[:n])
        P.act(s_[:n], s_[:n], AF.Sqrt, [s_, C.eps5], [s_], bias=C.eps5[:n], scale=1.0 / 128)
        P.op("dve", lambda E: E.reciprocal(s_[:n], s_[:n]), [s_], [s_])
        P.stt("dve", o[:n], o[:n], s_[:n, 0:1], gsub[:n], ALU.mult, ALU.mult, [o, s_, gsub], [o])
        P.tt("pool", og_[:n], o[:n], sg[h % 2][:n, t, :], ALU.mult, [o, sg[h % 2]], [og_])
        P.dma(C.og[t * 128:t * 128 + n, h * 128:(h + 1) * 128], og_[:n], reads=[og_])

    NH = int(os.environ.get('MK_HEADS', '16'))
    NOATT = int(os.environ.get('MK_NOATT', '0'))
    NOSAMP = int(os.environ.get('MK_NOSAMP', '0'))
    def gen_proj(h):
        wb = wbs[0]
        pieces = [(W[:, blk * D + h * 128: blk * D + (h + 1) * 128], blk * 128) for blk in range(4)]
        load_w_block(C, wf_bufs, wb, pieces, 512, ctr)
        QK, V, SG = qkT[h % 2], vaug[h % 2], sg[h % 2]
        for t in range(NT):
            n = tn(t)
            ps = C.pp[t % 2]
            for c in range(16):
                P.mm(ps[:n, :], C.hT[:, c, t * 128:t * 128 + n], wb[:, c, :], c == 0, c == 15, [C.hT, wb], [ps])
            k = cntp[0]
            cntp[0] += 1
            a_, t_, kv_, qb = ra[k % 2], rt[k % 2], kv[k % 2], qkb[k % 2]
            xv = ps[:n, 0:256].rearrange("p (a b c) -> p a b c", a=4, b=2)
            cosb = rope[:n, t, 0, :].rearrange("p (b c) -> p b c", b=2).unsqueeze(1).to_broadcast([n, 4, 2, 32])
            sinb = rope[:n, t, 1, :].rearrange("p (b c) -> p b c", b=2).unsqueeze(1).to_broadcast([n, 4, 2, 32])
            av = a_[:n].rearrange("p (a b c) -> p a b c", a=4, b=2)
            tv = t_[:n].rearrange("p (a b c) -> p a b c", a=4, b=2)
            P.tt("dve", av, xv, cosb, ALU.mult, [ps, rope], [a_])
            P.tt("dve", tv[:, :, 0, :], xv[:, :, 1, :], sinb[:, :, 0, :], ALU.mult, [ps, rope], [t_])
            P.tt("dve", tv[:, :, 1, :], xv[:, :, 0, :], sinb[:, :, 1, :], ALU.mult, [ps, rope], [t_])
            P.tt("pool", qb[:n, 0:128], a_[:n, 0:128], t_[:n, 0:128], ALU.add, [a_, t_], [qb])
            P.tt("pool", kv_[:n, 0, :], a_[:n, 128:256], t_[:n, 128:256], ALU.add, [a_, t_], [kv_])
            P.cp("pool", qb[:n, 128:256], kv_[:n, 0, :], [kv_], [qb])
            P.cp("act", kv_[:n, 1, :], ps[:n, 256:384], [ps], [kv_])
            P.cp("act", V[:n, t, 0:128], ps[:n, 256:384], [ps], [V])
            P.act(SG[:n, t, :], ps[:n, 384:512], AF.Silu, [ps], [SG])
            P.dma(C.o_ak[j, t * 128:t * 128 + n, h * 128:(h + 1) * 128], kv_[:n, 0, :], reads=[kv_])
            P.dma(C.o_av[j, t * 128:t * 128 + n, h * 128:(h + 1) * 128], kv_[:n, 1, :], reads=[kv_])
            for m in range(2):
                P.tr(C.ptb[:, m * 128:m * 128 + n], qb[:n, m * 128:(m + 1) * 128], C.identb[:n, :n],
                     [qb, C.identb], [C.ptb])
            P.cp("act", QK[:, :, t * 128:t * 128 + n],
                 C.ptb[:, 0:256].rearrange("p (a b) -> p a b", a=2)[:, :, :n], [C.ptb], [QK])
            yield

    def gen_att(h):
        QK, V, SG = qkT[h % 2], vaug[h % 2], sg[h % 2]
        for i in range(0 if not NOATT else 16, 16):
            k = cnt[0]
            cnt[0] += 1
            po = C.po[k % 2]
            for m in range(2):
                pt = PT[(k % 2) * 2 + m]
                nb = i + 1
                for g0 in range(0, nb, 4):
                    g1 = min(nb, g0 + 4)
                    sc = C.psc[(g0 // 4 + m) % 2]
                    for jb in range(g0, g1):
                        P.mm(sc[:, (jb - g0) * 128:(jb - g0 + 1) * 128],
                             QK[m * 64:(m + 1) * 64, 1, jb * 128:(jb + 1) * 128],
                             QK[m * 64:(m + 1) * 64, 0, i * 128:(i + 1) * 128], True, True, [QK], [sc])
                    P.act(pt[:, g0 * 128:g1 * 128], sc[:, 0:(g1 - g0) * 128], AF.Exp, [sc], [pt], scale=scale)
                P.op("pool", lambda E, pt=pt, i=i: E.memset(pt[64:128, i * 128:i * 128 + 64], 0.0), [], [pt])
                for jb in range(nb):
                    P.mm(po[:, m * 256:m * 256 + 129], pt[:, jb * 128:(jb + 1) * 128], V[:, jb, 0:129],
                         jb == 0, jb == nb - 1, [pt, V], [po])
                yield
            epilogue(h, i, 128, po, k)
            yield
        if NOSAMP:
            return
        for half in range(4):
            yield
            kc, vc = kcs[0], vcs[0]
            P.dma(kc[:], C.cak[j, half * 1024:(half + 1) * 1024, h * 128:(h + 1) * 128].rearrange(
                "(b p) d -> p b d", p=128), writes=[kc])
            P.dma(vc[:], C.cav[j, half * 1024:(half + 1) * 1024, h * 128:(h + 1) * 128].rearrange(
                "(b p) d -> p b d", p=128), writes=[vc])
            P.cp("pool", vca[:, half * 8:(half + 1) * 8, 0:128], vc[:], [vc], [vca])
            for g in range(2):
                for b in range(4):
                    P.tr(C.ptf[:, b * 128:(b + 1) * 128], kc[:, g * 4 + b, :], C.ident[:], [kc, C.ident], [C.ptf])
                P.cp("dve" if g % 2 else "act", kTc[:, half * 1024 + g * 512: half * 1024 + (g + 1) * 512], C.ptf[:],
                     [C.ptf], [kTc])
        k = cnt[0]
        cnt[0] += 1
        po = C.po[k % 2]
        for m in range(2):
            qs = QK[m * 64:(m + 1) * 64, 0, 2048:2112]
            for g in range(5):
                sc = C.psc[(g + m) % 2]
                pts = PTs[(g + m) % 2]
                nblk = 8 if g < 4 else 1
                for b in range(nblk):
                    jb = g * 8 + b
                    if jb < 32:
                        P.mm(sc[:, b * 64:(b + 1) * 64], kTc[m * 64:(m + 1) * 64, jb * 128:(jb + 1) * 128], qs,
                             True, True, [kTc, QK], [sc])
                    else:
                        P.mm(sc[:64, b * 64:(b + 1) * 64], QK[m * 64:(m + 1) * 64, 1, 2048:2112], qs,
                             True, True, [QK], [sc])
                rows = 128 if g < 4 else 64
                P.act(pts[:rows, 0:nblk * 64], sc[:rows, 0:nblk * 64], AF.Exp, [sc], [pts], scale=scale)
                for b in range(nblk):
                    jb = g * 8 + b
                    if jb < 32:
                        P.mm(po[:64, m * 256:m * 256 + 129], pts[:, b * 64:(b + 1) * 64], vca[:, jb, 0:129],
                             jb == 0, False, [pts, vca], [po])
                    else:
                        P.mm(po[:64, m * 256:m * 256 + 129], pts[:64, b * 64:(b + 1) * 64], V[:64, 16, 0:129],
                             False, True, [pts, V], [po])
        epilogue(h, 16, 64, po, k)


    def interleave(gens):
        gens = list(gens)
        while gens:
            for g_ in list(gens):
                try:
                    next(g_)
                except StopIteration:
                    gens.remove(g_)

    cntp = [0]
    interleave([gen_proj(0)])
    for h in range(NH):
        gs = [gen_att(h)]
        if h + 1 < NH:
            gs.append(gen_proj(h + 1))
        interleave(gs)


def rope_ops(C, n, xin, H, half, rope, t, a_, t_, rd):
    P = C.P
    Wd = H * 2 * half
    xv = xin.rearrange("p (a b c) -> p a b c", a=H, b=2)
    cosb = rope[:n, t, 0, :].rearrange("p (b c) -> p b c", b=2).unsqueeze(1).to_broadcast([n, H, 2, half])
    sinb = rope[:n, t, 1, :].rearrange("p (b c) -> p b c", b=2).unsqueeze(1).to_broadcast([n, H, 2, half])
    av = a_[:n, :Wd].rearrange("p (a b c) -> p a b c", a=H, b=2)
    tv = t_[:n, :Wd].rearrange("p (a b c) -> p a b c", a=H, b=2)
    P.tt("dve", av, xv, cosb, ALU.mult, rd + [rope], [a_])
    P.tt("dve", tv[:, :, 0, :], xv[:, :, 1, :], sinb[:, :, 0, :], ALU.mult, rd + [rope], [t_])
    P.tt("dve", tv[:, :, 1, :], xv[:, :, 0, :], sinb[:, :, 1, :], ALU.mult, rd + [rope], [t_])


def layer_b(C, li):
    P = C.P
    W = C.b_w_in
    OQ, OK_, OV, OQI, OKI, OG = 0, 2048, 2560, 3072, 4096, 4176
    NEG = -1.0e30
    with ExitStack() as bst:
        kT = P.sb([128, 4, NTOK], BF16, stack=bst, name="b_kT")
        vaug = P.sb([128, NT, 4, 132], BF16, stack=bst, name="b_vaug")
        kiT2 = P.sb([128, NTOK], BF16, stack=bst, name="b_kiT2")
        wis = P.sb([128, NT, 16], F32, stack=bst, name="b_wis")
        for g in range(4):
            P.op("pool", lambda E, g=g: E.memset(vaug[:, :, g, 128:129], 1.0), [], [vaug])
        with ExitStack() as lst:
            rope128 = P.sb([128, NT, 2, 128], F32, stack=lst)
            P.dma(rope128[:, 0:16], C.rope128[0:2048].rearrange("(t p) a b -> p t a b", p=128), writes=[rope128])
            P.dma(rope128[:64, 16], C.rope128[2048:2112], writes=[rope128])
            rope64 = P.sb([128, NT, 2, 64], F32, stack=lst)
            P.dma(rope64[:, 0:16], C.rope64[0:2048].rearrange("(t p) a b -> p t a b", p=128), writes=[rope64])
            P.dma(rope64[:64, 16], C.rope64[2048:2112], writes=[rope64])
            wf_bufs = [P.sb([128, 4, 512], F32, stack=lst) for _ in range(2)]
            wbs = [P.sb([128, 16, 512], BF16, stack=lst) for _ in range(2)]
            ra = [P.sb([128, 512], F32, stack=lst) for _ in range(2)]
            rt = [P.sb([128, 512], F32, stack=lst) for _ in range(2)]
            of = [P.sb([128, 512], F32, stack=lst) for _ in range(2)]
            ob = [P.sb([128, 512], BF16, stack=lst) for _ in range(2)]
            ctr = [0]
            blocks = [("k", OK_, 512), ("v", OV, 512), ("ki", OKI, 80), ("qi", OQI, 512), ("qi", OQI + 512, 512)]
            blocks += [("q", OQ + g * 512, 512) for g in range(4)] + [("g", OG + g * 512, 512) for g in range(4)]
            kk = 0
            blocks = blocks[:int(os.environ.get('MK_BK', '99'))]
            for bi, (kind, off, ncols) in enumerate(blocks):
                wb = wbs[bi % 2]
                load_w_block(C, wf_bufs, wb, [(W[:, off:off + ncols], 0)], ncols, ctr)
                for t in range(NT):
                    n = tn(t)
                    r0 = t * 128
                    ps = C.pp[t % 2]
                    for c in range(16):
                        P.mm(ps[:n, :ncols], C.hT[:, c, r0:r0 + n], wb[:, c, :ncols], c == 0, c == 15, [C.hT, wb], [ps])
                    a_, t_, f_, b_ = ra[kk % 2], rt[kk % 2], of[kk % 2], ob[kk % 2]
                    kk += 1
                    if kind == "k":
                        rope_ops(C, n, ps[:n, 0:512], 4, 64, rope128, t, a_, t_, [ps])
                        P.tt("pool", f_[:n], a_[:n], t_[:n], ALU.add, [a_, t_], [f_])
                        P.dma(C.o_bk[r0:r0 + n, :], f_[:n], reads=[f_])
                        P.cp("act", b_[:n], f_[:n], [f_], [b_])
                        for g in range(4):
                            P.tr(C.ptb[:, g * 128:g * 128 + n], b_[:n, g * 128:(g + 1) * 128], C.identb[:n, :n],
                                 [b_, C.identb], [C.ptb])
                        P.cp("act", kT[:, :, r0:r0 + n], C.ptb[:, 0:512].rearrange("p (a b) -> p a b", a=4)[:, :, :n],
                             [C.ptb], [kT])
                    elif kind == "v":
                        P.cp("act", f_[:n], ps[:n, :], [ps], [f_])
                        P.dma(C.o_bv[r0:r0 + n, :], f_[:n], reads=[f_])
                        for g in range(4):
                            P.cp("dve" if g % 2 else "pool", vaug[:n, t, g, 0:128], f_[:n, g * 128:(g + 1) * 128], [f_], [vaug])
                    elif kind == "ki":
                        rope_ops(C, n, ps[:n, 0:64], 1, 32, rope64, t, a_, t_, [ps])
                        P.tt("pool", f_[:n, 0:64], a_[:n, 0:64], t_[:n, 0:64], ALU.add, [a_, t_], [f_])
                        P.dma(C.o_bi[r0:r0 + n, :], f_[:n, 0:64], reads=[f_])
                        P.cp("act", b_[:n, 0:64], f_[:n, 0:64], [f_], [b_])
                        P.cp("act", b_[:n, 64:128], f_[:n, 0:64], [f_], [b_])
                        P.tr(C.ptb[:, 0:n], b_[:n, 0:128], C.identb[:n, :n], [b_, C.identb], [C.ptb])
                        P.cp("act", kiT2[:, r0:r0 + n], C.ptb[:, 0:n], [C.ptb], [kiT2])
                        P.ts("dve", wis[:n, t, :], ps[:n, 64:80], 0.25, None, ALU.mult, reads=[ps], writes=[wis])
                    elif kind == "qi":
                        rope_ops(C, n, ps[:n, 0:512], 8, 32, rope64, t, a_, t_, [ps])
                        P.tt("pool", b_[:n], a_[:n], t_[:n], ALU.add, [a_, t_], [b_])
                        P.dma(C.qid[r0:r0 + n, off - OQI:off - OQI + 512], b_[:n], reads=[b_])
                    elif kind == "q":
                        rope_ops(C, n, ps[:n, 0:512], 4, 64, rope128, t, a_, t_, [ps])
                        P.tt("pool", b_[:n], a_[:n], t_[:n], ALU.add, [a_, t_], [b_])
                        P.dma(C.qd[r0:r0 + n, off - OQ:off - OQ + 512], b_[:n], reads=[b_])
                    else:
                        P.act(b_[:n], ps[:n, :], AF.Silu, [ps], [b_])
                        P.dma(C.sgd[r0:r0 + n, off - OG:off - OG + 512], b_[:n], reads=[b_])
            P.fence()
        if int(os.environ.get('MK_B', '3')) < 2:
            return False
        with ExitStack() as lst:
            acc = P.sb([128, 4160], F32, stack=lst)
            work = P.sb([128, 4160], F32, stack=lst)
            maskb = P.sb([128, 4160], BF16, stack=lst)
            kiS = P.sb([128, 4160], BF16, stack=lst)
            qi_t = [P.sb([128, 1024], BF16, stack=lst) for _ in range(2)]
            qiT = [P.sb([128, 8, 128], BF16, stack=lst) for _ in range(2)]
            rl = [P.sb([128, 512], F32, stack=lst) for _ in range(2)]
            m8 = P.sb([128, 8], F32, stack=lst)
            thr = P.sb([128, 1], F32, stack=lst)
            cst = [P.sb([128, 8, 128], F32, stack=lst) for _ in range(2)]
            for qd_ in range(4):
                cs = cst[qd_ % 2]
                src = C.cbi[qd_ * 1024:(qd_ + 1) * 1024, :].rearrange("(b p) d -> p b d", p=128)
                P.dma(cs[:, :, 0:64], src, writes=[cs])
                P.dma(cs[:, :, 64:128], src, writes=[cs])
                for g in range(2):
                    for b in range(4):
                        P.tr(C.ptf[:, b * 128:(b + 1) * 128], cs[:, g * 4 + b, :], C.ident[:], [cs, C.ident], [C.ptf])
                    P.cp("dve" if g % 2 else "act", kiS[:, qd_ * 1024 + g * 512: qd_ * 1024 + (g + 1) * 512], C.ptf[:],
                         [C.ptf], [kiS])
            P.cp("pool", kiS[:, 4096:4160], kiT2[:, 2048:2112], [kiT2], [kiS])
            for t in range(NT):
                n = tn(t)
                r0 = t * 128
                S = 128 * (t + 1) if t < 16 else 4160
                keys = kiT2 if t < 16 else kiS
                qt, qT_ = qi_t[t % 2], qiT[t % 2]
                P.dma(qt[:n], C.qid[r0:r0 + n, :], writes=[qt])
                for hp in range(8):
                    P.tr(C.ptb[:, hp * 128:hp * 128 + n], qt[:n, hp * 128:(hp + 1) * 128], C.identb[:n, :n],
                         [qt, C.identb], [C.ptb])
                P.cp("act", qT_[:, :, :n], C.ptb[:].rearrange("p (a b) -> p a b", a=8)[:, :, :n], [C.ptb], [qT_])
                kq = 0
                for hd in range(16):
                    hp, par = hd // 2, hd % 2
                    for c0 in range(0, S, 512):
                        w = min(512, S - c0)
                        sc = C.psc[kq % 2]
                        r_ = rl[kq % 2]
                        kq += 1
                        P.mm(sc[:n, :w], qT_[par * 64:(par + 1) * 64, hp, :n], keys[par * 64:(par + 1) * 64, c0:c0 + w],
                             True, True, [qT_, keys], [sc])
                        P.act(r_[:n, :w], sc[:n, :w], AF.Relu, [sc], [r_], scale=0.125)
                        if hd == 0:
                            P.ts("dve", acc[:n, c0:c0 + w], r_[:n, :w], wis[:n, t, 0:1], None, ALU.mult,
                                 reads=[r_, wis], writes=[acc])
                        else:
                            P.stt("dve", acc[:n, c0:c0 + w], r_[:n, :w], wis[:n, t, hd:hd + 1], acc[:n, c0:c0 + w],
                                  ALU.mult, ALU.add, [r_, wis, acc], [acc])
                if t < 16:
                    P.op("dve", lambda E, t=t: E.memset(acc[0:64, t * 128 + 64:(t + 1) * 128], NEG), [], [acc])
                if t >= 2:
                    for rnd in range(32):
                        srcw = acc if rnd == 0 else work
                        P.op("dve", lambda E, srcw=srcw, n=n, S=S: E.max(out=m8[:n], in_=srcw[:n, :S]), [srcw], [m8])
                        if rnd < 31:
                            P.op("dve", lambda E, srcw=srcw, n=n, S=S: E.match_replace(
                                out=work[:n, :S], in_to_replace=m8[:n], in_values=srcw[:n, :S], imm_value=NEG),
                                [srcw, m8], [work])
                    P.cp("dve", thr[:n], m8[:n, 7:8], [m8], [thr])
                else:
                    P.op("dve", lambda E, n=n: E.memset(thr[:n], -1.0e29), [], [thr])
                P.ts("dve", maskb[:n, :S], acc[:n, :S], thr[:n, 0:1], None, ALU.is_ge, reads=[acc, thr], writes=[maskb])
                P.dma(C.maskd[r0:r0 + n, 0:S], maskb[:n, :S], reads=[maskb])
            P.fence()
        if int(os.environ.get('MK_B', '3')) < 3:
            return False
        with ExitStack() as lst:
            q_t = [P.sb([128, D], BF16, stack=lst) for _ in range(2)]
            sg_t = [P.sb([128, D], BF16, stack=lst) for _ in range(2)]
            mk = [P.sb([128, 4160], BF16, stack=lst) for _ in range(1)]
            maskT = P.sb([128, 2112], BF16, stack=lst)
            qT = P.sb([128, 16, 128], BF16, stack=lst)
            PTt = P.sb([128, 8448], BF16, stack=lst)
            kTc = P.sb([128, PAST], BF16, stack=lst)
            vca = P.sb([128, 32, 132], BF16, stack=lst)
            P.op("pool", lambda E: E.memset(vca[:, :, 128:129], 1.0), [], [vca])
            kcs = [P.sb([128, 8, 128], F32, stack=lst) for _ in range(2)]
            vcs = [P.sb([128, 8, 128], F32, stack=lst) for _ in range(2)]
            rr = P.sb([128, 4], F32, stack=lst)
            of = P.sb([128, 2, 128], F32, stack=lst)
            ogt = [P.sb([128, D], BF16, stack=lst) for _ in range(2)]
            scale = 128 ** -0.5
            kq = 0
            for t in range(NT):
                n = tn(t)
                r0 = t * 128
                S = 128 * (t + 1) if t < 16 else 4160
                nb = (S + 127) // 128
                q_, s_, m_, og_ = q_t[t % 2], sg_t[t % 2], mk[0], ogt[t % 2]
                P.dma(q_[:n], C.qd[r0:r0 + n, :], writes=[q_])
                P.dma(s_[:n], C.sgd[r0:r0 + n, :], writes=[s_])
                P.dma(m_[:n, :S], C.maskd[r0:r0 + n, 0:S], writes=[m_])
                for j0 in range(0, nb, 8):
                    j1 = min(nb, j0 + 8)
                    rmax = 0
                    for jb in range(j0, j1):
                        rows = min(128, S - jb * 128)
                        rmax = max(rmax, rows)
                        P.tr(C.ptb[:rows, (jb - j0) * 128:(jb - j0) * 128 + n], m_[:n, jb * 128:jb * 128 + rows],
                             C.identb[:n, :n], [m_, C.identb], [C.ptb])
                    full = [jb for jb in range(j0, j1) if min(128, S - jb * 128) == 128]
                    if full:
                        P.cp("act", maskT[:, j0 * n:(j0 + len(full)) * n].rearrange("p (a b) -> p a b", b=n),
                             C.ptb[:].rearrange("p (a b) -> p a b", a=8)[:, 0:len(full), :n], [C.ptb], [maskT])
                    if len(full) < j1 - j0:
                        jb = j1 - 1
                        P.cp("act", maskT[:64, jb * n:(jb + 1) * n], C.ptb[:64, (jb - j0) * 128:(jb - j0) * 128 + n],
                             [C.ptb], [maskT])
                for hp in range(2):
                    for hh in range(8):
                        hd = hp * 8 + hh
                        P.tr(C.ptb[:, hh * 128:hh * 128 + n], q_[:n, hd * 128:(hd + 1) * 128], C.identb[:n, :n],
                             [q_, C.identb], [C.ptb])
                    P.cp("dve", qT[:, hp * 8:(hp + 1) * 8, :n], C.ptb[:].rearrange("p (a b) -> p a b", a=8)[:, :, :n],
                         [C.ptb], [qT])
                for g in range(4):
                    if t == 16:
                        for qd_ in range(4):
                            kc, vc = kcs[qd_ % 2], vcs[qd_ % 2]
                            P.dma(kc[:], C.cbk[qd_ * 1024:(qd_ + 1) * 1024, g * 128:(g + 1) * 128].rearrange(
                                "(b p) d -> p b d", p=128), writes=[kc])
                            P.dma(vc[:], C.cbv[qd_ * 1024:(qd_ + 1) * 1024, g * 128:(g + 1) * 128].rearrange(
                                "(b p) d -> p b d", p=128), writes=[vc])
                            P.cp("pool", vca[:, qd_ * 8:(qd_ + 1) * 8, 0:128], vc[:], [vc], [vca])
                            for gg in range(2):
                                for b in range(4):
                                    P.tr(C.ptf[:, b * 128:(b + 1) * 128], kc[:, gg * 4 + b, :], C.ident[:],
                                         [kc, C.ident], [C.ptf])
                                P.cp("dve" if gg % 2 else "act",
                                     kTc[:, qd_ * 1024 + gg * 512: qd_ * 1024 + (gg + 1) * 512], C.ptf[:], [C.ptf], [kTc])

                    def kblk(jb):
                        if t < 16:
                            return kT[:, g, jb * 128:(jb + 1) * 128], vaug[:, jb, g, 0:129], 128, [kT], [vaug]
                        if jb < 32:
                            return kTc[:, jb * 128:(jb + 1) * 128], vca[:, jb, 0:129], 128, [kTc], [vca]
                        return kT[:, g, 2048:2112], vaug[:64, 16, g, 0:129], 64, [kT], [vaug]

                    for jb in range(nb):
                        ka, va, rows, kr, vr = kblk(jb)
                        sc = C.psc[kq % 2]
                        kq += 1
                        P.mm(sc[:rows, 0:4 * n], ka, qT[:, g * 4:(g + 1) * 4, :n], True, True, kr + [qT], [sc])
                        pt = PTt[:rows, jb * 4 * n:(jb + 1) * 4 * n]
                        P.act(pt, sc[:rows, 0:4 * n], AF.Exp, [sc], [PTt], scale=scale)
                        pt3 = pt.rearrange("p (a b) -> p a b", a=4)
                        mT = maskT[:rows, jb * n:(jb + 1) * n].unsqueeze(1).to_broadcast([rows, 4, n])
                        P.tt("pool" if jb % 2 else "dve", pt3, pt3, mT, ALU.mult, [PTt, maskT], [PTt])
                    for r in range(4):
                        po = C.po[r // 2]
                        for jb in range(nb):
                            ka, va, rows, kr, vr = kblk(jb)
                            P.mm(po[:n, (r % 2) * 256:(r % 2) * 256 + 129],
                                 PTt[:rows, jb * 4 * n + r * n: jb * 4 * n + (r + 1) * n], va,
                                 jb == 0, jb == nb - 1, [PTt] + vr, [po])
                    for b in range(2):
                        po = C.po[b]
                        pov = po[:].rearrange("p (a b) -> p a b", a=2)
                        P.op("dve", lambda E, pov=pov, b=b, n=n: E.reciprocal(rr[:n, b * 2:b * 2 + 2], pov[:n, :, 128]),
                             [po], [rr])
                        P.tt("dve", of[:n], pov[:n, :, 0:128],
                             rr[:n, b * 2:b * 2 + 2].unsqueeze(2).to_broadcast([n, 2, 128]), ALU.mult, [po, rr], [of])
                        c0 = g * 512 + b * 256
                        P.tt("pool", og_[:n, c0:c0 + 256], of[:n].rearrange("p a b -> p (a b)"), s_[:n, c0:c0 + 256],
                             ALU.mult, [of, s_], [og_])
                P.dma(C.og[r0:r0 + n, :], og_[:n], reads=[og_])
            P.fence()
    return True


def layer_c1(C, li):
    P = C.P
    lst = C.lst
    hT = C.hT
    ms = P.sb([128, 128], F32, stack=lst)
    P.dma(ms[0:96, :], C.c_mu.rearrange("n (c p) -> (n c) p", p=128), writes=[ms])
    P.dma(ms[96:112, :], C.sshift.rearrange("o (c p) -> (o c) p", p=128), writes=[ms])
    P.tr(C.ptf[:, 0:112], ms[0:112, :], C.ident[:112, :112], [ms, C.ident], [C.ptf])
    mu = P.sb([128, 112], F32, stack=lst)
    om = P.sb([128, 96], F32, stack=lst)
    P.cp("act", mu[:], C.ptf[:, 0:112], [C.ptf], [mu])
    P.ts("dve", om[:], mu[:, 0:96], -1.0, 1.0, ALU.mult, ALU.add, reads=[mu], writes=[om])
    lerpT = P.sb([128, 16, NTOK], BF16, stack=lst)
    tmpb = [P.sb([128, NTOK], BF16, stack=lst) for _ in range(2)]
    wf_bufs = [P.sb([128, 4, 512], F32, stack=lst) for _ in range(2)]
    wbs = [P.sb([128, 16, 512], BF16, stack=lst) for _ in range(1)]
    ev = [P.sb([128, 512], F32, stack=lst) for _ in range(3)]
    ctr = [0]
    kk = 0
    dsts = [C.c_r, C.c_k, C.c_v, C.c_sg]
    for nidx in range(6):
        for c in range(16):
            tm = tmpb[c % 2]
            col = nidx * 16 + c
            P.ts("dve", tm[:], hT[:, c, :], om[:, col:col + 1], None, ALU.mult, reads=[hT, om], writes=[tm])
            P.stt("dve", lerpT[:, c, 1:NTOK], hT[:, c, 0:NTOK - 1], mu[:, col:col + 1], tm[:, 1:NTOK],
                  ALU.mult, ALU.add, [hT, mu, tm], [lerpT])
            P.cp("dve", lerpT[:, c, 0:1], tm[:, 0:1], [tm], [lerpT])
            P.stt("dve", lerpT[:, c, 2048:2049], mu[:, 96 + c:97 + c], mu[:, col:col + 1], tm[:, 2048:2049],
                  ALU.mult, ALU.add, [mu, tm], [lerpT])
        if nidx < 4:
            for blk in range(4):
                wb = wbs[0]
                load_w_block(C, wf_bufs, wb, [(C.c_w_rkvg[nidx][:, blk * 512:(blk + 1) * 512], 0)], 512, ctr)
                for t in range(NT):
                    n = tn(t)
                    r0 = t * 128
                    ps = C.pp[t % 2]
                    for c in range(16):
                        P.mm(ps[:n, :], lerpT[:, c, r0:r0 + n], wb[:, c, :], c == 0, c == 15, [lerpT, wb], [ps])
                    e_ = ev[kk % 3]
                    kk += 1
                    if nidx < 3:
                        P.cp("act", e_[:n], ps[:n, :], [ps], [e_])
                    else:
                        P.act(e_[:n], ps[:n, :], AF.Silu, [ps], [e_])
                    P.dma(dsts[nidx][r0:r0 + n, blk * 512:(blk + 1) * 512], e_[:n], reads=[e_])
        else:
            wsrc = C.c_w_la if nidx == 4 else C.c_a_la
            dstT = C.tT if nidx == 4 else C.aT
            wf = wf_bufs[0]
            wb = wbs[0]
            for qtr in range(4):
                P.dma(wf[:, :, 0:96], wsrc[qtr * 512:(qtr + 1) * 512, :].rearrange("(c p) n -> p c n", p=128), writes=[wf])
                P.cp("dve", wb[:, qtr * 4:(qtr + 1) * 4, 0:96], wf[:, :, 0:96], [wf], [wb])
            for tb in range(0, NTOK, 512):
                w = min(512, NTOK - tb)
                ps = C.pp[(tb // 512) % 2]
                for c in range(16):
                    P.mm(ps[:96, :w], wb[:, c, 0:96], lerpT[:, c, tb:tb + w], c == 0, c == 15, [lerpT, wb], [ps])
                if nidx == 4:
                    P.act(dstT[:96, tb:tb + w], ps[:96, :w], AF.Tanh, [ps], [dstT])
                else:
                    P.cp("act", dstT[:96, tb:tb + w], ps[:96, :w], [ps], [dstT])
    return True


def layer_c2(C, li):
    P = C.P
    lst = C.lst
    cb = {}
    for nm in ("c_w0", "c_a0", "c_k_k", "c_k_a", "c_r_k"):
        cb[nm] = P.sb([128, D], F32, stack=lst, name="cb_" + nm)
        P.dma(cb[nm][:], getattr(C, nm)[0:1, :].partition_broadcast(128), writes=[cb[nm]])
    lbf = P.sb([128, D], F32, stack=lst)
    wlb = P.sb([128, D], BF16, stack=lst)
    alb = P.sb([128, D], BF16, stack=lst)
    P.dma(lbf[:96], C.c_w_lb[:, :], writes=[lbf])
    P.cp("dve", wlb[:96], lbf[:96], [lbf], [wlb])
    P.dma(lbf[:96], C.c_a_lb[:, :], writes=[lbf])
    P.cp("dve", alb[:96], lbf[:96], [lbf], [alb])
    tri = P.sb([128, 128], F32, stack=lst)
    P.dma(tri[:], C.cmask[4], writes=[tri])
    selc = P.sb([128, 2], F32, stack=lst)
    P.dma(selc[:], C.selc[:, :], writes=[selc])
    eps12 = P.sb([128, 1], F32, stack=lst)
    B = {nm: P.sb([128, D], F32, stack=lst, name="c2_" + nm) for nm in
         ("R", "K", "V", "A", "W", "KK", "T1", "T2", "CUM", "G")}
    ssq = P.sb([128, 32], F32, stack=lst)
    ob16 = [P.sb([128, D], BF16, stack=lst) for _ in range(2)]
    bs = P.sb([128, 32], F32, stack=lst)
    v3 = lambda tl, n: tl[:n].rearrange("p (a b) -> p a b", a=32)
    for t in range(NT):
        n = tn(t)
        r0 = t * 128
        nc_ = 2 if t < 16 else 1
        R, K, V, A, W, KK, T1, T2, CUM, G = (B[x] for x in ("R", "K", "V", "A", "W", "KK", "T1", "T2", "CUM", "G"))
        P.dma(R[:n], C.c_r[r0:r0 + n, :], writes=[R])
        P.dma(K[:n], C.c_k[r0:r0 + n, :], writes=[K])
        P.dma(V[:n], C.c_v[r0:r0 + n, :], writes=[V])
        for blk in range(4):
            cs = slice(blk * 512, (blk + 1) * 512)
            ps = C.pp[blk % 2]
            P.mm(ps[:n, :], C.tT[:96, r0:r0 + n], wlb[:96, cs], True, True, [C.tT, wlb], [ps])
            P.tt("dve", W[:n, cs], ps[:n, :], cb["c_w0"][:n, cs], ALU.add, [ps, cb["c_w0"]], [W])
            ps2 = C.psc[blk % 2]
            P.mm(ps2[:n, :], C.aT[:96, r0:r0 + n], alb[:96, cs], True, True, [C.aT, alb], [ps2])
            P.tt("dve", A[:n, cs], ps2[:n, :], cb["c_a0"][:n, cs], ALU.add, [ps2, cb["c_a0"]], [A])
        P.act(W[:n], W[:n], AF.Sigmoid, [W], [W])
        P.ts("pool", W[:n], W[:n], -math.exp(-0.5), None, ALU.mult, reads=[W], writes=[W])
        P.act(A[:n], A[:n], AF.Sigmoid, [A], [A])
        P.tt("pool", KK[:n], K[:n], cb["c_k_k"][:n], ALU.mult, [K, cb["c_k_k"]], [KK])
        P.tt("pool", T1[:n], KK[:n], KK[:n], ALU.mult, [KK], [T1])
        P.op("dve", lambda E, n=n, T1=T1: E.reduce_sum(ssq[:n], v3(T1, n), AX.X), [T1], [ssq])
        P.act(ssq[:n], ssq[:n], AF.Sqrt, [ssq], [ssq])
        P.ts("dve", ssq[:n], ssq[:n], 1e-12, None, ALU.max, reads=[ssq], writes=[ssq])
        P.op("dve", lambda E, n=n: E.reciprocal(ssq[:n], ssq[:n]), [ssq], [ssq])
        P.tt("dve", v3(KK, n), v3(KK, n), ssq[:n].unsqueeze(2).to_broadcast([n, 32, 64]), ALU.mult, [KK, ssq], [KK])
        P.stt("dve", T1[:n], A[:n], -1.0, cb["c_k_a"][:n], ALU.add, ALU.mult, [A, cb["c_k_a"]], [T1])
        P.stt("dve", T2[:n], T1[:n], 1.0, K[:n], ALU.add, ALU.mult, [T1, K], [T2])
        P.tt("pool", T1[:n], R[:n], T2[:n], ALU.mult, [R, T2], [T1])
        P.tt("pool", T1[:n], T1[:n], cb["c_r_k"][:n], ALU.mult, [T1, cb["c_r_k"]], [T1])
        P.op("dve", lambda E, n=n, T1=T1: E.reduce_sum(bs[:n], v3(T1, n), AX.X), [T1], [bs])
        P.tt("dve", v3(T1, n), v3(V, n), bs[:n].unsqueeze(2).to_broadcast([n, 32, 64]), ALU.mult, [V, bs], [T1])
        P.dma(C.c_bv[r0:r0 + n, :], T1[:n], reads=[T1])
        for blk in range(4):
            cs = slice(blk * 512, (blk + 1) * 512)
            ps = C.po[blk % 2]
            P.mm(ps[:n, :], tri[:n, :n], W[:n, cs], True, True, [tri, W], [ps])
            P.cp("act", CUM[:n, cs], ps[:n, :], [ps], [CUM])
        for fc in range(16):
            P.mm(C.ptf[:, fc * 2:fc * 2 + nc_], W[:n, fc * 128:(fc + 1) * 128], selc[:n, 0:nc_], True, True,
                 [W, selc], [C.ptf])
        P.act(C.gC[:, :, t * 2:t * 2 + nc_], C.ptf[:, 0:32].rearrange("p (a b) -> p a b", b=2)[:, :, 0:nc_], AF.Exp,
              [C.ptf], [C.gC])
        P.act(G[:n], CUM[:n], AF.Exp, [CUM], [G])
        P.tt("dve", ob16[0][:n], R[:n], G[:n], ALU.mult, [R, G], [ob16[0]])
        P.dma(C.c_Rt[r0:r0 + n, :], ob16[0][:n], reads=[ob16[0]])
        P.act(G[:n], CUM[:n], AF.Exp, [CUM], [G], scale=-1.0)
        P.tt("dve", ob16[1][:n], T2[:n], G[:n], ALU.mult, [T2, G], [ob16[1]])
        P.dma(C.c_Kt[r0:r0 + n, :], ob16[1][:n], reads=[ob16[1]])
        P.tt("dve", A[:n], A[:n], KK[:n], ALU.mult, [A, KK], [A])
        P.tt("dve", ob16[0][:n], A[:n], G[:n], ALU.mult, [A, G], [ob16[0]])
        P.dma(C.c_Bt[r0:r0 + n, :], ob16[0][:n], reads=[ob16[0]])
        P.tt("dve", CUM[:n], CUM[:n], W[:n], ALU.subtract, [CUM, W], [CUM])
        P.act(G[:n], CUM[:n], AF.Exp, [CUM], [G])
        P.stt("dve", ob16[1][:n], KK[:n], -1.0, G[:n], ALU.mult, ALU.mult, [KK, G], [ob16[1]])
        P.dma(C.c_At[r0:r0 + n, :], ob16[1][:n], reads=[ob16[1]])


def layer_c3(C, li):
    P = C.P
    lst = C.lst
    mk = [P.sb([128, 128], F32, stack=lst, name="c3m%d" % i) for i in range(5)]
    for i in range(5):
        P.dma(mk[i][:], C.cmask[i], writes=[mk[i]])
    SEL2f, BDM, MUS, MLS, MUI = mk
    SEL2 = P.sb([128, 128], BF16, stack=lst, name="c3sel2b")
    P.cp("dve", SEL2[:], SEL2f[:], [SEL2f], [SEL2])
    lnw2 = P.sb([128, 16, 64], F32, stack=lst)
    lnb2 = P.sb([128, 16, 64], F32, stack=lst)
    for h in range(2):
        for dst, src in ((lnw2, C.c_ln_w), (lnb2, C.c_ln_b)):
            sv = src[0:1, :].rearrange("o (a h v) -> o a h v", h=2, v=64)[:, :, h, :]
            P.dma(dst[h * 64:(h + 1) * 64], sv.partition_broadcast(64), writes=[dst])
    epsg = P.sb([128, 1], F32, stack=lst)
    P.op("dve", lambda E: E.memset(epsg[:], 64e-5), [], [epsg])
    banks = [C.pp[0], C.pp[1], C.psc[0], C.psc[1], C.po[0], C.po[1], C.ptf]
    bk = [0]

    def bank():
        b = banks[bk[0] % len(banks)]
        bk[0] += 1
        return b

    tok = {nm: [P.sb([128, NT, 128], BF16, stack=lst, name="c3_%s%d" % (nm, i)) for i in range(2)]
           for nm in ("At", "Bt", "Kt", "Rt")}
    hv = {nm: [P.sb([128, 33, 64], F32, stack=lst, name="c3_%s%d" % (nm, i)) for i in range(2)]
          for nm in ("V2", "BV2", "SG2")}
    srcs = {"At": C.c_At, "Bt": C.c_Bt, "Kt": C.c_Kt, "Rt": C.c_Rt, "V2": C.c_v, "BV2": C.c_bv, "SG2": C.c_sg}
    sq = lambda nm: [[P.sb([128, 128], F32, stack=lst, name="c3q_%s%d" % (nm, i)) for i in range(2)]]
    Q = {nm: [P.sb([128, 128], BF16, stack=lst, name="c3q_%s%d" % (nm, i)) for i in range(2)]
         for nm in ("BD_A", "BD_B", "BD_K", "BD_R", "BDT_B", "BDT_K", "N", "NT", "AakT", "WbT", "WkT", "Pm",
                    "Ma", "MTa", "Mb", "MTb")}
    S2s = [P.sb([128, 64], F32, stack=lst) for _ in range(2)]
    S2gs = [P.sb([128, 64], F32, stack=lst) for _ in range(2)]
    Xss = [P.sb([128, 64], BF16, stack=lst) for _ in range(2)]
    Uss = [P.sb([128, 64], BF16, stack=lst) for _ in range(2)]
    cen = [P.sb([128, 64], F32, stack=lst) for _ in range(4)]
    junk = P.sb([128, 64], F32, stack=lst)
    st1 = [P.sb([128, 1], F32, stack=lst) for _ in range(4)]
    st2 = [P.sb([128, 1], F32, stack=lst) for _ in range(4)]
    ogt = [P.sb([128, 64], BF16, stack=lst) for _ in range(4)]
    sws = [P.sb([128, 128], F32, stack=lst) for _ in range(2)]
    stos = [P.sb([128, 128], F32, stack=lst) for _ in range(2)]
    Q2 = [Q, {nm: [P.sb([128, 128], BF16, stack=lst, name="c3r_%s%d" % (nm, i)) for i in range(2)] for nm in Q}]
    S2bs = [P.sb([128, 64], BF16, stack=lst) for _ in range(2)]
    V2bs = [P.sb([128, 33, 64], BF16, stack=lst) for _ in range(2)]
    NHP = int(os.environ.get("MK_NHP", "16"))

    def stream(hp, sx):
        S2, S2g, Xs, Us, sw, sto = S2s[sx], S2gs[sx], Xss[sx], Uss[sx], sws[sx], stos[sx]
        QQ = Q2[sx]
        S2b, V2b = S2bs[sx], V2bs[sx]

        def save_state(which):
            b = bank()
            P.tr(b[:64, 0:128], S2[:, :], C.ident[:, :], [S2, C.ident], [b])
            P.cp("act", sto[:64, :], b[:64, 0:128], [b], [sto])
            P.dma(C.o_cw[which, hp * 128:(hp + 1) * 128, :].rearrange("(h v) k -> v h k", h=2),
                  sto[:64, :].rearrange("v (h k) -> v h k", h=2), reads=[sto])

        fs = slice(hp * 128, (hp + 1) * 128)
        for nm in ("At", "Bt", "Kt", "Rt"):
            tl = tok[nm][sx]
            P.dma(tl[:, 0:16, :], srcs[nm][0:2048, fs].rearrange("(t p) f -> p t f", p=128), writes=[tl])
            P.dma(tl[:64, 16, :], srcs[nm][2048:2112, fs], writes=[tl])
        for nm in ("V2", "BV2", "SG2"):
            tl = hv[nm][sx]
            for h in range(2):
                hs_ = slice(hp * 128 + h * 64, hp * 128 + (h + 1) * 64)
                P.dma(tl[h * 64:(h + 1) * 64, 0:32, :], srcs[nm][0:2048, hs_].rearrange("(c j) v -> j c v", j=64),
                      writes=[tl])
                P.dma(tl[h * 64:(h + 1) * 64, 32, :], srcs[nm][2048:2112, hs_], writes=[tl])
        At, Bt, Kt, Rt = (tok[x][sx] for x in ("At", "Bt", "Kt", "Rt"))
        V2, BV2, SG2 = (hv[x][sx] for x in ("V2", "BV2", "SG2"))
        P.op("dve", lambda E: E.memset(S2[:], 0.0), [], [S2])
        P.op("dve", lambda E: E.memset(S2b[:], 0.0), [], [S2b])
        P.cp("dve", V2b[:], V2[:], [V2], [V2b])
        yield

        def prod(dst, lhsT, rhs, mask, rd):
            b = bank()
            P.mm(b[:, 0:128], lhsT, rhs, True, True, rd, [b])
            P.tt("dve", dst[:], b[:, 0:128], mask[:], ALU.mult, [b, mask], [dst])

        def pre(c):
            k = c % 2
            t, cp = (c // 2, c % 2) if c < 32 else (16, 0)
            rows = slice(cp * 64, cp * 64 + 64)
            q = {nm: QQ[nm][k] for nm in QQ}
            prod(q["BD_A"], At[rows, t, :], SEL2[rows, :], BDM, [At, SEL2])
            prod(q["BD_B"], Bt[rows, t, :], SEL2[rows, :], BDM, [Bt, SEL2])
            yield
            prod(q["BD_K"], Kt[rows, t, :], SEL2[rows, :], BDM, [Kt, SEL2])
            prod(q["BD_R"], Rt[rows, t, :], SEL2[rows, :], BDM, [Rt, SEL2])
            yield
            prod(q["BDT_B"], SEL2[rows, :], Bt[rows, t, :], BDM, [Bt, SEL2])
            prod(q["BDT_K"], SEL2[rows, :], Kt[rows, t, :], BDM, [Kt, SEL2])
            yield
            prod(q["N"], q["BD_B"][:], q["BD_A"][:], MUS, [q["BD_B"], q["BD_A"]])
            prod(q["NT"], q["BD_A"][:], q["BD_B"][:], MLS, [q["BD_B"], q["BD_A"]])
            yield
            prod(q["AakT"], q["BD_K"][:], q["BD_A"][:], MUS, [q["BD_K"], q["BD_A"]])
            prod(q["WbT"], q["BD_B"][:], q["BD_R"][:], MUI, [q["BD_B"], q["BD_R"]])
            prod(q["WkT"], q["BD_K"][:], q["BD_R"][:], MUI, [q["BD_K"], q["BD_R"]])
            yield
            Pm = q["Pm"]
            P.tt("dve", Pm[:], q["N"][:], C.identb[:], ALU.add, [q["N"], C.identb], [Pm])
            M, MT = q["N"], q["NT"]
            alt = [(q["Ma"], q["MTa"]), (q["Mb"], q["MTb"])]
            for lvl in range(5):
                M2, M2T = alt[lvl % 2]
                if lvl < 4:
                    b = bank()
                    P.mm(b[:, 0:128], MT[:], M[:], True, True, [MT, M], [b])
                    P.cp("act", M2[:], b[:, 0:128], [b], [M2])
                b = bank()
                P.mm(b[:, 0:128], M[:], MT[:], True, True, [MT, M], [b])
                P.cp("act", M2T[:], b[:, 0:128], [b], [M2T])
                yield
                b = bank()
                P.mm(b[:, 0:128], M2T[:], Pm[:], True, True, [M2T, Pm], [b])
                P.tt("dve", Pm[:], b[:, 0:128], Pm[:], ALU.add, [b, Pm], [Pm])
                M, MT = M2, M2T
                yield

        def serial(c):
            k = c % 2
            q = {nm: QQ[nm][k] for nm in QQ}
            Pm = q["Pm"]
            Vc = V2b[:, c, :]
            b = bank()
            P.mm(b[:, 0:64], q["BD_A"][:], S2b[:], True, False, [q["BD_A"], S2b], [b])
            P.mm(b[:, 0:64], q["AakT"][:], Vc, False, True, [q["AakT"], V2b], [b])
            P.cp("act", Xs[:], b[:, 0:64], [b], [Xs])
            yield
            b = bank()
            P.mm(b[:, 0:64], Pm[:], Xs[:], True, True, [Pm, Xs], [b])
            P.cp("act", Us[:], b[:, 0:64], [b], [Us])
            yield
            bo = bank()
            P.mm(bo[:, 0:64], q["BD_R"][:], S2b[:], True, False, [q["BD_R"], S2b], [bo])
            P.mm(bo[:, 0:64], q["WbT"][:], Us[:], False, False, [q["WbT"], Us], [bo])
            P.mm(bo[:, 0:64], q["WkT"][:], Vc, False, True, [q["WkT"], V2b], [bo])
            bd = bank()
            P.mm(bd[:, 0:64], q["BDT_B"][:], Us[:], True, False, [q["BDT_B"], Us], [bd])
            P.mm(bd[:, 0:64], q["BDT_K"][:], Vc, False, True, [q["BDT_K"], V2b], [bd])
            gcol = C.gC[:, hp, c:c + 1]
            P.ts("dve", S2g[:], S2[:], gcol, None, ALU.mult, reads=[S2, C.gC], writes=[S2g])
            P.stt("dve", S2[:], bd[:, 0:64], gcol, S2g[:], ALU.mult, ALU.add, [bd, C.gC, S2g], [S2])
            P.cp("dve", S2b[:], S2[:], [S2], [S2b])
            yield
            e = sx * 2 + k
            ce, s1, s2_, og_ = cen[e], st1[e], st2[e], ogt[e]
            P.op("dve", lambda E, bo=bo, s1=s1: E.reduce_sum(s1[:], bo[:, 0:64], AX.X), [bo], [s1])
            P.ts("dve", s1[:], s1[:], -1.0 / 64, None, ALU.mult, reads=[s1], writes=[s1])
            P.ts("dve", ce[:], bo[:, 0:64], s1[:, 0:1], None, ALU.add, reads=[bo, s1], writes=[ce])
            P.act(junk[:], ce[:], AF.Square, [ce], [junk, s2_], accum_out=s2_[:])
            P.act(s2_[:], s2_[:], AF.Sqrt, [s2_, epsg], [s2_], bias=epsg[:], scale=1.0 / 64)
            P.op("dve", lambda E, s2_=s2_: E.reciprocal(s2_[:], s2_[:]), [s2_], [s2_])
            yield
            P.stt("dve", ce[:], ce[:], s2_[:, 0:1], lnw2[:, hp, :], ALU.mult, ALU.mult, [ce, s2_, lnw2], [ce])
            P.tt("pool", ce[:], ce[:], lnb2[:, hp, :], ALU.add, [ce, lnb2], [ce])
            P.tt("pool", ce[:], ce[:], BV2[:, c, :], ALU.add, [ce, BV2], [ce])
            P.tt("pool", og_[:], ce[:], SG2[:, c, :], ALU.mult, [ce, SG2], [og_])
            for h in range(2):
                P.dma(C.og[c * 64:(c + 1) * 64, hp * 128 + h * 64: hp * 128 + (h + 1) * 64], og_[h * 64:(h + 1) * 64, :],
                      reads=[og_])
            yield

        yield from pre(0)
        for c in range(33):
            if c + 1 < 33:
                yield from pre(c + 1)
            if c == 32:
                save_state(0)
                P.dma(sw[:64, :].rearrange("v (h k) -> v h k", h=2),
                      C.swkv[hp * 128:(hp + 1) * 128, :].rearrange("(h v) k -> v h k", h=2), writes=[sw])
                b = bank()
                P.tr(b[:, 0:64], sw[:64, :], C.ident[:64, :64], [sw, C.ident], [b])
                P.cp("act", S2[:], b[:, 0:64], [b], [S2])
                P.cp("dve", S2b[:], S2[:], [S2], [S2b])
            yield from serial(c)
        save_state(1)

    def interleave(gens):
        gens = list(gens)
        while gens:
            for g_ in list(gens):
                try:
                    next(g_)
                except StopIteration:
                    gens.remove(g_)

    for hp0 in range(0, NHP, 2):
        interleave([stream(hp0 + d, d) for d in range(2) if hp0 + d < NHP])
    return True


def phase_out(C, x_src, li):
    P = C.P
    lst = C.lst
    ogT = P.sb([128, 16, NTOK], BF16, stack=lst, name="ogT%d" % li)
    ob = [P.sb([128, D], BF16, stack=lst) for _ in range(2)]
    for t in range(NT):
        n = tn(t)
        o = ob[t % 2]
        P.dma(o[:n], C.og[t * 128:t * 128 + n, :], writes=[o])
        for gi in range(2):
            for jj in range(8):
                c = gi * 8 + jj
                P.tr(C.ptb[:, jj * 128:jj * 128 + n], o[:n, c * 128:(c + 1) * 128], C.identb[:n, :n],
                     [o, C.identb], [C.ptb])
            src = C.ptb[:].rearrange("p (a b) -> p a b", a=8)[:, :, :n]
            P.cp("act" if gi == 0 else "dve", ogT[:, gi * 8:(gi + 1) * 8, t * 128:t * 128 + n], src, [C.ptb], [ogT])
    wf_bufs = [P.sb([128, 4, 512], F32, stack=lst) for _ in range(2)]
    wbs = [P.sb([128, 16, 512], BF16, stack=lst) for _ in range(2)]
    xb = [P.sb([128, 512], F32, stack=lst) for _ in range(3)]
    ctr = [0]
    k = 0
    for blk in range(4):
        wb = wbs[blk % 2]
        load_w_block(C, wf_bufs, wb, [(C.w_out[li][:, blk * 512:(blk + 1) * 512], 0)], 512, ctr)
        for t in range(NT):
            n = tn(t)
            ps = C.pp[t % 2]
            xt = xb[k % 3]
            k += 1
            P.dma(xt[:n], x_src[t * 128:t * 128 + n, blk * 512:(blk + 1) * 512], writes=[xt])
            for c in range(16):
                P.mm(ps[:n, :], ogT[:, c, t * 128:t * 128 + n], wb[:, c, :], c == 0, c == 15, [ogT, wb], [ps])
            P.tt("dve", xt[:n], xt[:n], ps[:n, :], ALU.add, [xt, ps], [xt])
            P.dma(C.xs[t * 128:t * 128 + n, blk * 512:(blk + 1) * 512], xt[:n], reads=[xt])


def phase_final(C, x_src):
    P = C.P
    lst = C.lst
    P.dma(C.gb[:], C.final_g[0:1, :].partition_broadcast(128), writes=[C.gb])
    xb = [P.sb([128, D], F32, stack=lst) for _ in range(2)]
    junk = P.sb([128, D], BF16, stack=lst)
    ss = [P.sb([128, 1], F32, stack=lst) for _ in range(2)]
    rs = [P.sb([128, 1], F32, stack=lst) for _ in range(2)]
    for t in range(NT):
        n = tn(t)
        xt, s, r = xb[t % 2], ss[t % 2], rs[t % 2]
        P.dma(xt[:n], x_src[t * 128:t * 128 + n, :], writes=[xt])
        P.act(junk[:n], xt[:n], AF.Square, [xt], [junk, s], accum_out=s[:n])
        P.act(r[:n], s[:n], AF.Sqrt, [s, C.eps6], [r], bias=C.eps6[:n], scale=1.0 / D)
        P.op("dve", lambda E, r=r, n=n: E.reciprocal(r[:n], r[:n]), [r], [r])
        P.stt("dve", xt[:n], xt[:n], r[:n, 0:1], C.gb[:n], ALU.mult, ALU.mult, [xt, r, C.gb], [xt])
        P.dma(C.y[t * 128:t * 128 + n, :], xt[:n], reads=[xt])


def const_masks():
    i = np.arange(128)
    same = (i[:, None] // 64) == (i[None, :] // 64)
    s_, t_ = i[:, None] % 64, i[None, :] % 64
    sel2 = (s_ == t_)
    m = np.stack([sel2, same, same & (s_ < t_), same & (s_ > t_), same & (s_ <= t_)]).astype(np.float32)
    return np.ascontiguousarray(m)


def const_selc():
    i = np.arange(128)
    return np.ascontiguousarray(((i[:, None] // 64) == np.arange(2)[None, :]).astype(np.float32))


def rope_table(dh):
    half = dh // 2
    pos = np.concatenate([np.arange(2048), PAST + np.arange(64)]).astype(np.float32)
    inv = np.power(np.float32(10000.0), -np.arange(half, dtype=np.float32) * np.float32(2.0 / dh)).astype(np.float32)
    ang = pos[:, None] * inv[None, :]
    cos = np.cos(ang).astype(np.float32)
    sin = np.sin(ang).astype(np.float32)
    tab = np.stack([np.concatenate([cos, cos], 1), np.concatenate([-sin, sin], 1)], 1)
    return np.ascontiguousarray(tab.astype(np.float32))


_NC_CACHE = {}


def kernel(x_prompt, x_sample, cache_a_k, cache_a_v, cache_b_k, cache_b_v, cache_b_kidx, state_c_wkv, state_c_shift,
           norm_g, final_g, w_out, a_w_in, a_lam, a_subln_g, b_w_in, c_mu, c_w_rkvg, c_w0, c_w_la, c_w_lb, c_a0,
           c_a_la, c_a_lb, c_k_k, c_k_a, c_r_k, c_ln_w, c_ln_b):
    stage = int(os.environ.get("MK_STAGE", "4"))
    skey = (stage, os.environ.get("MK_HEADS"), os.environ.get("MK_NOATT"), os.environ.get("MK_NOSAMP"), os.environ.get("MK_B"), os.environ.get("MK_BK"), os.environ.get("MK_NHP"))
    ncores = int(os.environ.get("MK_CORES", "8"))
    f = lambda a: np.ascontiguousarray(np.asarray(a, dtype=np.float32))
    if skey not in _NC_CACHE:
        _NC_CACHE[skey] = build_nc(stage)
    nc = _NC_CACHE[skey]
    shared = {
        "norm_g": f(norm_g), "final_g": f(final_g).reshape(1, D), "w_out": f(w_out), "a_w_in": f(a_w_in),
        "a_lam": f(a_lam).reshape(2, 256), "a_subln_g": f(a_subln_g), "b_w_in": f(b_w_in)[0],
        "c_mu": f(c_mu)[0], "c_w_rkvg": f(c_w_rkvg)[0], "c_w0": f(c_w0), "c_w_la": f(c_w_la)[0],
        "c_w_lb": f(c_w_lb)[0], "c_a0": f(c_a0), "c_a_la": f(c_a_la)[0], "c_a_lb": f(c_a_lb)[0],
        "c_k_k": f(c_k_k), "c_k_a": f(c_k_a), "c_r_k": f(c_r_k).reshape(1, D), "c_ln_w": f(c_ln_w),
        "c_ln_b": f(c_ln_b), "rope64": rope_table(64), "rope128": rope_table(128),
        "ident": np.eye(128, dtype=np.float32), "cmask": const_masks(), "selc": const_selc(),
    }
    in_maps = []
    for c in range(ncores):
        m = dict(shared)
        m["xin"] = np.concatenate([f(x_prompt[c // 2]), f(x_sample[c])], 0)
        m["cak"] = f(cache_a_k[:, c]).reshape(2, PAST, D)
        m["cav"] = f(cache_a_v[:, c]).reshape(2, PAST, D)
        m["cbk"] = f(cache_b_k[0, c]).reshape(PAST, 512)
        m["cbv"] = f(cache_b_v[0, c]).reshape(PAST, 512)
        m["cbi"] = f(cache_b_kidx[0, c])
        m["swkv"] = f(state_c_wkv[0, c]).reshape(2048, 64)
        m["sshift"] = f(state_c_shift[0, c]).reshape(1, D)
        in_maps.append(m)
    tr = bool(int(os.environ.get('MK_TRACE', '0')))
    res = run_bass_kernel_spmd(nc, in_maps, core_ids=list(range(ncores)), **({'trace': True} if tr else {}))
    if tr:
        print('EXEC_NS', res.exec_time_ns)
        try:
            import ast, collections
            insts = res.instructions_and_trace[0]
            src = open(__file__).read()
            funcs = [(n.lineno, n.end_lineno, n.name) for n in ast.parse(src).body if isinstance(n, ast.FunctionDef)]
            def fn_of(line):
                for a, b, nm in funcs:
                    if a <= line <= b:
                        return nm
                return '?'
            t0 = min(i.timestamp for i in insts)
            t1 = max(i.end_timestamp for i in insts)
            NBK = 48
            w = (t1 - t0) / NBK
            busy = collections.defaultdict(lambda: [0.0] * NBK)
            fnb = [collections.Counter() for _ in range(NBK)]
            for i in insts:
                if i.is_seq_only:
                    continue
                b = min(NBK - 1, int((i.timestamp - t0) / w))
                busy[str(i.engine)][b] += i.duration
                fnb[b][fn_of(i.source_line)] += i.duration
            print('TOTAL ms', (t1 - t0) / 1e6, 'bucket us', w / 1e3)
            for e, v in busy.items():
                print('%-10s tot %5.1f%% | ' % (e[:10], 100 * sum(v) / (t1 - t0)) + ' '.join('%2d' % min(99, int(100 * x / w)) for x in v))
            print('phase: ' + ' '.join((c.most_common(1)[0][0][-2:] if c else '--') for c in fnb))
        except Exception as ex:
            print('trace summary failed', ex)
    R = res.results
    nb = 4
    y_p = np.zeros((4, 2048, D), np.float32)
    y_s = np.zeros((8, 64, D), np.float32)
    akp = np.zeros((2, 4, 2048, 16, 128), np.float32)
    avp = np.zeros_like(akp)
    aks = np.zeros((2, 8, 64, 16, 128), np.float32)
    avs = np.zeros_like(aks)
    bkp = np.zeros((1, 4, 2048, 4, 128), np.float32)
    bvp = np.zeros_like(bkp)
    bip = np.zeros((1, 4, 2048, 64), np.float32)
    bks = np.zeros((1, 8, 64, 4, 128), np.float32)
    bvs = np.zeros_like(bks)
    bis = np.zeros((1, 8, 64, 64), np.float32)
    cwp = np.zeros((1, 4, 32, 64, 64), np.float32)
    csp = np.zeros((1, 4, D), np.float32)
    cws = np.zeros((1, 8, 32, 64, 64), np.float32)
    css = np.zeros((1, 8, D), np.float32)
    for c in range(ncores):
        r = R[c]
        p = c // 2
        if c % 2 == 0:
            y_p[p] = r["y"][:2048]
            akp[:, p] = r["o_ak"][:, :2048].reshape(2, 2048, 16, 128)
            avp[:, p] = r["o_av"][:, :2048].reshape(2, 2048, 16, 128)
            bkp[0, p] = r["o_bk"][:2048].reshape(2048, 4, 128)
            bvp[0, p] = r["o_bv"][:2048].reshape(2048, 4, 128)
            bip[0, p] = r["o_bi"][:2048]
            cwp[0, p] = r["o_cw"][0].reshape(32, 64, 64)
            csp[0, p] = r["o_cs"][0]
        y_s[c] = r["y"][2048:]
        aks[:, c] = r["o_ak"][:, 2048:].reshape(2, 64, 16, 128)
        avs[:, c] = r["o_av"][:, 2048:].reshape(2, 64, 16, 128)
        bks[0, c] = r["o_bk"][2048:].reshape(64, 4, 128)
        bvs[0, c] = r["o_bv"][2048:].reshape(64, 4, 128)
        bis[0, c] = r["o_bi"][2048:]
        cws[0, c] = r["o_cw"][1].reshape(32, 64, 64)
        css[0, c] = r["o_cs"][1]
    return (y_p, y_s, akp, avp, aks, avs, bkp, bvp, bip, bks, bvs, bis, cwp, csp, cws, css)
```

```python
import os
import math
import numpy as np
from contextlib import ExitStack
import concourse.bass as bass
import concourse.mybir as mybir
from concourse.bass_utils import run_bass_kernel_spmd

F32 = mybir.dt.float32
BF16 = mybir.dt.bfloat16
AF = mybir.ActivationFunctionType
ALU = mybir.AluOpType
AX = mybir.AxisListType

D = 2048
NT = 17
NTOK = 2112
PAST = 4096
DEPTH = 4


def tn(t):
    return 128 if t < 16 else 64


class Buf:
    __slots__ = ("name", "lw", "rd")

    def __init__(self, name=""):
        self.name = name
        self.lw = None
        self.rd = []


class T:
    def __init__(self, t, nb=1, name=""):
        self.t = t
        self.bs = [Buf(name + str(i)) for i in range(nb)]

    def __getitem__(self, k):
        return self.t[k]


class Prog:
    ENGS = ("pe", "act", "dve", "pool", "sp")
    NDMA = {"sp": 12, "act": 4, "pool": 8}

    def __init__(self, nc, stack):
        self.nc = nc
        self.stack = stack
        self.q = {e: [] for e in self.ENGS}
        self.sem = {e: stack.enter_context(nc.semaphore("s_" + e)) for e in self.ENGS}
        self.cnt = {e: 0 for e in self.ENGS}
        self.seen = {e: {} for e in self.ENGS}
        self.pend = {e: {} for e in self.ENGS}
        self.dsem = {}
        self.dcnt = {}
        self.di = {}
        for e, n in self.NDMA.items():
            self.dsem[e] = [stack.enter_context(nc.semaphore("d_%s%d" % (e, i))) for i in range(n)]
            self.dcnt[e] = [0] * n
            self.di[e] = 0
        self.n_ins = 0
        self.uid = 0

    def sb(self, shape, dt, nb=1, name=None, stack=None):
        self.uid += 1
        name = name or "t%d" % self.uid
        t = (stack or self.stack).enter_context(self.nc.sbuf_tensor(name, list(shape), dt))
        return T(t, nb, name)

    def ps(self, shape, dt=F32, nb=1, name=None, stack=None):
        self.uid += 1
        name = name or "p%d" % self.uid
        t = (stack or self.stack).enter_context(self.nc.psum_tensor(name, list(shape), dt))
        return T(t, nb, name)

    def _bufs(self, xs):
        out = []
        for x in xs:
            if isinstance(x, T):
                out.extend(x.bs)
            elif isinstance(x, Buf):
                out.append(x)
            elif x is None:
                pass
            else:
                out.extend(self._bufs(x))
        return out

    def fence(self):
        snap = {}
        for e in self.ENGS:
            if self.cnt[e] > 0:
                snap[("c", e)] = self.cnt[e]
        for e in self.NDMA:
            for i, c in enumerate(self.dcnt[e]):
                if c > 0:
                    snap[("d", e, i)] = c
        for e in self.ENGS:
            for k, v in snap.items():
                if e == "pe" and k == ("c", "pe"):
                    continue
                if self.pend[e].get(k, 0) < v:
                    self.pend[e][k] = v

    def op(self, eng, fn, reads=(), writes=(), dma=False):
        reads = self._bufs(reads)
        writes = self._bufs(writes)
        deps = dict(self.pend[eng])
        self.pend[eng] = {}

        def add(ev):
            if ev is None:
                return
            k, v = ev
            if eng == "pe" and k == ("c", "pe"):
                return
            if deps.get(k, 0) < v:
                deps[k] = v

        for b in reads:
            add(b.lw)
        for b in writes:
            add(b.lw)
            for r in b.rd:
                add(r)
        if dma:
            i = self.di[eng] % len(self.dsem[eng])
            self.di[eng] += 1
            if self.dcnt[eng][i] > 0:
                add((("d", eng, i), self.dcnt[eng][i]))
            self.dcnt[eng][i] += 16
            ev = (("d", eng, i), self.dcnt[eng][i])
        else:
            self.cnt[eng] += 1
            ev = (("c", eng), self.cnt[eng])
        waits = []
        seen = self.seen[eng]
        for k, v in deps.items():
            if seen.get(k, 0) < v:
                seen[k] = v
                waits.append((k, v))
        self.q[eng].append((waits, fn, ev))
        for b in reads:
            b.rd.append(ev)
            if len(b.rd) > 64:
                m = {}
                for k, v in b.rd:
                    if m.get(k, 0) < v:
                        m[k] = v
                b.rd = list(m.items())
        for b in writes:
            b.lw = ev
            b.rd = []
        self.n_ins += 1
        return ev

    def _semof(self, k):
        if k[0] == "c":
            return self.sem[k[1]]
        return self.dsem[k[1]][k[2]]

    def emit(self):
        nc = self.nc
        fin = []
        for e in self.NDMA:
            for i, c in enumerate(self.dcnt[e]):
                if c > 0:
                    fin.append((("d", e, i), c))
        for e in self.ENGS:
            if e != "sp" and self.cnt[e] > 0:
                fin.append((("c", e), self.cnt[e]))
        engobj = {"pe": "tensor", "act": "scalar", "dve": "vector", "pool": "gpsimd", "sp": "sync"}
        with nc.Block() as block:
            for e in self.ENGS:
                def body(engine, e=e):
                    for waits, fn, ev in self.q[e]:
                        for k, v in waits:
                            engine.wait_ge(self._semof(k), v)
                        ins = fn(engine)
                        ins.then_inc(self._semof(ev[0]), 16 if ev[0][0] == "d" else 1)
                    if e == "sp":
                        for k, v in fin:
                            engine.wait_ge(self._semof(k), v)
                getattr(block, engobj[e])(body)

    def dma(self, out, in_, reads=(), writes=(), eng="sp", **kw):
        return self.op(eng, lambda E: E.dma_start(out=out, in_=in_, **kw), reads, writes, dma=True)

    def mm(self, out, lhsT, rhs, start, stop, reads=(), writes=()):
        return self.op("pe", lambda E: E.matmul(out, lhsT, rhs, start=start, stop=stop), reads, writes)

    def tr(self, out, in_, ident, reads=(), writes=()):
        return self.op("pe", lambda E: E.transpose(out, in_, ident), reads, writes)

    def act(self, out, in_, func, reads=(), writes=(), **kw):
        return self.op("act", lambda E: E.activation(out=out, in_=in_, func=func, **kw), reads, writes)

    def tt(self, eng, out, in0, in1, op, reads=(), writes=()):
        return self.op(eng, lambda E: E.tensor_tensor(out, in0, in1, op), reads, writes)

    def ts(self, eng, out, in0, s1, s2, op0, op1=None, reads=(), writes=(), **kw):
        if op1 is None:
            return self.op(eng, lambda E: E.tensor_scalar(out, in0, s1, s2, op0, **kw), reads, writes)
        return self.op(eng, lambda E: E.tensor_scalar(out, in0, s1, s2, op0, op1, **kw), reads, writes)

    def stt(self, eng, out, in0, scalar, in1, op0, op1, reads=(), writes=()):
        return self.op(eng, lambda E: E.scalar_tensor_tensor(out, in0, scalar, in1, op0, op1), reads, writes)

    def cp(self, eng, out, in_, reads=(), writes=()):
        if eng == "act":
            return self.op(eng, lambda E: E.copy(out, in_), reads, writes)
        return self.op(eng, lambda E: E.tensor_copy(out, in_), reads, writes)


class Ctx:
    pass


def build_nc(stage):
    nc = bass.Bass("TRN2", target_bir_lowering=False)
    C = Ctx()
    C.nc = nc
    dt_in = lambda n, s, d=F32: nc.dram_tensor(n, list(s), d, kind="ExternalInput").ap()
    dt_out = lambda n, s, d=F32: nc.dram_tensor(n, list(s), d, kind="ExternalOutput").ap()
    dt_tmp = lambda n, s, d=F32: nc.dram_tensor(n, list(s), d, kind="Internal").ap()
    C.xin = dt_in("xin", [NTOK, D])
    C.cak = dt_in("cak", [2, PAST, 16 * 128])
    C.cav = dt_in("cav", [2, PAST, 16 * 128])
    C.cbk = dt_in("cbk", [PAST, 4 * 128])
    C.cbv = dt_in("cbv", [PAST, 4 * 128])
    C.cbi = dt_in("cbi", [PAST, 64])
    C.swkv = dt_in("swkv", [32 * 64, 64])
    C.sshift = dt_in("sshift", [1, D])
    C.norm_g = dt_in("norm_g", [DEPTH, D])
    C.final_g = dt_in("final_g", [1, D])
    C.w_out = dt_in("w_out", [DEPTH, D, D])
    C.a_w_in = dt_in("a_w_in", [2, D, 4 * D])
    C.a_lam = dt_in("a_lam", [2, 4 * 64])
    C.a_subln_g = dt_in("a_subln_g", [2, 128])
    C.b_w_in = dt_in("b_w_in", [D, 6224])
    C.c_mu = dt_in("c_mu", [6, D])
    C.c_w_rkvg = dt_in("c_w_rkvg", [4, D, D])
    C.c_w0 = dt_in("c_w0", [1, D])
    C.c_w_la = dt_in("c_w_la", [D, 96])
    C.c_w_lb = dt_in("c_w_lb", [96, D])
    C.c_a0 = dt_in("c_a0", [1, D])
    C.c_a_la = dt_in("c_a_la", [D, 96])
    C.c_a_lb = dt_in("c_a_lb", [96, D])
    C.c_k_k = dt_in("c_k_k", [1, D])
    C.c_k_a = dt_in("c_k_a", [1, D])
    C.c_r_k = dt_in("c_r_k", [1, D])
    C.c_ln_w = dt_in("c_ln_w", [1, D])
    C.c_ln_b = dt_in("c_ln_b", [1, D])
    C.rope64 = dt_in("rope64", [NTOK, 2, 64])
    C.rope128 = dt_in("rope128", [NTOK, 2, 128])
    C.ident_d = dt_in("ident", [128, 128])
    C.y = dt_out("y", [NTOK, D])
    C.o_ak = dt_out("o_ak", [2, NTOK, D])
    C.o_av = dt_out("o_av", [2, NTOK, D])
    C.o_bk = dt_out("o_bk", [NTOK, 512])
    C.o_bv = dt_out("o_bv", [NTOK, 512])
    C.o_bi = dt_out("o_bi", [NTOK, 64])
    C.o_cw = dt_out("o_cw", [2, 32 * 64, 64])
    C.o_cs = dt_out("o_cs", [2, D])
    C.xs = dt_tmp("xs", [NTOK, D])
    C.og = dt_tmp("og", [NTOK, D], BF16)
    C.qid = dt_tmp("qid", [NTOK, 1024], BF16)
    C.qd = dt_tmp("qd", [NTOK, D], BF16)
    C.sgd = dt_tmp("sgd", [NTOK, D], BF16)
    C.maskd = dt_tmp("maskd", [NTOK, 4160], BF16)
    for nm in ("c_r", "c_k", "c_v", "c_sg", "c_bv"):
        setattr(C, nm, dt_tmp(nm, [NTOK, D]))
    for nm in ("c_At", "c_Bt", "c_Kt", "c_Rt"):
        setattr(C, nm, dt_tmp(nm, [NTOK, D], BF16))
    C.cmask = dt_in("cmask", [5, 128, 128])
    C.selc = dt_in("selc", [128, 2])

    with ExitStack() as st:
        P = Prog(nc, st)
        C.P = P
        C.ident = P.sb([128, 128], F32, name="identf")
        C.identb = P.sb([128, 128], BF16, name="identb")
        P.dma(C.ident[:], C.ident_d[:, :], writes=[C.ident])
        P.cp("dve", C.identb[:], C.ident[:], [C.ident], [C.identb])
        C.gb = P.sb([128, D], F32, name="gb")
        C.eps6 = P.sb([128, 1], F32, name="eps6")
        P.op("dve", lambda E: E.memset(C.eps6[:], 1e-6), [], [C.eps6])
        C.eps5 = P.sb([128, 1], F32, name="eps5")
        P.op("dve", lambda E: E.memset(C.eps5[:], 1e-5), [], [C.eps5])
        C.pp = [P.ps([128, 512], F32, name="pp%d" % i) for i in range(2)]
        C.ptb = P.ps([128, 1024], BF16, name="ptb")
        C.psc = [P.ps([128, 512], F32, name="psc%d" % i) for i in range(2)]
        C.po = [P.ps([128, 512], F32, name="po%d" % i) for i in range(2)]
        C.ptf = P.ps([128, 512], F32, name="ptf")

        x_src = C.xin
        for li in range(DEPTH):
            if li >= stage:
                break
            kind, j = li % 3, li // 3
            with ExitStack() as cst:
                if kind == 2:
                    C.tT = P.sb([128, NTOK], BF16, stack=cst, name='c_tT')
                    C.aT = P.sb([128, NTOK], BF16, stack=cst, name='c_aT')
                    C.gC = P.sb([128, 16, 34], F32, stack=cst, name='c_gC')
                done = False
                with ExitStack() as hs:
                    C.hT = P.sb([128, 16, NTOK], BF16, stack=hs, name="hT%d" % li)
                    with ExitStack() as lst:
                        C.lst = lst
                        phase_norm(C, x_src, C.norm_g[li:li + 1, :], li, want_last=(kind == 2))
                        P.fence()
                    with ExitStack() as lst:
                        C.lst = lst
                        if kind == 0:
                            layer_a(C, li, j)
                            done = True
                        elif kind == 1:
                            done = layer_b(C, li)
                        else:
                            done = layer_c1(C, li)
                        P.fence()
                if kind == 2 and done:
                    with ExitStack() as lst:
                        C.lst = lst
                        layer_c2(C, li)
                        P.fence()
                    with ExitStack() as lst:
                        C.lst = lst
                        layer_c3(C, li)
                        P.fence()
            if not done:
                continue
            with ExitStack() as lst:
                C.lst = lst
                phase_out(C, x_src, li)
                P.fence()
            x_src = C.xs
        with ExitStack() as lst:
            C.lst = lst
            phase_final(C, x_src)
        P.emit()
    return nc


def phase_norm(C, x_src, g_row, li, want_last=False):
    P = C.P
    lst = C.lst
    P.dma(C.gb[:], g_row.partition_broadcast(128), writes=[C.gb])
    xb = [P.sb([128, D], F32, stack=lst) for _ in range(2)]
    junk = P.sb([128, D], BF16, stack=lst)
    hb = [P.sb([128, D], BF16, stack=lst) for _ in range(2)]
    ss = [P.sb([128, 1], F32, stack=lst) for _ in range(2)]
    rs = [P.sb([128, 1], F32, stack=lst) for _ in range(2)]
    for t in range(NT):
        n = tn(t)
        xt, h, s, r = xb[t % 2], hb[t % 2], ss[t % 2], rs[t % 2]
        P.dma(xt[:n], x_src[t * 128:t * 128 + n, :], writes=[xt])
        P.act(junk[:n], xt[:n], AF.Square, [xt], [junk, s], accum_out=s[:n])
        P.act(r[:n], s[:n], AF.Sqrt, [s, C.eps6], [r], bias=C.eps6[:n], scale=1.0 / D)
        P.op("dve", lambda E, r=r, n=n: E.reciprocal(r[:n], r[:n]), [r], [r])
        P.stt("dve", h[:n], xt[:n], r[:n, 0:1], C.gb[:n], ALU.mult, ALU.mult, [xt, r, C.gb], [h])
        if want_last and t >= 15:
            hl = P.sb([128, D], F32, stack=lst)
            P.stt("dve", hl[:n], xt[:n], r[:n, 0:1], C.gb[:n], ALU.mult, ALU.mult, [xt, r, C.gb], [hl])
            P.dma(C.o_cs[t - 15:t - 14, :], hl[n - 1:n, :], reads=[hl])
        for gi in range(2):
            for jj in range(8):
                c = gi * 8 + jj
                P.tr(C.ptb[:, jj * 128:jj * 128 + n], h[:n, c * 128:(c + 1) * 128], C.identb[:n, :n],
                     [h, C.identb], [C.ptb])
            src = C.ptb[:].rearrange("p (a b) -> p a b", a=8)[:, :, :n]
            P.cp("act" if gi == 0 else "dve", C.hT[:, gi * 8:(gi + 1) * 8, t * 128:t * 128 + n], src, [C.ptb], [C.hT])


def load_w_block(C, wf_bufs, wb, pieces, ncols, ctr):
    P = C.P
    for qtr in range(4):
        wf = wf_bufs[(ctr[0]) % 2]
        for ap, off in pieces:
            w = ap.shape[1]
            P.dma(wf[:, :, off:off + w], ap[qtr * 512:(qtr + 1) * 512, :].rearrange("(c p) n -> p c n", p=128),
                  writes=[wf])
        eng = ("dve", "act", "dve")[ctr[0] % 3]
        P.cp(eng, wb[:, qtr * 4:(qtr + 1) * 4, :ncols], wf[:, :, :ncols], [wf], [wb])
        ctr[0] += 1


def layer_a(C, li, j):
    P = C.P
    lst = C.lst
    lam_init = 0.8 - 0.6 * math.exp(-0.3 * li)
    rope = P.sb([128, NT, 2, 64], F32, stack=lst, name="rope_a%d" % li)
    P.dma(rope[:, 0:16], C.rope64[0:2048].rearrange("(t p) a b -> p t a b", p=128), writes=[rope])
    P.dma(rope[:64, 16], C.rope64[2048:2112], writes=[rope])
    lamt = P.sb([128, 4, 64], F32, stack=lst)
    P.dma(lamt[:].rearrange("p a b -> p (a b)"), C.a_lam[j:j + 1, :].partition_broadcast(128), writes=[lamt])
    lprod = P.sb([128, 2, 64], F32, stack=lst)
    lsum = P.sb([128, 2], F32, stack=lst)
    neglam = P.sb([128, 1], F32, stack=lst)
    lv = lamt[:].rearrange("p (a b) c -> p a b c", b=2)
    P.tt("dve", lprod[:], lv[:, :, 0, :], lv[:, :, 1, :], ALU.mult, [lamt], [lprod])
    P.op("dve", lambda E: E.reduce_sum(lsum[:], lprod[:], AX.X), [lprod], [lsum])
    P.act(lsum[:], lsum[:], AF.Exp, [lsum], [lsum])
    P.tt("dve", neglam[:], lsum[:, 1:2], lsum[:, 0:1], ALU.subtract, [lsum], [neglam])
    P.ts("dve", neglam[:], neglam[:], -lam_init, None, ALU.add, reads=[neglam], writes=[neglam])
    gsub = P.sb([128, 128], F32, stack=lst)
    P.dma(gsub[:], C.a_subln_g[j:j + 1, :].partition_broadcast(128), writes=[gsub])
    P.ts("dve", gsub[:], gsub[:], 1.0 - lam_init, None, ALU.mult, reads=[gsub], writes=[gsub])

    wf_bufs = [P.sb([128, 4, 512], F32, stack=lst) for _ in range(2)]
    wbs = [P.sb([128, 16, 512], BF16, stack=lst) for _ in range(1)]
    qkT = [P.sb([128, 2, NTOK], BF16, stack=lst) for _ in range(2)]
    vaug = [P.sb([128, NT, 132], BF16, stack=lst) for _ in range(2)]
    sg = [P.sb([128, NT, 128], BF16, stack=lst) for _ in range(2)]
    for v in vaug:
        P.op("pool", lambda E, v=v: E.memset(v[:, :, 128:129], 1.0), [], [v])
    ra = [P.sb([128, 256], F32, stack=lst) for _ in range(2)]
    rt = [P.sb([128, 256], F32, stack=lst) for _ in range(2)]
    kv = [P.sb([128, 2, 128], F32, stack=lst) for _ in range(2)]
    qkb = [P.sb([128, 256], BF16, stack=lst) for _ in range(2)]
    PT = [P.sb([128, 2048], BF16, stack=lst) for _ in range(4)]
    ot = [P.sb([128, 128], F32, stack=lst) for _ in range(2)]
    ogt = [P.sb([128, 128], BF16, stack=lst) for _ in range(2)]
    rr = [P.sb([128, 2], F32, stack=lst) for _ in range(2)]
    s2 = [P.sb([128, 1], F32, stack=lst) for _ in range(2)]
    junk = P.sb([128, 128], F32, stack=lst)
    kcs = [P.sb([128, 8, 128], F32, stack=lst) for _ in range(1)]
    vcs = [P.sb([128, 8, 128], F32, stack=lst) for _ in range(1)]
    kTc = P.sb([128, PAST], BF16, stack=lst)
    vca = P.sb([128, 32, 132], BF16, stack=lst)
    P.op("pool", lambda E: E.memset(vca[:, :, 128:129], 1.0), [], [vca])
    PTs = [P.sb([128, 512], BF16, stack=lst) for _ in range(2)]

    ctr = [0]
    cnt = [0]
    W = C.a_w_in[j]
    scale = 64 ** -0.5

    def epilogue(h, t, n, po, k):
        o, og_, r_, s_ = ot[k % 2], ogt[k % 2], rr[k % 2], s2[k % 2]
        pov = po[:].rearrange("p (a b) -> p a b", a=2)
        P.op("dve", lambda E: E.reciprocal(r_[:n], pov[:n, :, 128]), [po], [r_])
        P.tt("dve", r_[:n, 1:2], r_[:n, 1:2], neglam[:n], ALU.mult, [r_, neglam], [r_])
        P.ts("dve", o[:n], po[:n, 0:128], r_[:n, 0:1], None, ALU.mult, reads=[po, r_], writes=[o])
        P.stt("dve", o[:n], po[:n, 256:384], r_[:n, 1:2], o[:n], ALU.mult, ALU.add, [po, r_, o], [o])
        P.act(junk[:n], o[:n], AF.Square, [o], [junk, s_], accum_out=s_[:n])
        P.act(s_[:n], s_[:n], AF.Sqrt, [s_, C.eps5], [s_], bias=C.eps5[:n], scale=1.0 / 128)
        P.op("dve", lambda E: E.reciprocal(s_[:n], s_[:n]), [s_], [s_])
        P.stt("dve", o[:n], o[:n], s_[:n, 0:1], gsub[:n], ALU.mult, ALU.mult, [o, s_, gsub], [o])
        P.tt("dve", og_[:n], o[:n], sg[h % 2][:n, t, :], ALU.mult, [o, sg[h % 2]], [og_])
        P.dma(C.og[t * 128:t * 128 + n, h * 128:(h + 1) * 128], og_[:n], reads=[og_])

    NH = int(os.environ.get('MK_HEADS', '16'))
    NOATT = int(os.environ.get('MK_NOATT', '0'))
    NOSAMP = int(os.environ.get('MK_NOSAMP', '0'))
    def gen_proj(h):
        wb = wbs[0]
        pieces = [(W[:, blk * D + h * 128: blk * D + (h + 1) * 128], blk * 128) for blk in range(4)]
        load_w_block(C, wf_bufs, wb, pieces, 512, ctr)
        QK, V, SG = qkT[h % 2], vaug[h % 2], sg[h % 2]
        for t in range(NT):
            n = tn(t)
            ps = C.pp[t % 2]
            for c in range(16):
                P.mm(ps[:n, :], C.hT[:, c, t * 128:t * 128 + n], wb[:, c, :], c == 0, c == 15, [C.hT, wb], [ps])
            k = cntp[0]
            cntp[0] += 1
            a_, t_, kv_, qb = ra[k % 2], rt[k % 2], kv[k % 2], qkb[k % 2]
            xv = ps[:n, 0:256].rearrange("p (a b c) -> p a b c", a=4, b=2)
            cosb = rope[:n, t, 0, :].rearrange("p (b c) -> p b c", b=2).unsqueeze(1).to_broadcast([n, 4, 2, 32])
            sinb = rope[:n, t, 1, :].rearrange("p (b c) -> p b c", b=2).unsqueeze(1).to_broadcast([n, 4, 2, 32])
            av = a_[:n].rearrange("p (a b c) -> p a b c", a=4, b=2)
            tv = t_[:n].rearrange("p (a b c) -> p a b c", a=4, b=2)
            P.tt("dve", av, xv, cosb, ALU.mult, [ps, rope], [a_])
            P.tt("dve", tv[:, :, 0, :], xv[:, :, 1, :], sinb[:, :, 0, :], ALU.mult, [ps, rope], [t_])
            P.tt("dve", tv[:, :, 1, :], xv[:, :, 0, :], sinb[:, :, 1, :], ALU.mult, [ps, rope], [t_])
            P.tt("dve", qb[:n, 0:128], a_[:n, 0:128], t_[:n, 0:128], ALU.add, [a_, t_], [qb])
            P.tt("dve", kv_[:n, 0, :], a_[:n, 128:256], t_[:n, 128:256], ALU.add, [a_, t_], [kv_])
            P.cp("dve", qb[:n, 128:256], kv_[:n, 0, :], [kv_], [qb])
            P.cp("act", kv_[:n, 1, :], ps[:n, 256:384], [ps], [kv_])
            P.cp("act", V[:n, t, 0:128], ps[:n, 256:384], [ps], [V])
            P.act(SG[:n, t, :], ps[:n, 384:512], AF.Silu, [ps], [SG])
            P.dma(C.o_ak[j, t * 128:t * 128 + n, h * 128:(h + 1) * 128], kv_[:n, 0, :], reads=[kv_])
            P.dma(C.o_av[j, t * 128:t * 128 + n, h * 128:(h + 1) * 128], kv_[:n, 1, :], reads=[kv_])
            for m in range(2):
                P.tr(C.ptb[:, m * 128:m * 128 + n], qb[:n, m * 128:(m + 1) * 128], C.identb[:n, :n],
                     [qb, C.identb], [C.ptb])
            P.cp("act", QK[:, :, t * 128:t * 128 + n],
                 C.ptb[:, 0:256].rearrange("p (a b) -> p a b", a=2)[:, :, :n], [C.ptb], [QK])
            yield

    def gen_att(h):
        QK, V, SG = qkT[h % 2], vaug[h % 2], sg[h % 2]
        for i in range(0 if not NOATT else 16, 16):
            k = cnt[0]
            cnt[0] += 1
            po = C.po[k % 2]
            for m in range(2):
                pt = PT[(k % 2) * 2 + m]
                nb = i + 1
                for g0 in range(0, nb, 4):
                    g1 = min(nb, g0 + 4)
                    sc = C.psc[(g0 // 4 + m) % 2]
                    for jb in range(g0, g1):
                        P.mm(sc[:, (jb - g0) * 128:(jb - g0 + 1) * 128],
                             QK[m * 64:(m + 1) * 64, 1, jb * 128:(jb + 1) * 128],
                             QK[m * 64:(m + 1) * 64, 0, i * 128:(i + 1) * 128], True, True, [QK], [sc])
                    P.act(pt[:, g0 * 128:g1 * 128], sc[:, 0:(g1 - g0) * 128], AF.Exp, [sc], [pt], scale=scale)
                P.op("pool", lambda E, pt=pt, i=i: E.memset(pt[64:128, i * 128:i * 128 + 64], 0.0), [], [pt])
                for jb in range(nb):
                    P.mm(po[:, m * 256:m * 256 + 129], pt[:, jb * 128:(jb + 1) * 128], V[:, jb, 0:129],
                         jb == 0, jb == nb - 1, [pt, V], [po])
                yield
            epilogue(h, i, 128, po, k)
            yield
        if NOSAMP:
            return
        for half in range(4):
            yield
            kc, vc = kcs[0], vcs[0]
            P.dma(kc[:], C.cak[j, half * 1024:(half + 1) * 1024, h * 128:(h + 1) * 128].rearrange(
                "(b p) d -> p b d", p=128), writes=[kc])
            P.dma(vc[:], C.cav[j, half * 1024:(half + 1) * 1024, h * 128:(h + 1) * 128].rearrange(
                "(b p) d -> p b d", p=128), writes=[vc])
            P.cp("pool", vca[:, half * 8:(half + 1) * 8, 0:128], vc[:], [vc], [vca])
            for g in range(2):
                for b in range(4):
                    P.tr(C.ptf[:, b * 128:(b + 1) * 128], kc[:, g * 4 + b, :], C.ident[:], [kc, C.ident], [C.ptf])
                P.cp("dve" if g % 2 else "act", kTc[:, half * 1024 + g * 512: half * 1024 + (g + 1) * 512], C.ptf[:],
                     [C.ptf], [kTc])
        k = cnt[0]
        cnt[0] += 1
        po = C.po[k % 2]
        for m in range(2):
            qs = QK[m * 64:(m + 1) * 64, 0, 2048:2112]
            for g in range(5):
                sc = C.psc[(g + m) % 2]
                pts = PTs[(g + m) % 2]
                nblk = 8 if g < 4 else 1
                for b in range(nblk):
                    jb = g * 8 + b
                    if jb < 32:
                        P.mm(sc[:, b * 64:(b + 1) * 64], kTc[m * 64:(m + 1) * 64, jb * 128:(jb + 1) * 128], qs,
                             True, True, [kTc, QK], [sc])
                    else:
                        P.mm(sc[:64, b * 64:(b + 1) * 64], QK[m * 64:(m + 1) * 64, 1, 2048:2112], qs,
                             True, True, [QK], [sc])
                rows = 128 if g < 4 else 64
                P.act(pts[:rows, 0:nblk * 64], sc[:rows, 0:nblk * 64], AF.Exp, [sc], [pts], scale=scale)
                for b in range(nblk):
                    jb = g * 8 + b
                    if jb < 32:
                        P.mm(po[:64, m * 256:m * 256 + 129], pts[:, b * 64:(b + 1) * 64], vca[:, jb, 0:129],
                             jb == 0, False, [pts, vca], [po])
                    else:
                        P.mm(po[:64, m * 256:m * 256 + 129], pts[:64, b * 64:(b + 1) * 64], V[:64, 16, 0:129],
                             False, True, [pts, V], [po])
        epilogue(h, 16, 64, po, k)


    def interleave(gens):
        gens = list(gens)
        while gens:
            for g_ in list(gens):
                try:
                    next(g_)
                except StopIteration:
                    gens.remove(g_)

    cntp = [0]
    interleave([gen_proj(0)])
    for h in range(NH):
        gs = [gen_att(h)]
        if h + 1 < NH:
            gs.append(gen_proj(h + 1))
        interleave(gs)


def rope_ops(C, n, xin, H, half, rope, t, a_, t_, rd):
    P = C.P
    Wd = H * 2 * half
    xv = xin.rearrange("p (a b c) -> p a b c", a=H, b=2)
    cosb = rope[:n, t, 0, :].rearrange("p (b c) -> p b c", b=2).unsqueeze(1).to_broadcast([n, H, 2, half])
    sinb = rope[:n, t, 1, :].rearrange("p (b c) -> p b c", b=2).unsqueeze(1).to_broadcast([n, H, 2, half])
    av = a_[:n, :Wd].rearrange("p (a b c) -> p a b c", a=H, b=2)
    tv = t_[:n, :Wd].rearrange("p (a b c) -> p a b c", a=H, b=2)
    P.tt("dve", av, xv, cosb, ALU.mult, rd + [rope], [a_])
    P.tt("dve", tv[:, :, 0, :], xv[:, :, 1, :], sinb[:, :, 0, :], ALU.mult, rd + [rope], [t_])
    P.tt("dve", tv[:, :, 1, :], xv[:, :, 0, :], sinb[:, :, 1, :], ALU.mult, rd + [rope], [t_])


def layer_b(C, li):
    P = C.P
    W = C.b_w_in
    OQ, OK_, OV, OQI, OKI, OG = 0, 2048, 2560, 3072, 4096, 4176
    NEG = -1.0e30
    with ExitStack() as bst:
        kT = P.sb([128, 4, NTOK], BF16, stack=bst, name="b_kT")
        vaug = P.sb([128, NT, 4, 132], BF16, stack=bst, name="b_vaug")
        kiT2 = P.sb([128, NTOK], BF16, stack=bst, name="b_kiT2")
        wis = P.sb([128, NT, 16], F32, stack=bst, name="b_wis")
        for g in range(4):
            P.op("pool", lambda E, g=g: E.memset(vaug[:, :, g, 128:129], 1.0), [], [vaug])
        with ExitStack() as lst:
            rope128 = P.sb([128, NT, 2, 128], F32, stack=lst)
            P.dma(rope128[:, 0:16], C.rope128[0:2048].rearrange("(t p) a b -> p t a b", p=128), writes=[rope128])
            P.dma(rope128[:64, 16], C.rope128[2048:2112], writes=[rope128])
            rope64 = P.sb([128, NT, 2, 64], F32, stack=lst)
            P.dma(rope64[:, 0:16], C.rope64[0:2048].rearrange("(t p) a b -> p t a b", p=128), writes=[rope64])
            P.dma(rope64[:64, 16], C.rope64[2048:2112], writes=[rope64])
            wf_bufs = [P.sb([128, 4, 512], F32, stack=lst) for _ in range(2)]
            wbs = [P.sb([128, 16, 512], BF16, stack=lst) for _ in range(2)]
            ra = [P.sb([128, 512], F32, stack=lst) for _ in range(2)]
            rt = [P.sb([128, 512], F32, stack=lst) for _ in range(2)]
            of = [P.sb([128, 512], F32, stack=lst) for _ in range(2)]
            ob = [P.sb([128, 512], BF16, stack=lst) for _ in range(2)]
            ctr = [0]
            blocks = [("k", OK_, 512), ("v", OV, 512), ("ki", OKI, 80), ("qi", OQI, 512), ("qi", OQI + 512, 512)]
            blocks += [("q", OQ + g * 512, 512) for g in range(4)] + [("g", OG + g * 512, 512) for g in range(4)]
            kk = 0
            blocks = blocks[:int(os.environ.get('MK_BK', '99'))]
            for bi, (kind, off, ncols) in enumerate(blocks):
                wb = wbs[bi % 2]
                load_w_block(C, wf_bufs, wb, [(W[:, off:off + ncols], 0)], ncols, ctr)
                for t in range(NT):
                    n = tn(t)
                    r0 = t * 128
                    ps = C.pp[t % 2]
                    for c in range(16):
                        P.mm(ps[:n, :ncols], C.hT[:, c, r0:r0 + n], wb[:, c, :ncols], c == 0, c == 15, [C.hT, wb], [ps])
                    a_, t_, f_, b_ = ra[kk % 2], rt[kk % 2], of[kk % 2], ob[kk % 2]
                    kk += 1
                    if kind == "k":
                        rope_ops(C, n, ps[:n, 0:512], 4, 64, rope128, t, a_, t_, [ps])
                        P.tt("dve", f_[:n], a_[:n], t_[:n], ALU.add, [a_, t_], [f_])
                        P.dma(C.o_bk[r0:r0 + n, :], f_[:n], reads=[f_])
                        P.cp("act", b_[:n], f_[:n], [f_], [b_])
                        for g in range(4):
                            P.tr(C.ptb[:, g * 128:g * 128 + n], b_[:n, g * 128:(g + 1) * 128], C.identb[:n, :n],
                                 [b_, C.identb], [C.ptb])
                        P.cp("act", kT[:, :, r0:r0 + n], C.ptb[:, 0:512].rearrange("p (a b) -> p a b", a=4)[:, :, :n],
                             [C.ptb], [kT])
                    elif kind == "v":
                        P.cp("act", f_[:n], ps[:n, :], [ps], [f_])
                        P.dma(C.o_bv[r0:r0 + n, :], f_[:n], reads=[f_])
                        for g in range(4):
                            P.cp("dve", vaug[:n, t, g, 0:128], f_[:n, g * 128:(g + 1) * 128], [f_], [vaug])
                    elif kind == "ki":
                        rope_ops(C, n, ps[:n, 0:64], 1, 32, rope64, t, a_, t_, [ps])
                        P.tt("dve", f_[:n, 0:64], a_[:n, 0:64], t_[:n, 0:64], ALU.add, [a_, t_], [f_])
                        P.dma(C.o_bi[r0:r0 + n, :], f_[:n, 0:64], reads=[f_])
                        P.cp("act", b_[:n, 0:64], f_[:n, 0:64], [f_], [b_])
                        P.cp("act", b_[:n, 64:128], f_[:n, 0:64], [f_], [b_])
                        P.tr(C.ptb[:, 0:n], b_[:n, 0:128], C.identb[:n, :n], [b_, C.identb], [C.ptb])
                        P.cp("act", kiT2[:, r0:r0 + n], C.ptb[:, 0:n], [C.ptb], [kiT2])
                        P.ts("dve", wis[:n, t, :], ps[:n, 64:80], 0.25, None, ALU.mult, reads=[ps], writes=[wis])
                    elif kind == "qi":
                        rope_ops(C, n, ps[:n, 0:512], 8, 32, rope64, t, a_, t_, [ps])
                        P.tt("dve", b_[:n], a_[:n], t_[:n], ALU.add, [a_, t_], [b_])
                        P.dma(C.qid[r0:r0 + n, off - OQI:off - OQI + 512], b_[:n], reads=[b_])
                    elif kind == "q":
                        rope_ops(C, n, ps[:n, 0:512], 4, 64, rope128, t, a_, t_, [ps])
                        P.tt("dve", b_[:n], a_[:n], t_[:n], ALU.add, [a_, t_], [b_])
                        P.dma(C.qd[r0:r0 + n, off - OQ:off - OQ + 512], b_[:n], reads=[b_])
                    else:
                        P.act(b_[:n], ps[:n, :], AF.Silu, [ps], [b_])
                        P.dma(C.sgd[r0:r0 + n, off - OG:off - OG + 512], b_[:n], reads=[b_])
            P.fence()
        if int(os.environ.get('MK_B', '3')) < 2:
            return False
        with ExitStack() as lst:
            acc = P.sb([128, 4160], F32, stack=lst)
            work = P.sb([128, 4160], F32, stack=lst)
            maskb = P.sb([128, 4160], BF16, stack=lst)
            kiS = P.sb([128, 4160], BF16, stack=lst)
            qi_t = [P.sb([128, 1024], BF16, stack=lst) for _ in range(2)]
            qiT = [P.sb([128, 8, 128], BF16, stack=lst) for _ in range(2)]
            rl = [P.sb([128, 512], F32, stack=lst) for _ in range(2)]
            m8 = P.sb([128, 8], F32, stack=lst)
            thr = P.sb([128, 1], F32, stack=lst)
            cst = [P.sb([128, 8, 128], F32, stack=lst) for _ in range(2)]
            for qd_ in range(4):
                cs = cst[qd_ % 2]
                src = C.cbi[qd_ * 1024:(qd_ + 1) * 1024, :].rearrange("(b p) d -> p b d", p=128)
                P.dma(cs[:, :, 0:64], src, writes=[cs])
                P.dma(cs[:, :, 64:128], src, writes=[cs])
                for g in range(2):
                    for b in range(4):
                        P.tr(C.ptf[:, b * 128:(b + 1) * 128], cs[:, g * 4 + b, :], C.ident[:], [cs, C.ident], [C.ptf])
                    P.cp("dve" if g % 2 else "act", kiS[:, qd_ * 1024 + g * 512: qd_ * 1024 + (g + 1) * 512], C.ptf[:],
                         [C.ptf], [kiS])
            P.cp("pool", kiS[:, 4096:4160], kiT2[:, 2048:2112], [kiT2], [kiS])
            for t in range(NT):
                n = tn(t)
                r0 = t * 128
                S = 128 * (t + 1) if t < 16 else 4160
                keys = kiT2 if t < 16 else kiS
                qt, qT_ = qi_t[t % 2], qiT[t % 2]
                P.dma(qt[:n], C.qid[r0:r0 + n, :], writes=[qt])
                for hp in range(8):
                    P.tr(C.ptb[:, hp * 128:hp * 128 + n], qt[:n, hp * 128:(hp + 1) * 128], C.identb[:n, :n],
                         [qt, C.identb], [C.ptb])
                P.cp("act", qT_[:, :, :n], C.ptb[:].rearrange("p (a b) -> p a b", a=8)[:, :, :n], [C.ptb], [qT_])
                kq = 0
                for hd in range(16):
                    hp, par = hd // 2, hd % 2
                    for c0 in range(0, S, 512):
                        w = min(512, S - c0)
                        sc = C.psc[kq % 2]
                        r_ = rl[kq % 2]
                        kq += 1
                        P.mm(sc[:n, :w], qT_[par * 64:(par + 1) * 64, hp, :n], keys[par * 64:(par + 1) * 64, c0:c0 + w],
                             True, True, [qT_, keys], [sc])
                        P.act(r_[:n, :w], sc[:n, :w], AF.Relu, [sc], [r_], scale=0.125)
                        if hd == 0:
                            P.ts("dve", acc[:n, c0:c0 + w], r_[:n, :w], wis[:n, t, 0:1], None, ALU.mult,
                                 reads=[r_, wis], writes=[acc])
                        else:
                            P.stt("dve", acc[:n, c0:c0 + w], r_[:n, :w], wis[:n, t, hd:hd + 1], acc[:n, c0:c0 + w],
                                  ALU.mult, ALU.add, [r_, wis, acc], [acc])
                if t < 16:
                    P.op("dve", lambda E, t=t: E.memset(acc[0:64, t * 128 + 64:(t + 1) * 128], NEG), [], [acc])
                if t >= 2:
                    for rnd in range(32):
                        srcw = acc if rnd == 0 else work
                        P.op("dve", lambda E, srcw=srcw, n=n, S=S: E.max(out=m8[:n], in_=srcw[:n, :S]), [srcw], [m8])
                        if rnd < 31:
                            P.op("dve", lambda E, srcw=srcw, n=n, S=S: E.match_replace(
                                out=work[:n, :S], in_to_replace=m8[:n], in_values=srcw[:n, :S], imm_value=NEG),
                                [srcw, m8], [work])
                    P.cp("dve", thr[:n], m8[:n, 7:8], [m8], [thr])
                else:
                    P.op("dve", lambda E, n=n: E.memset(thr[:n], -1.0e29), [], [thr])
                P.ts("dve", maskb[:n, :S], acc[:n, :S], thr[:n, 0:1], None, ALU.is_ge, reads=[acc, thr], writes=[maskb])
                P.dma(C.maskd[r0:r0 + n, 0:S], maskb[:n, :S], reads=[maskb])
            P.fence()
        if int(os.environ.get('MK_B', '3')) < 3:
            return False
        with ExitStack() as lst:
            q_t = [P.sb([128, D], BF16, stack=lst) for _ in range(2)]
            sg_t = [P.sb([128, D], BF16, stack=lst) for _ in range(2)]
            mk = [P.sb([128, 4160], BF16, stack=lst) for _ in range(1)]
            maskT = P.sb([128, 2112], BF16, stack=lst)
            qT = P.sb([128, 16, 128], BF16, stack=lst)
            PTt = P.sb([128, 8448], BF16, stack=lst)
            kTc = P.sb([128, PAST], BF16, stack=lst)
            vca = P.sb([128, 32, 132], BF16, stack=lst)
            P.op("pool", lambda E: E.memset(vca[:, :, 128:129], 1.0), [], [vca])
            kcs = [P.sb([128, 8, 128], F32, stack=lst) for _ in range(2)]
            vcs = [P.sb([128, 8, 128], F32, stack=lst) for _ in range(2)]
            rr = P.sb([128, 4], F32, stack=lst)
            of = P.sb([128, 2, 128], F32, stack=lst)
            ogt = [P.sb([128, D], BF16, stack=lst) for _ in range(2)]
            scale = 128 ** -0.5
            kq = 0
            for t in range(NT):
                n = tn(t)
                r0 = t * 128
                S = 128 * (t + 1) if t < 16 else 4160
                nb = (S + 127) // 128
                q_, s_, m_, og_ = q_t[t % 2], sg_t[t % 2], mk[0], ogt[t % 2]
                P.dma(q_[:n], C.qd[r0:r0 + n, :], writes=[q_])
                P.dma(s_[:n], C.sgd[r0:r0 + n, :], writes=[s_])
                P.dma(m_[:n, :S], C.maskd[r0:r0 + n, 0:S], writes=[m_])
                for j0 in range(0, nb, 8):
                    j1 = min(nb, j0 + 8)
                    rmax = 0
                    for jb in range(j0, j1):
                        rows = min(128, S - jb * 128)
                        rmax = max(rmax, rows)
                        P.tr(C.ptb[:rows, (jb - j0) * 128:(jb - j0) * 128 + n], m_[:n, jb * 128:jb * 128 + rows],
                             C.identb[:n, :n], [m_, C.identb], [C.ptb])
                    full = [jb for jb in range(j0, j1) if min(128, S - jb * 128) == 128]
                    if full:
                        P.cp("act", maskT[:, j0 * n:(j0 + len(full)) * n].rearrange("p (a b) -> p a b", b=n),
                             C.ptb[:].rearrange("p (a b) -> p a b", a=8)[:, 0:len(full), :n], [C.ptb], [maskT])
                    if len(full) < j1 - j0:
                        jb = j1 - 1
                        P.cp("act", maskT[:64, jb * n:(jb + 1) * n], C.ptb[:64, (jb - j0) * 128:(jb - j0) * 128 + n],
                             [C.ptb], [maskT])
                for hp in range(2):
                    for hh in range(8):
                        hd = hp * 8 + hh
                        P.tr(C.ptb[:, hh * 128:hh * 128 + n], q_[:n, hd * 128:(hd + 1) * 128], C.identb[:n, :n],
                             [q_, C.identb], [C.ptb])
                    P.cp("dve", qT[:, hp * 8:(hp + 1) * 8, :n], C.ptb[:].rearrange("p (a b) -> p a b", a=8)[:, :, :n],
                         [C.ptb], [qT])
                for g in range(4):
                    if t == 16:
                        for qd_ in range(4):
                            kc, vc = kcs[qd_ % 2], vcs[qd_ % 2]
                            P.dma(kc[:], C.cbk[qd_ * 1024:(qd_ + 1) * 1024, g * 128:(g + 1) * 128].rearrange(
                                "(b p) d -> p b d", p=128), writes=[kc])
                            P.dma(vc[:], C.cbv[qd_ * 1024:(qd_ + 1) * 1024, g * 128:(g + 1) * 128].rearrange(
                                "(b p) d -> p b d", p=128), writes=[vc])
                            P.cp("pool", vca[:, qd_ * 8:(qd_ + 1) * 8, 0:128], vc[:], [vc], [vca])
                            for gg in range(2):
                                for b in range(4):
                                    P.tr(C.ptf[:, b * 128:(b + 1) * 128], kc[:, gg * 4 + b, :], C.ident[:],
                                         [kc, C.ident], [C.ptf])
                                P.cp("dve" if gg % 2 else "act",
                                     kTc[:, qd_ * 1024 + gg * 512: qd_ * 1024 + (gg + 1) * 512], C.ptf[:], [C.ptf], [kTc])

                    def kblk(jb):
                        if t < 16:
                            return kT[:, g, jb * 128:(jb + 1) * 128], vaug[:, jb, g, 0:129], 128, [kT], [vaug]
                        if jb < 32:
                            return kTc[:, jb * 128:(jb + 1) * 128], vca[:, jb, 0:129], 128, [kTc], [vca]
                        return kT[:, g, 2048:2112], vaug[:64, 16, g, 0:129], 64, [kT], [vaug]

                    for jb in range(nb):
                        ka, va, rows, kr, vr = kblk(jb)
                        sc = C.psc[kq % 2]
                        kq += 1
                        P.mm(sc[:rows, 0:4 * n], ka, qT[:, g * 4:(g + 1) * 4, :n], True, True, kr + [qT], [sc])
                        pt = PTt[:rows, jb * 4 * n:(jb + 1) * 4 * n]
                        P.act(pt, sc[:rows, 0:4 * n], AF.Exp, [sc], [PTt], scale=scale)
                        pt3 = pt.rearrange("p (a b) -> p a b", a=4)
                        mT = maskT[:rows, jb * n:(jb + 1) * n].unsqueeze(1).to_broadcast([rows, 4, n])
                        P.tt("dve", pt3, pt3, mT, ALU.mult, [PTt, maskT], [PTt])
                    for r in range(4):
                        po = C.po[r // 2]
                        for jb in range(nb):
                            ka, va, rows, kr, vr = kblk(jb)
                            P.mm(po[:n, (r % 2) * 256:(r % 2) * 256 + 129],
                                 PTt[:rows, jb * 4 * n + r * n: jb * 4 * n + (r + 1) * n], va,
                                 jb == 0, jb == nb - 1, [PTt] + vr, [po])
                    for b in range(2):
                        po = C.po[b]
                        pov = po[:].rearrange("p (a b) -> p a b", a=2)
                        P.op("dve", lambda E, pov=pov, b=b, n=n: E.reciprocal(rr[:n, b * 2:b * 2 + 2], pov[:n, :, 128]),
                             [po], [rr])
                        P.tt("dve", of[:n], pov[:n, :, 0:128],
                             rr[:n, b * 2:b * 2 + 2].unsqueeze(2).to_broadcast([n, 2, 128]), ALU.mult, [po, rr], [of])
                        c0 = g * 512 + b * 256
                        P.tt("dve", og_[:n, c0:c0 + 256], of[:n].rearrange("p a b -> p (a b)"), s_[:n, c0:c0 + 256],
                             ALU.mult, [of, s_], [og_])
                P.dma(C.og[r0:r0 + n, :], og_[:n], reads=[og_])
            P.fence()
    return True


def layer_c1(C, li):
    P = C.P
    lst = C.lst
    hT = C.hT
    ms = P.sb([128, 128], F32, stack=lst)
    P.dma(ms[0:96, :], C.c_mu.rearrange("n (c p) -> (n c) p", p=128), writes=[ms])
    P.dma(ms[96:112, :], C.sshift.rearrange("o (c p) -> (o c) p", p=128), writes=[ms])
    P.tr(C.ptf[:, 0:112], ms[0:112, :], C.ident[:112, :112], [ms, C.ident], [C.ptf])
    mu = P.sb([128, 112], F32, stack=lst)
    om = P.sb([128, 96], F32, stack=lst)
    P.cp("act", mu[:], C.ptf[:, 0:112], [C.ptf], [mu])
    P.ts("dve", om[:], mu[:, 0:96], -1.0, 1.0, ALU.mult, ALU.add, reads=[mu], writes=[om])
    lerpT = P.sb([128, 16, NTOK], BF16, stack=lst)
    tmpb = [P.sb([128, NTOK], BF16, stack=lst) for _ in range(2)]
    wf_bufs = [P.sb([128, 4, 512], F32, stack=lst) for _ in range(2)]
    wbs = [P.sb([128, 16, 512], BF16, stack=lst) for _ in range(1)]
    ev = [P.sb([128, 512], F32, stack=lst) for _ in range(3)]
    ctr = [0]
    kk = 0
    dsts = [C.c_r, C.c_k, C.c_v, C.c_sg]
    for nidx in range(6):
        for c in range(16):
            tm = tmpb[c % 2]
            col = nidx * 16 + c
            P.ts("dve", tm[:], hT[:, c, :], om[:, col:col + 1], None, ALU.mult, reads=[hT, om], writes=[tm])
            P.stt("dve", lerpT[:, c, 1:NTOK], hT[:, c, 0:NTOK - 1], mu[:, col:col + 1], tm[:, 1:NTOK],
                  ALU.mult, ALU.add, [hT, mu, tm], [lerpT])
            P.cp("dve", lerpT[:, c, 0:1], tm[:, 0:1], [tm], [lerpT])
            P.stt("dve", lerpT[:, c, 2048:2049], mu[:, 96 + c:97 + c], mu[:, col:col + 1], tm[:, 2048:2049],
                  ALU.mult, ALU.add, [mu, tm], [lerpT])
        if nidx < 4:
            for blk in range(4):
                wb = wbs[0]
                load_w_block(C, wf_bufs, wb, [(C.c_w_rkvg[nidx][:, blk * 512:(blk + 1) * 512], 0)], 512, ctr)
                for t in range(NT):
                    n = tn(t)
                    r0 = t * 128
                    ps = C.pp[t % 2]
                    for c in range(16):
                        P.mm(ps[:n, :], lerpT[:, c, r0:r0 + n], wb[:, c, :], c == 0, c == 15, [lerpT, wb], [ps])
                    e_ = ev[kk % 3]
                    kk += 1
                    if nidx < 3:
                        P.cp("act", e_[:n], ps[:n, :], [ps], [e_])
                    else:
                        P.act(e_[:n], ps[:n, :], AF.Silu, [ps], [e_])
                    P.dma(dsts[nidx][r0:r0 + n, blk * 512:(blk + 1) * 512], e_[:n], reads=[e_])
        else:
            wsrc = C.c_w_la if nidx == 4 else C.c_a_la
            dstT = C.tT if nidx == 4 else C.aT
            wf = wf_bufs[0]
            wb = wbs[0]
            for qtr in range(4):
                P.dma(wf[:, :, 0:96], wsrc[qtr * 512:(qtr + 1) * 512, :].rearrange("(c p) n -> p c n", p=128), writes=[wf])
                P.cp("dve", wb[:, qtr * 4:(qtr + 1) * 4, 0:96], wf[:, :, 0:96], [wf], [wb])
            for tb in range(0, NTOK, 512):
                w = min(512, NTOK - tb)
                ps = C.pp[(tb // 512) % 2]
                for c in range(16):
                    P.mm(ps[:96, :w], wb[:, c, 0:96], lerpT[:, c, tb:tb + w], c == 0, c == 15, [lerpT, wb], [ps])
                if nidx == 4:
                    P.act(dstT[:96, tb:tb + w], ps[:96, :w], AF.Tanh, [ps], [dstT])
                else:
                    P.cp("act", dstT[:96, tb:tb + w], ps[:96, :w], [ps], [dstT])
    return True


def layer_c2(C, li):
    P = C.P
    lst = C.lst
    cb = {}
    for nm in ("c_w0", "c_a0", "c_k_k", "c_k_a", "c_r_k"):
        cb[nm] = P.sb([128, D], F32, stack=lst, name="cb_" + nm)
        P.dma(cb[nm][:], getattr(C, nm)[0:1, :].partition_broadcast(128), writes=[cb[nm]])
    lbf = P.sb([128, D], F32, stack=lst)
    wlb = P.sb([128, D], BF16, stack=lst)
    alb = P.sb([128, D], BF16, stack=lst)
    P.dma(lbf[:96], C.c_w_lb[:, :], writes=[lbf])
    P.cp("dve", wlb[:96], lbf[:96], [lbf], [wlb])
    P.dma(lbf[:96], C.c_a_lb[:, :], writes=[lbf])
    P.cp("dve", alb[:96], lbf[:96], [lbf], [alb])
    tri = P.sb([128, 128], F32, stack=lst)
    P.dma(tri[:], C.cmask[4], writes=[tri])
    selc = P.sb([128, 2], F32, stack=lst)
    P.dma(selc[:], C.selc[:, :], writes=[selc])
    eps12 = P.sb([128, 1], F32, stack=lst)
    B = {nm: P.sb([128, D], F32, stack=lst, name="c2_" + nm) for nm in
         ("R", "K", "V", "A", "W", "KK", "T1", "T2", "CUM", "G")}
    ssq = P.sb([128, 32], F32, stack=lst)
    ob16 = [P.sb([128, D], BF16, stack=lst) for _ in range(2)]
    bs = P.sb([128, 32], F32, stack=lst)
    v3 = lambda tl, n: tl[:n].rearrange("p (a b) -> p a b", a=32)
    for t in range(NT):
        n = tn(t)
        r0 = t * 128
        nc_ = 2 if t < 16 else 1
        R, K, V, A, W, KK, T1, T2, CUM, G = (B[x] for x in ("R", "K", "V", "A", "W", "KK", "T1", "T2", "CUM", "G"))
        P.dma(R[:n], C.c_r[r0:r0 + n, :], writes=[R])
        P.dma(K[:n], C.c_k[r0:r0 + n, :], writes=[K])
        P.dma(V[:n], C.c_v[r0:r0 + n, :], writes=[V])
        for blk in range(4):
            cs = slice(blk * 512, (blk + 1) * 512)
            ps = C.pp[blk % 2]
            P.mm(ps[:n, :], C.tT[:96, r0:r0 + n], wlb[:96, cs], True, True, [C.tT, wlb], [ps])
            P.tt("dve", W[:n, cs], ps[:n, :], cb["c_w0"][:n, cs], ALU.add, [ps, cb["c_w0"]], [W])
            ps2 = C.psc[blk % 2]
            P.mm(ps2[:n, :], C.aT[:96, r0:r0 + n], alb[:96, cs], True, True, [C.aT, alb], [ps2])
            P.tt("dve", A[:n, cs], ps2[:n, :], cb["c_a0"][:n, cs], ALU.add, [ps2, cb["c_a0"]], [A])
        P.act(W[:n], W[:n], AF.Sigmoid, [W], [W])
        P.ts("dve", W[:n], W[:n], -math.exp(-0.5), None, ALU.mult, reads=[W], writes=[W])
        P.act(A[:n], A[:n], AF.Sigmoid, [A], [A])
        P.tt("dve", KK[:n], K[:n], cb["c_k_k"][:n], ALU.mult, [K, cb["c_k_k"]], [KK])
        P.tt("dve", T1[:n], KK[:n], KK[:n], ALU.mult, [KK], [T1])
        P.op("dve", lambda E, n=n, T1=T1: E.reduce_sum(ssq[:n], v3(T1, n), AX.X), [T1], [ssq])
        P.act(ssq[:n], ssq[:n], AF.Sqrt, [ssq], [ssq])
        P.ts("dve", ssq[:n], ssq[:n], 1e-12, None, ALU.max, reads=[ssq], writes=[ssq])
        P.op("dve", lambda E, n=n: E.reciprocal(ssq[:n], ssq[:n]), [ssq], [ssq])
        P.tt("dve", v3(KK, n), v3(KK, n), ssq[:n].unsqueeze(2).to_broadcast([n, 32, 64]), ALU.mult, [KK, ssq], [KK])
        P.stt("dve", T1[:n], A[:n], -1.0, cb["c_k_a"][:n], ALU.add, ALU.mult, [A, cb["c_k_a"]], [T1])
        P.stt("dve", T2[:n], T1[:n], 1.0, K[:n], ALU.add, ALU.mult, [T1, K], [T2])
        P.tt("dve", T1[:n], R[:n], T2[:n], ALU.mult, [R, T2], [T1])
        P.tt("dve", T1[:n], T1[:n], cb["c_r_k"][:n], ALU.mult, [T1, cb["c_r_k"]], [T1])
        P.op("dve", lambda E, n=n, T1=T1: E.reduce_sum(bs[:n], v3(T1, n), AX.X), [T1], [bs])
        P.tt("dve", v3(T1, n), v3(V, n), bs[:n].unsqueeze(2).to_broadcast([n, 32, 64]), ALU.mult, [V, bs], [T1])
        P.dma(C.c_bv[r0:r0 + n, :], T1[:n], reads=[T1])
        for blk in range(4):
            cs = slice(blk * 512, (blk + 1) * 512)
            ps = C.po[blk % 2]
            P.mm(ps[:n, :], tri[:n, :n], W[:n, cs], True, True, [tri, W], [ps])
            P.cp("act", CUM[:n, cs], ps[:n, :], [ps], [CUM])
        for fc in range(16):
            P.mm(C.ptf[:, fc * 2:fc * 2 + nc_], W[:n, fc * 128:(fc + 1) * 128], selc[:n, 0:nc_], True, True,
                 [W, selc], [C.ptf])
        P.act(C.gC[:, :, t * 2:t * 2 + nc_], C.ptf[:, 0:32].rearrange("p (a b) -> p a b", b=2)[:, :, 0:nc_], AF.Exp,
              [C.ptf], [C.gC])
        P.act(G[:n], CUM[:n], AF.Exp, [CUM], [G])
        P.tt("dve", ob16[0][:n], R[:n], G[:n], ALU.mult, [R, G], [ob16[0]])
        P.dma(C.c_Rt[r0:r0 + n, :], ob16[0][:n], reads=[ob16[0]])
        P.act(G[:n], CUM[:n], AF.Exp, [CUM], [G], scale=-1.0)
        P.tt("dve", ob16[1][:n], T2[:n], G[:n], ALU.mult, [T2, G], [ob16[1]])
        P.dma(C.c_Kt[r0:r0 + n, :], ob16[1][:n], reads=[ob16[1]])
        P.tt("dve", A[:n], A[:n], KK[:n], ALU.mult, [A, KK], [A])
        P.tt("dve", ob16[0][:n], A[:n], G[:n], ALU.mult, [A, G], [ob16[0]])
        P.dma(C.c_Bt[r0:r0 + n, :], ob16[0][:n], reads=[ob16[0]])
        P.tt("dve", CUM[:n], CUM[:n], W[:n], ALU.subtract, [CUM, W], [CUM])
        P.act(G[:n], CUM[:n], AF.Exp, [CUM], [G])
        P.stt("dve", ob16[1][:n], KK[:n], -1.0, G[:n], ALU.mult, ALU.mult, [KK, G], [ob16[1]])
        P.dma(C.c_At[r0:r0 + n, :], ob16[1][:n], reads=[ob16[1]])


def layer_c3(C, li):
    P = C.P
    lst = C.lst
    mk = [P.sb([128, 128], F32, stack=lst, name="c3m%d" % i) for i in range(5)]
    for i in range(5):
        P.dma(mk[i][:], C.cmask[i], writes=[mk[i]])
    SEL2f, BDM, MUS, MLS, MUI = mk
    SEL2 = P.sb([128, 128], BF16, stack=lst, name="c3sel2b")
    P.cp("dve", SEL2[:], SEL2f[:], [SEL2f], [SEL2])
    lnw2 = P.sb([128, 16, 64], F32, stack=lst)
    lnb2 = P.sb([128, 16, 64], F32, stack=lst)
    for h in range(2):
        for dst, src in ((lnw2, C.c_ln_w), (lnb2, C.c_ln_b)):
            sv = src[0:1, :].rearrange("o (a h v) -> o a h v", h=2, v=64)[:, :, h, :]
            P.dma(dst[h * 64:(h + 1) * 64], sv.partition_broadcast(64), writes=[dst])
    epsg = P.sb([128, 1], F32, stack=lst)
    P.op("dve", lambda E: E.memset(epsg[:], 64e-5), [], [epsg])
    banks = [C.pp[0], C.pp[1], C.psc[0], C.psc[1], C.po[0], C.po[1], C.ptf]
    bk = [0]

    def bank():
        b = banks[bk[0] % len(banks)]
        bk[0] += 1
        return b

    tok = {nm: [P.sb([128, NT, 128], BF16, stack=lst, name="c3_%s%d" % (nm, i)) for i in range(2)]
           for nm in ("At", "Bt", "Kt", "Rt")}
    hv = {nm: [P.sb([128, 33, 64], F32, stack=lst, name="c3_%s%d" % (nm, i)) for i in range(2)]
          for nm in ("V2", "BV2", "SG2")}
    srcs = {"At": C.c_At, "Bt": C.c_Bt, "Kt": C.c_Kt, "Rt": C.c_Rt, "V2": C.c_v, "BV2": C.c_bv, "SG2": C.c_sg}
    sq = lambda nm: [[P.sb([128, 128], F32, stack=lst, name="c3q_%s%d" % (nm, i)) for i in range(2)]]
    Q = {nm: [P.sb([128, 128], BF16, stack=lst, name="c3q_%s%d" % (nm, i)) for i in range(2)]
         for nm in ("BD_A", "BD_B", "BD_K", "BD_R", "BDT_B", "BDT_K", "N", "NT", "AakT", "WbT", "WkT", "Pm",
                    "Ma", "MTa", "Mb", "MTb")}
    S2s = [P.sb([128, 64], F32, stack=lst) for _ in range(2)]
    S2gs = [P.sb([128, 64], F32, stack=lst) for _ in range(2)]
    Xss = [P.sb([128, 64], BF16, stack=lst) for _ in range(2)]
    Uss = [P.sb([128, 64], BF16, stack=lst) for _ in range(2)]
    cen = [P.sb([128, 64], F32, stack=lst) for _ in range(4)]
    junk = P.sb([128, 64], F32, stack=lst)
    st1 = [P.sb([128, 1], F32, stack=lst) for _ in range(4)]
    st2 = [P.sb([128, 1], F32, stack=lst) for _ in range(4)]
    ogt = [P.sb([128, 64], BF16, stack=lst) for _ in range(4)]
    sws = [P.sb([128, 128], F32, stack=lst) for _ in range(2)]
    stos = [P.sb([128, 128], F32, stack=lst) for _ in range(2)]
    Q2 = [Q, {nm: [P.sb([128, 128], BF16, stack=lst, name="c3r_%s%d" % (nm, i)) for i in range(2)] for nm in Q}]
    S2bs = [P.sb([128, 64], BF16, stack=lst) for _ in range(2)]
    V2bs = [P.sb([128, 33, 64], BF16, stack=lst) for _ in range(2)]
    NHP = int(os.environ.get("MK_NHP", "16"))

    def stream(hp, sx):
        S2, S2g, Xs, Us, sw, sto = S2s[sx], S2gs[sx], Xss[sx], Uss[sx], sws[sx], stos[sx]
        QQ = Q2[sx]
        S2b, V2b = S2bs[sx], V2bs[sx]

        def save_state(which):
            b = bank()
            P.tr(b[:64, 0:128], S2[:, :], C.ident[:, :], [S2, C.ident], [b])
            P.cp("act", sto[:64, :], b[:64, 0:128], [b], [sto])
            P.dma(C.o_cw[which, hp * 128:(hp + 1) * 128, :].rearrange("(h v) k -> v h k", h=2),
                  sto[:64, :].rearrange("v (h k) -> v h k", h=2), reads=[sto])

        fs = slice(hp * 128, (hp + 1) * 128)
        for nm in ("At", "Bt", "Kt", "Rt"):
            tl = tok[nm][sx]
            P.dma(tl[:, 0:16, :], srcs[nm][0:2048, fs].rearrange("(t p) f -> p t f", p=128), writes=[tl])
            P.dma(tl[:64, 16, :], srcs[nm][2048:2112, fs], writes=[tl])
        for nm in ("V2", "BV2", "SG2"):
            tl = hv[nm][sx]
            for h in range(2):
                hs_ = slice(hp * 128 + h * 64, hp * 128 + (h + 1) * 64)
                P.dma(tl[h * 64:(h + 1) * 64, 0:32, :], srcs[nm][0:2048, hs_].rearrange("(c j) v -> j c v", j=64),
                      writes=[tl])
                P.dma(tl[h * 64:(h + 1) * 64, 32, :], srcs[nm][2048:2112, hs_], writes=[tl])
        At, Bt, Kt, Rt = (tok[x][sx] for x in ("At", "Bt", "Kt", "Rt"))
        V2, BV2, SG2 = (hv[x][sx] for x in ("V2", "BV2", "SG2"))
        P.op("dve", lambda E: E.memset(S2[:], 0.0), [], [S2])
        P.op("dve", lambda E: E.memset(S2b[:], 0.0), [], [S2b])
        P.cp("dve", V2b[:], V2[:], [V2], [V2b])
        yield

        def prod(dst, lhsT, rhs, mask, rd):
            b = bank()
            P.mm(b[:, 0:128], lhsT, rhs, True, True, rd, [b])
            P.tt("dve", dst[:], b[:, 0:128], mask[:], ALU.mult, [b, mask], [dst])

        def pre(c):
            k = c % 2
            t, cp = (c // 2, c % 2) if c < 32 else (16, 0)
            rows = slice(cp * 64, cp * 64 + 64)
            q = {nm: QQ[nm][k] for nm in QQ}
            prod(q["BD_A"], At[rows, t, :], SEL2[rows, :], BDM, [At, SEL2])
            prod(q["BD_B"], Bt[rows, t, :], SEL2[rows, :], BDM, [Bt, SEL2])
            yield
            prod(q["BD_K"], Kt[rows, t, :], SEL2[rows, :], BDM, [Kt, SEL2])
            prod(q["BD_R"], Rt[rows, t, :], SEL2[rows, :], BDM, [Rt, SEL2])
            yield
            prod(q["BDT_B"], SEL2[rows, :], Bt[rows, t, :], BDM, [Bt, SEL2])
            prod(q["BDT_K"], SEL2[rows, :], Kt[rows, t, :], BDM, [Kt, SEL2])
            yield
            prod(q["N"], q["BD_B"][:], q["BD_A"][:], MUS, [q["BD_B"], q["BD_A"]])
            prod(q["NT"], q["BD_A"][:], q["BD_B"][:], MLS, [q["BD_B"], q["BD_A"]])
            yield
            prod(q["AakT"], q["BD_K"][:], q["BD_A"][:], MUS, [q["BD_K"], q["BD_A"]])
            prod(q["WbT"], q["BD_B"][:], q["BD_R"][:], MUI, [q["BD_B"], q["BD_R"]])
            prod(q["WkT"], q["BD_K"][:], q["BD_R"][:], MUI, [q["BD_K"], q["BD_R"]])
            yield
            Pm = q["Pm"]
            P.tt("dve", Pm[:], q["N"][:], C.identb[:], ALU.add, [q["N"], C.identb], [Pm])
            M, MT = q["N"], q["NT"]
            alt = [(q["Ma"], q["MTa"]), (q["Mb"], q["MTb"])]
            for lvl in range(5):
                M2, M2T = alt[lvl % 2]
                if lvl < 4:
                    b = bank()
                    P.mm(b[:, 0:128], MT[:], M[:], True, True, [MT, M], [b])
                    P.cp("act", M2[:], b[:, 0:128], [b], [M2])
                b = bank()
                P.mm(b[:, 0:128], M[:], MT[:], True, True, [MT, M], [b])
                P.cp("act", M2T[:], b[:, 0:128], [b], [M2T])
                yield
                b = bank()
                P.mm(b[:, 0:128], M2T[:], Pm[:], True, True, [M2T, Pm], [b])
                P.tt("dve", Pm[:], b[:, 0:128], Pm[:], ALU.add, [b, Pm], [Pm])
                M, MT = M2, M2T
                yield

        def serial(c):
            k = c % 2
            q = {nm: QQ[nm][k] for nm in QQ}
            Pm = q["Pm"]
            Vc = V2b[:, c, :]
            b = bank()
            P.mm(b[:, 0:64], q["BD_A"][:], S2b[:], True, False, [q["BD_A"], S2b], [b])
            P.mm(b[:, 0:64], q["AakT"][:], Vc, False, True, [q["AakT"], V2b], [b])
            P.cp("act", Xs[:], b[:, 0:64], [b], [Xs])
            yield
            b = bank()
            P.mm(b[:, 0:64], Pm[:], Xs[:], True, True, [Pm, Xs], [b])
            P.cp("act", Us[:], b[:, 0:64], [b], [Us])
            yield
            bo = bank()
            P.mm(bo[:, 0:64], q["BD_R"][:], S2b[:], True, False, [q["BD_R"], S2b], [bo])
            P.mm(bo[:, 0:64], q["WbT"][:], Us[:], False, False, [q["WbT"], Us], [bo])
            P.mm(bo[:, 0:64], q["WkT"][:], Vc, False, True, [q["WkT"], V2b], [bo])
            bd = bank()
            P.mm(bd[:, 0:64], q["BDT_B"][:], Us[:], True, False, [q["BDT_B"], Us], [bd])
            P.mm(bd[:, 0:64], q["BDT_K"][:], Vc, False, True, [q["BDT_K"], V2b], [bd])
            gcol = C.gC[:, hp, c:c + 1]
            P.ts("dve", S2g[:], S2[:], gcol, None, ALU.mult, reads=[S2, C.gC], writes=[S2g])
            P.stt("dve", S2[:], bd[:, 0:64], gcol, S2g[:], ALU.mult, ALU.add, [bd, C.gC, S2g], [S2])
            P.cp("dve", S2b[:], S2[:], [S2], [S2b])
            yield
            e = sx * 2 + k
            ce, s1, s2_, og_ = cen[e], st1[e], st2[e], ogt[e]
            P.op("dve", lambda E, bo=bo, s1=s1: E.reduce_sum(s1[:], bo[:, 0:64], AX.X), [bo], [s1])
            P.ts("dve", s1[:], s1[:], -1.0 / 64, None, ALU.mult, reads=[s1], writes=[s1])
            P.ts("dve", ce[:], bo[:, 0:64], s1[:, 0:1], None, ALU.add, reads=[bo, s1], writes=[ce])
            P.act(junk[:], ce[:], AF.Square, [ce], [junk, s2_], accum_out=s2_[:])
            P.act(s2_[:], s2_[:], AF.Sqrt, [s2_, epsg], [s2_], bias=epsg[:], scale=1.0 / 64)
            P.op("dve", lambda E, s2_=s2_: E.reciprocal(s2_[:], s2_[:]), [s2_], [s2_])
            yield
            P.stt("dve", ce[:], ce[:], s2_[:, 0:1], lnw2[:, hp, :], ALU.mult, ALU.mult, [ce, s2_, lnw2], [ce])
            P.tt("pool", ce[:], ce[:], lnb2[:, hp, :], ALU.add, [ce, lnb2], [ce])
            P.tt("pool", ce[:], ce[:], BV2[:, c, :], ALU.add, [ce, BV2], [ce])
            P.tt("pool", og_[:], ce[:], SG2[:, c, :], ALU.mult, [ce, SG2], [og_])
            for h in range(2):
                P.dma(C.og[c * 64:(c + 1) * 64, hp * 128 + h * 64: hp * 128 + (h + 1) * 64], og_[h * 64:(h + 1) * 64, :],
                      reads=[og_])
            yield

        yield from pre(0)
        for c in range(33):
            if c + 1 < 33:
                yield from pre(c + 1)
            if c == 32:
                save_state(0)
                P.dma(sw[:64, :].rearrange("v (h k) -> v h k", h=2),
                      C.swkv[hp * 128:(hp + 1) * 128, :].rearrange("(h v) k -> v h k", h=2), writes=[sw])
                b = bank()
                P.tr(b[:, 0:64], sw[:64, :], C.ident[:64, :64], [sw, C.ident], [b])
                P.cp("act", S2[:], b[:, 0:64], [b], [S2])
                P.cp("dve", S2b[:], S2[:], [S2], [S2b])
            yield from serial(c)
        save_state(1)

    def interleave(gens):
        gens = list(gens)
        while gens:
            for g_ in list(gens):
                try:
                    next(g_)
                except StopIteration:
                    gens.remove(g_)

    for hp0 in range(0, NHP, 2):
        interleave([stream(hp0 + d, d) for d in range(2) if hp0 + d < NHP])
    return True


def phase_out(C, x_src, li):
    P = C.P
    lst = C.lst
    ogT = P.sb([128, 16, NTOK], BF16, stack=lst, name="ogT%d" % li)
    ob = [P.sb([128, D], BF16, stack=lst) for _ in range(2)]
    for t in range(NT):
        n = tn(t)
        o = ob[t % 2]
        P.dma(o[:n], C.og[t * 128:t * 128 + n, :], writes=[o])
        for gi in range(2):
            for jj in range(8):
                c = gi * 8 + jj
                P.tr(C.ptb[:, jj * 128:jj * 128 + n], o[:n, c * 128:(c + 1) * 128], C.identb[:n, :n],
                     [o, C.identb], [C.ptb])
            src = C.ptb[:].rearrange("p (a b) -> p a b", a=8)[:, :, :n]
            P.cp("act" if gi == 0 else "dve", ogT[:, gi * 8:(gi + 1) * 8, t * 128:t * 128 + n], src, [C.ptb], [ogT])
    wf_bufs = [P.sb([128, 4, 512], F32, stack=lst) for _ in range(2)]
    wbs = [P.sb([128, 16, 512], BF16, stack=lst) for _ in range(2)]
    xb = [P.sb([128, 512], F32, stack=lst) for _ in range(3)]
    ctr = [0]
    k = 0
    for blk in range(4):
        wb = wbs[blk % 2]
        load_w_block(C, wf_bufs, wb, [(C.w_out[li][:, blk * 512:(blk + 1) * 512], 0)], 512, ctr)
        for t in range(NT):
            n = tn(t)
            ps = C.pp[t % 2]
            xt = xb[k % 3]
            k += 1
            P.dma(xt[:n], x_src[t * 128:t * 128 + n, blk * 512:(blk + 1) * 512], writes=[xt])
            for c in range(16):
                P.mm(ps[:n, :], ogT[:, c, t * 128:t * 128 + n], wb[:, c, :], c == 0, c == 15, [ogT, wb], [ps])
            P.tt("dve", xt[:n], xt[:n], ps[:n, :], ALU.add, [xt, ps], [xt])
            P.dma(C.xs[t * 128:t * 128 + n, blk * 512:(blk + 1) * 512], xt[:n], reads=[xt])


def phase_final(C, x_src):
    P = C.P
    lst = C.lst
    P.dma(C.gb[:], C.final_g[0:1, :].partition_broadcast(128), writes=[C.gb])
    xb = [P.sb([128, D], F32, stack=lst) for _ in range(2)]
    junk = P.sb([128, D], BF16, stack=lst)
    ss = [P.sb([128, 1], F32, stack=lst) for _ in range(2)]
    rs = [P.sb([128, 1], F32, stack=lst) for _ in range(2)]
    for t in range(NT):
        n = tn(t)
        xt, s, r = xb[t % 2], ss[t % 2], rs[t % 2]
        P.dma(xt[:n], x_src[t * 128:t * 128 + n, :], writes=[xt])
        P.act(junk[:n], xt[:n], AF.Square, [xt], [junk, s], accum_out=s[:n])
        P.act(r[:n], s[:n], AF.Sqrt, [s, C.eps6], [r], bias=C.eps6[:n], scale=1.0 / D)
        P.op("dve", lambda E, r=r, n=n: E.reciprocal(r[:n], r[:n]), [r], [r])
        P.stt("dve", xt[:n], xt[:n], r[:n, 0:1], C.gb[:n], ALU.mult, ALU.mult, [xt, r, C.gb], [xt])
        P.dma(C.y[t * 128:t * 128 + n, :], xt[:n], reads=[xt])


def const_masks():
    i = np.arange(128)
    same = (i[:, None] // 64) == (i[None, :] // 64)
    s_, t_ = i[:, None] % 64, i[None, :] % 64
    sel2 = (s_ == t_)
    m = np.stack([sel2, same, same & (s_ < t_), same & (s_ > t_), same & (s_ <= t_)]).astype(np.float32)
    return np.ascontiguousarray(m)


def const_selc():
    i = np.arange(128)
    return np.ascontiguousarray(((i[:, None] // 64) == np.arange(2)[None, :]).astype(np.float32))


def rope_table(dh):
    half = dh // 2
    pos = np.concatenate([np.arange(2048), PAST + np.arange(64)]).astype(np.float32)
    inv = np.power(np.float32(10000.0), -np.arange(half, dtype=np.float32) * np.float32(2.0 / dh)).astype(np.float32)
    ang = pos[:, None] * inv[None, :]
    cos = np.cos(ang).astype(np.float32)
    sin = np.sin(ang).astype(np.float32)
    tab = np.stack([np.concatenate([cos, cos], 1), np.concatenate([-sin, sin], 1)], 1)
    return np.ascontiguousarray(tab.astype(np.float32))


_NC_CACHE = {}


def kernel(x_prompt, x_sample, cache_a_k, cache_a_v, cache_b_k, cache_b_v, cache_b_kidx, state_c_wkv, state_c_shift,
           norm_g, final_g, w_out, a_w_in, a_lam, a_subln_g, b_w_in, c_mu, c_w_rkvg, c_w0, c_w_la, c_w_lb, c_a0,
           c_a_la, c_a_lb, c_k_k, c_k_a, c_r_k, c_ln_w, c_ln_b):
    stage = int(os.environ.get("MK_STAGE", "4"))
    skey = (stage, os.environ.get("MK_HEADS"), os.environ.get("MK_NOATT"), os.environ.get("MK_NOSAMP"), os.environ.get("MK_B"), os.environ.get("MK_BK"), os.environ.get("MK_NHP"))
    ncores = int(os.environ.get("MK_CORES", "8"))
    f = lambda a: np.ascontiguousarray(np.asarray(a, dtype=np.float32))
    if skey not in _NC_CACHE:
        _NC_CACHE[skey] = build_nc(stage)
    nc = _NC_CACHE[skey]
    shared = {
        "norm_g": f(norm_g), "final_g": f(final_g).reshape(1, D), "w_out": f(w_out), "a_w_in": f(a_w_in),
        "a_lam": f(a_lam).reshape(2, 256), "a_subln_g": f(a_subln_g), "b_w_in": f(b_w_in)[0],
        "c_mu": f(c_mu)[0], "c_w_rkvg": f(c_w_rkvg)[0], "c_w0": f(c_w0), "c_w_la": f(c_w_la)[0],
        "c_w_lb": f(c_w_lb)[0], "c_a0": f(c_a0), "c_a_la": f(c_a_la)[0], "c_a_lb": f(c_a_lb)[0],
        "c_k_k": f(c_k_k), "c_k_a": f(c_k_a), "c_r_k": f(c_r_k).reshape(1, D), "c_ln_w": f(c_ln_w),
        "c_ln_b": f(c_ln_b), "rope64": rope_table(64), "rope128": rope_table(128),
        "ident": np.eye(128, dtype=np.float32), "cmask": const_masks(), "selc": const_selc(),
    }
    in_maps = []
    for c in range(ncores):
        m = dict(shared)
        m["xin"] = np.concatenate([f(x_prompt[c // 2]), f(x_sample[c])], 0)
        m["cak"] = f(cache_a_k[:, c]).reshape(2, PAST, D)
        m["cav"] = f(cache_a_v[:, c]).reshape(2, PAST, D)
        m["cbk"] = f(cache_b_k[0, c]).reshape(PAST, 512)
        m["cbv"] = f(cache_b_v[0, c]).reshape(PAST, 512)
        m["cbi"] = f(cache_b_kidx[0, c])
        m["swkv"] = f(state_c_wkv[0, c]).reshape(2048, 64)
        m["sshift"] = f(state_c_shift[0, c]).reshape(1, D)
        in_maps.append(m)
    tr = bool(int(os.environ.get('MK_TRACE', '0')))
    res = run_bass_kernel_spmd(nc, in_maps, core_ids=list(range(ncores)), **({'trace': True} if tr else {}))
    if tr:
        print('EXEC_NS', res.exec_time_ns)
        try:
            import ast, collections
            insts = res.instructions_and_trace[0]
            src = open(__file__).read()
            funcs = [(n.lineno, n.end_lineno, n.name) for n in ast.parse(src).body if isinstance(n, ast.FunctionDef)]
            def fn_of(line):
                for a, b, nm in funcs:
                    if a <= line <= b:
                        return nm
                return '?'
            t0 = min(i.timestamp for i in insts)
            t1 = max(i.end_timestamp for i in insts)
            NBK = 48
            w = (t1 - t0) / NBK
            busy = collections.defaultdict(lambda: [0.0] * NBK)
            fnb = [collections.Counter() for _ in range(NBK)]
            for i in insts:
                if i.is_seq_only:
                    continue
                b = min(NBK - 1, int((i.timestamp - t0) / w))
                busy[str(i.engine)][b] += i.duration
                fnb[b][fn_of(i.source_line)] += i.duration
            print('TOTAL ms', (t1 - t0) / 1e6, 'bucket us', w / 1e3)
            for e, v in busy.items():
                print('%-10s tot %5.1f%% | ' % (e[:10], 100 * sum(v) / (t1 - t0)) + ' '.join('%2d' % min(99, int(100 * x / w)) for x in v))
            print('phase: ' + ' '.join((c.most_common(1)[0][0][-2:] if c else '--') for c in fnb))
        except Exception as ex:
            print('trace summary failed', ex)
    R = res.results
    nb = 4
    y_p = np.zeros((4, 2048, D), np.float32)
    y_s = np.zeros((8, 64, D), np.float32)
    akp = np.zeros((2, 4, 2048, 16, 128), np.float32)
    avp = np.zeros_like(akp)
    aks = np.zeros((2, 8, 64, 16, 128), np.float32)
    avs = np.zeros_like(aks)
    bkp = np.zeros((1, 4, 2048, 4, 128), np.float32)
    bvp = np.zeros_like(bkp)
    bip = np.zeros((1, 4, 2048, 64), np.float32)
    bks = np.zeros((1, 8, 64, 4, 128), np.float32)
    bvs = np.zeros_like(bks)
    bis = np.zeros((1, 8, 64, 64), np.float32)
    cwp = np.zeros((1, 4, 32, 64, 64), np.float32)
    csp = np.zeros((1, 4, D), np.float32)
    cws = np.zeros((1, 8, 32, 64, 64), np.float32)
    css = np.zeros((1, 8, D), np.float32)
    for c in range(ncores):
        r = R[c]
        p = c // 2
        if c % 2 == 0:
            y_p[p] = r["y"][:2048]
            akp[:, p] = r["o_ak"][:, :2048].reshape(2, 2048, 16, 128)
            avp[:, p] = r["o_av"][:, :2048].reshape(2, 2048, 16, 128)
            bkp[0, p] = r["o_bk"][:2048].reshape(2048, 4, 128)
            bvp[0, p] = r["o_bv"][:2048].reshape(2048, 4, 128)
            bip[0, p] = r["o_bi"][:2048]
            cwp[0, p] = r["o_cw"][0].reshape(32, 64, 64)
            csp[0, p] = r["o_cs"][0]
        y_s[c] = r["y"][2048:]
        aks[:, c] = r["o_ak"][:, 2048:].reshape(2, 64, 16, 128)
        avs[:, c] = r["o_av"][:, 2048:].reshape(2, 64, 16, 128)
        bks[0, c] = r["o_bk"][2048:].reshape(64, 4, 128)
        bvs[0, c] = r["o_bv"][2048:].reshape(64, 4, 128)
        bis[0, c] = r["o_bi"][2048:]
        cws[0, c] = r["o_cw"][1].reshape(32, 64, 64)
        css[0, c] = r["o_cs"][1]
    return (y_p, y_s, akp, avp, aks, avs, bkp, bvp, bip, bks, bvs, bis, cwp, csp, cws, css)
```

```python
import os
import math
import numpy as np
from contextlib import ExitStack
import concourse.bass as bass
import concourse.mybir as mybir
from concourse.bass_utils import run_bass_kernel_spmd

F32 = mybir.dt.float32
BF16 = mybir.dt.bfloat16
AF = mybir.ActivationFunctionType
ALU = mybir.AluOpType
AX = mybir.AxisListType

D = 2048
NT = 17
NTOK = 2112
PAST = 4096
DEPTH = 4


def tn(t):
    return 128 if t < 16 else 64


class Buf:
    __slots__ = ("name", "lw", "rd")

    def __init__(self, name=""):
        self.name = name
        self.lw = None
        self.rd = []


class T:
    def __init__(self, t, nb=1, name=""):
        self.t = t
        self.bs = [Buf(name + str(i)) for i in range(nb)]

    def __getitem__(self, k):
        return self.t[k]


class Prog:
    ENGS = ("pe", "act", "dve", "pool", "sp")
    NDMA = {"sp": 12, "act": 4, "pool": 8}

    def __init__(self, nc, stack):
        self.nc = nc
        self.stack = stack
        self.q = {e: [] for e in self.ENGS}
        self.sem = {e: stack.enter_context(nc.semaphore("s_" + e)) for e in self.ENGS}
        self.cnt = {e: 0 for e in self.ENGS}
        self.seen = {e: {} for e in self.ENGS}
        self.pend = {e: {} for e in self.ENGS}
        self.dsem = {}
        self.dcnt = {}
        self.di = {}
        for e, n in self.NDMA.items():
            self.dsem[e] = [stack.enter_context(nc.semaphore("d_%s%d" % (e, i))) for i in range(n)]
            self.dcnt[e] = [0] * n
            self.di[e] = 0
        self.n_ins = 0
        self.uid = 0

    def sb(self, shape, dt, nb=1, name=None, stack=None):
        self.uid += 1
        name = name or "t%d" % self.uid
        t = (stack or self.stack).enter_context(self.nc.sbuf_tensor(name, list(shape), dt))
        return T(t, nb, name)

    def ps(self, shape, dt=F32, nb=1, name=None, stack=None):
        self.uid += 1
        name = name or "p%d" % self.uid
        t = (stack or self.stack).enter_context(self.nc.psum_tensor(name, list(shape), dt))
        return T(t, nb, name)

    def _bufs(self, xs):
        out = []
        for x in xs:
            if isinstance(x, T):
                out.extend(x.bs)
            elif isinstance(x, Buf):
                out.append(x)
            elif x is None:
                pass
            else:
                out.extend(self._bufs(x))
        return out

    def fence(self):
        snap = {}
        for e in self.ENGS:
            if self.cnt[e] > 0:
                snap[("c", e)] = self.cnt[e]
        for e in self.NDMA:
            for i, c in enumerate(self.dcnt[e]):
                if c > 0:
                    snap[("d", e, i)] = c
        for e in self.ENGS:
            for k, v in snap.items():
                if e == "pe" and k == ("c", "pe"):
                    continue
                if self.pend[e].get(k, 0) < v:
                    self.pend[e][k] = v

    def op(self, eng, fn, reads=(), writes=(), dma=False):
        reads = self._bufs(reads)
        writes = self._bufs(writes)
        deps = dict(self.pend[eng])
        self.pend[eng] = {}

        def add(ev):
            if ev is None:
                return
            k, v = ev
            if eng == "pe" and k == ("c", "pe"):
                return
            if deps.get(k, 0) < v:
                deps[k] = v

        for b in reads:
            add(b.lw)
        for b in writes:
            add(b.lw)
            for r in b.rd:
                add(r)
        if dma:
            i = self.di[eng] % len(self.dsem[eng])
            self.di[eng] += 1
            if self.dcnt[eng][i] > 0:
                add((("d", eng, i), self.dcnt[eng][i]))
            self.dcnt[eng][i] += 16
            ev = (("d", eng, i), self.dcnt[eng][i])
        else:
            self.cnt[eng] += 1
            ev = (("c", eng), self.cnt[eng])
        waits = []
        seen = self.seen[eng]
        for k, v in deps.items():
            if seen.get(k, 0) < v:
                seen[k] = v
                waits.append((k, v))
        self.q[eng].append((waits, fn, ev))
        for b in reads:
            b.rd.append(ev)
            if len(b.rd) > 64:
                m = {}
                for k, v in b.rd:
                    if m.get(k, 0) < v:
                        m[k] = v
                b.rd = list(m.items())
        for b in writes:
            b.lw = ev
            b.rd = []
        self.n_ins += 1
        return ev

    def _semof(self, k):
        if k[0] == "c":
            return self.sem[k[1]]
        return self.dsem[k[1]][k[2]]

    def emit(self):
        nc = self.nc
        fin = []
        for e in self.NDMA:
            for i, c in enumerate(self.dcnt[e]):
                if c > 0:
                    fin.append((("d", e, i), c))
        for e in self.ENGS:
            if e != "sp" and self.cnt[e] > 0:
                fin.append((("c", e), self.cnt[e]))
        engobj = {"pe": "tensor", "act": "scalar", "dve": "vector", "pool": "gpsimd", "sp": "sync"}
        with nc.Block() as block:
            for e in self.ENGS:
                def body(engine, e=e):
                    for waits, fn, ev in self.q[e]:
                        for k, v in waits:
                            engine.wait_ge(self._semof(k), v)
                        ins = fn(engine)
                        ins.then_inc(self._semof(ev[0]), 16 if ev[0][0] == "d" else 1)
                    if e == "sp":
                        for k, v in fin:
                            engine.wait_ge(self._semof(k), v)
                getattr(block, engobj[e])(body)

    def dma(self, out, in_, reads=(), writes=(), eng="sp", **kw):
        return self.op(eng, lambda E: E.dma_start(out=out, in_=in_, **kw), reads, writes, dma=True)

    def mm(self, out, lhsT, rhs, start, stop, reads=(), writes=()):
        return self.op("pe", lambda E: E.matmul(out, lhsT, rhs, start=start, stop=stop), reads, writes)

    def tr(self, out, in_, ident, reads=(), writes=()):
        return self.op("pe", lambda E: E.transpose(out, in_, ident), reads, writes)

    def act(self, out, in_, func, reads=(), writes=(), **kw):
        return self.op("act", lambda E: E.activation(out=out, in_=in_, func=func, **kw), reads, writes)

    def tt(self, eng, out, in0, in1, op, reads=(), writes=()):
        return self.op(eng, lambda E: E.tensor_tensor(out, in0, in1, op), reads, writes)

    def ts(self, eng, out, in0, s1, s2, op0, op1=None, reads=(), writes=(), **kw):
        if op1 is None:
            return self.op(eng, lambda E: E.tensor_scalar(out, in0, s1, s2, op0, **kw), reads, writes)
        return self.op(eng, lambda E: E.tensor_scalar(out, in0, s1, s2, op0, op1, **kw), reads, writes)

    def stt(self, eng, out, in0, scalar, in1, op0, op1, reads=(), writes=()):
        return self.op(eng, lambda E: E.scalar_tensor_tensor(out, in0, scalar, in1, op0, op1), reads, writes)

    def cp(self, eng, out, in_, reads=(), writes=()):
        if eng == "act":
            return self.op(eng, lambda E: E.copy(out, in_), reads, writes)
        return self.op(eng, lambda E: E.tensor_copy(out, in_), reads, writes)


class Ctx:
    pass


def build_nc(stage):
    nc = bass.Bass("TRN2", target_bir_lowering=False)
    C = Ctx()
    C.nc = nc
    dt_in = lambda n, s, d=F32: nc.dram_tensor(n, list(s), d, kind="ExternalInput").ap()
    dt_out = lambda n, s, d=F32: nc.dram_tensor(n, list(s), d, kind="ExternalOutput").ap()
    dt_tmp = lambda n, s, d=F32: nc.dram_tensor(n, list(s), d, kind="Internal").ap()
    C.xin = dt_in("xin", [NTOK, D])
    C.cak = dt_in("cak", [2, PAST, 16 * 128])
    C.cav = dt_in("cav", [2, PAST, 16 * 128])
    C.cbk = dt_in("cbk", [PAST, 4 * 128])
    C.cbv = dt_in("cbv", [PAST, 4 * 128])
    C.cbi = dt_in("cbi", [PAST, 64])
    C.swkv = dt_in("swkv", [32 * 64, 64])
    C.sshift = dt_in("sshift", [1, D])
    C.norm_g = dt_in("norm_g", [DEPTH, D])
    C.final_g = dt_in("final_g", [1, D])
    C.w_out = dt_in("w_out", [DEPTH, D, D])
    C.a_w_in = dt_in("a_w_in", [2, D, 4 * D])
    C.a_lam = dt_in("a_lam", [2, 4 * 64])
    C.a_subln_g = dt_in("a_subln_g", [2, 128])
    C.b_w_in = dt_in("b_w_in", [D, 6224])
    C.c_mu = dt_in("c_mu", [6, D])
    C.c_w_rkvg = dt_in("c_w_rkvg", [4, D, D])
    C.c_w0 = dt_in("c_w0", [1, D])
    C.c_w_la = dt_in("c_w_la", [D, 96])
    C.c_w_lb = dt_in("c_w_lb", [96, D])
    C.c_a0 = dt_in("c_a0", [1, D])
    C.c_a_la = dt_in("c_a_la", [D, 96])
    C.c_a_lb = dt_in("c_a_lb", [96, D])
    C.c_k_k = dt_in("c_k_k", [1, D])
    C.c_k_a = dt_in("c_k_a", [1, D])
    C.c_r_k = dt_in("c_r_k", [1, D])
    C.c_ln_w = dt_in("c_ln_w", [1, D])
    C.c_ln_b = dt_in("c_ln_b", [1, D])
    C.rope64 = dt_in("rope64", [NTOK, 2, 64])
    C.rope128 = dt_in("rope128", [NTOK, 2, 128])
    C.ident_d = dt_in("ident", [128, 128])
    C.y = dt_out("y", [NTOK, D])
    C.o_ak = dt_out("o_ak", [2, NTOK, D])
    C.o_av = dt_out("o_av", [2, NTOK, D])
    C.o_bk = dt_out("o_bk", [NTOK, 512])
    C.o_bv = dt_out("o_bv", [NTOK, 512])
    C.o_bi = dt_out("o_bi", [NTOK, 64])
    C.o_cw = dt_out("o_cw", [2, 32 * 64, 64])
    C.o_cs = dt_out("o_cs", [2, D])
    C.xs = dt_tmp("xs", [NTOK, D])
    C.og = dt_tmp("og", [NTOK, D], BF16)
    C.qid = dt_tmp("qid", [NTOK, 1024], BF16)
    C.qd = dt_tmp("qd", [NTOK, D], BF16)
    C.sgd = dt_tmp("sgd", [NTOK, D], BF16)
    C.maskd = dt_tmp("maskd", [NTOK, 4160], BF16)
    for nm in ("c_r", "c_k", "c_v", "c_sg", "c_bv"):
        setattr(C, nm, dt_tmp(nm, [NTOK, D]))
    for nm in ("c_At", "c_Bt", "c_Kt", "c_Rt"):
        setattr(C, nm, dt_tmp(nm, [NTOK, D], BF16))
    C.cmask = dt_in("cmask", [5, 128, 128])
    C.selc = dt_in("selc", [128, 2])

    with ExitStack() as st:
        P = Prog(nc, st)
        C.P = P
        C.ident = P.sb([128, 128], F32, name="identf")
        C.identb = P.sb([128, 128], BF16, name="identb")
        P.dma(C.ident[:], C.ident_d[:, :], writes=[C.ident])
        P.cp("dve", C.identb[:], C.ident[:], [C.ident], [C.identb])
        C.eps6 = P.sb([128, 1], F32, name="eps6")
        P.op("dve", lambda E: E.memset(C.eps6[:], 1e-6), [], [C.eps6])
        C.eps5 = P.sb([128, 1], F32, name="eps5")
        P.op("dve", lambda E: E.memset(C.eps5[:], 1e-5), [], [C.eps5])
        C.pp = [P.ps([128, 512], F32, name="pp%d" % i) for i in range(2)]
        C.ptb = P.ps([128, 1024], BF16, name="ptb")
        C.psc = [P.ps([128, 512], F32, name="psc%d" % i) for i in range(2)]
        C.po = [P.ps([128, 512], F32, name="po%d" % i) for i in range(2)]
        C.ptf = P.ps([128, 512], F32, name="ptf")

        x_src = C.xin
        for li in range(DEPTH):
            if li >= stage:
                break
            kind, j = li % 3, li // 3
            with ExitStack() as cst:
                if kind == 2:
                    C.tT = P.sb([128, NTOK], BF16, stack=cst, name='c_tT')
                    C.aT = P.sb([128, NTOK], BF16, stack=cst, name='c_aT')
                    C.gC = P.sb([128, 16, 34], F32, stack=cst, name='c_gC')
                done = False
                with ExitStack() as hs:
                    C.hT = P.sb([128, 16, NTOK], BF16, stack=hs, name="hT%d" % li)
                    with ExitStack() as lst:
                        C.lst = lst
                        phase_norm(C, x_src, C.norm_g[li:li + 1, :], li, want_last=(kind == 2))
                        P.fence()
                    with ExitStack() as lst:
                        C.lst = lst
                        if kind == 0:
                            layer_a(C, li, j)
                            done = True
                        elif kind == 1:
                            done = layer_b(C, li)
                        else:
                            done = layer_c1(C, li)
                        P.fence()
                if kind == 2 and done:
                    with ExitStack() as lst:
                        C.lst = lst
                        layer_c2(C, li)
                        P.fence()
                    with ExitStack() as lst:
                        C.lst = lst
                        layer_c3(C, li)
                        P.fence()
            if not done:
                continue
            with ExitStack() as lst:
                C.lst = lst
                phase_out(C, x_src, li)
                P.fence()
            x_src = C.xs
        with ExitStack() as lst:
            C.lst = lst
            phase_final(C, x_src)
        P.emit()
    return nc


def phase_norm(C, x_src, g_row, li, want_last=False):
    P = C.P
    lst = C.lst
    C.gb = P.sb([128, D], F32, stack=lst)
    P.dma(C.gb[:], g_row.partition_broadcast(128), writes=[C.gb])
    xb = [P.sb([128, D], F32, stack=lst) for _ in range(2)]
    junk = P.sb([128, D], BF16, stack=lst)
    hb = [P.sb([128, D], BF16, stack=lst) for _ in range(2)]
    ss = [P.sb([128, 1], F32, stack=lst) for _ in range(2)]
    rs = [P.sb([128, 1], F32, stack=lst) for _ in range(2)]
    for t in range(NT):
        n = tn(t)
        xt, h, s, r = xb[t % 2], hb[t % 2], ss[t % 2], rs[t % 2]
        P.dma(xt[:n], x_src[t * 128:t * 128 + n, :], writes=[xt])
        P.act(junk[:n], xt[:n], AF.Square, [xt], [junk, s], accum_out=s[:n])
        P.act(r[:n], s[:n], AF.Sqrt, [s, C.eps6], [r], bias=C.eps6[:n], scale=1.0 / D)
        P.op("dve", lambda E, r=r, n=n: E.reciprocal(r[:n], r[:n]), [r], [r])
        P.stt("dve", h[:n], xt[:n], r[:n, 0:1], C.gb[:n], ALU.mult, ALU.mult, [xt, r, C.gb], [h])
        if want_last and t >= 15:
            hl = P.sb([128, D], F32, stack=lst)
            P.stt("dve", hl[:n], xt[:n], r[:n, 0:1], C.gb[:n], ALU.mult, ALU.mult, [xt, r, C.gb], [hl])
            P.dma(C.o_cs[t - 15:t - 14, :], hl[n - 1:n, :], reads=[hl])
        for gi in range(2):
            for jj in range(8):
                c = gi * 8 + jj
                P.tr(C.ptb[:, jj * 128:jj * 128 + n], h[:n, c * 128:(c + 1) * 128], C.identb[:n, :n],
                     [h, C.identb], [C.ptb])
            src = C.ptb[:].rearrange("p (a b) -> p a b", a=8)[:, :, :n]
            P.cp("act" if gi == 0 else "dve", C.hT[:, gi * 8:(gi + 1) * 8, t * 128:t * 128 + n], src, [C.ptb], [C.hT])


def load_w_block(C, wf_bufs, wb, pieces, ncols, ctr):
    P = C.P
    for qtr in range(4):
        wf = wf_bufs[(ctr[0]) % 2]
        for ap, off in pieces:
            w = ap.shape[1]
            P.dma(wf[:, :, off:off + w], ap[qtr * 512:(qtr + 1) * 512, :].rearrange("(c p) n -> p c n", p=128),
                  writes=[wf])
        eng = ("dve", "act", "dve")[ctr[0] % 3]
        P.cp(eng, wb[:, qtr * 4:(qtr + 1) * 4, :ncols], wf[:, :, :ncols], [wf], [wb])
        ctr[0] += 1


def layer_a(C, li, j):
    P = C.P
    lst = C.lst
    lam_init = 0.8 - 0.6 * math.exp(-0.3 * li)
    rope = P.sb([128, NT, 2, 64], F32, stack=lst, name="rope_a%d" % li)
    P.dma(rope[:, 0:16], C.rope64[0:2048].rearrange("(t p) a b -> p t a b", p=128), writes=[rope])
    P.dma(rope[:64, 16], C.rope64[2048:2112], writes=[rope])
    lamt = P.sb([128, 4, 64], F32, stack=lst)
    P.dma(lamt[:].rearrange("p a b -> p (a b)"), C.a_lam[j:j + 1, :].partition_broadcast(128), writes=[lamt])
    lprod = P.sb([128, 2, 64], F32, stack=lst)
    lsum = P.sb([128, 2], F32, stack=lst)
    neglam = P.sb([128, 1], F32, stack=lst)
    lv = lamt[:].rearrange("p (a b) c -> p a b c", b=2)
    P.tt("dve", lprod[:], lv[:, :, 0, :], lv[:, :, 1, :], ALU.mult, [lamt], [lprod])
    P.op("dve", lambda E: E.reduce_sum(lsum[:], lprod[:], AX.X), [lprod], [lsum])
    P.act(lsum[:], lsum[:], AF.Exp, [lsum], [lsum])
    P.tt("dve", neglam[:], lsum[:, 1:2], lsum[:, 0:1], ALU.subtract, [lsum], [neglam])
    P.ts("dve", neglam[:], neglam[:], -lam_init, None, ALU.add, reads=[neglam], writes=[neglam])
    gsub = P.sb([128, 128], F32, stack=lst)
    P.dma(gsub[:], C.a_subln_g[j:j + 1, :].partition_broadcast(128), writes=[gsub])
    P.ts("dve", gsub[:], gsub[:], 1.0 - lam_init, None, ALU.mult, reads=[gsub], writes=[gsub])

    wf_bufs = [P.sb([128, 4, 512], F32, stack=lst) for _ in range(2)]
    wbs = [P.sb([128, 16, 512], BF16, stack=lst) for _ in range(2)]
    qkT = [P.sb([128, 2, NTOK], BF16, stack=lst) for _ in range(2)]
    vaug = [P.sb([128, NT, 132], BF16, stack=lst) for _ in range(2)]
    sg = [P.sb([128, NT, 128], BF16, stack=lst) for _ in range(2)]
    for v in vaug:
        P.op("pool", lambda E, v=v: E.memset(v[:, :, 128:129], 1.0), [], [v])
    ra = [P.sb([128, 256], F32, stack=lst) for _ in range(2)]
    rt = [P.sb([128, 256], F32, stack=lst) for _ in range(2)]
    kv = [P.sb([128, 2, 128], F32, stack=lst) for _ in range(2)]
    qkb = [P.sb([128, 256], BF16, stack=lst) for _ in range(2)]
    PT = [P.sb([128, 2048], BF16, stack=lst) for _ in range(2)]
    ot = [P.sb([128, 128], F32, stack=lst) for _ in range(2)]
    ogt = [P.sb([128, 128], BF16, stack=lst) for _ in range(2)]
    rr = [P.sb([128, 2], F32, stack=lst) for _ in range(2)]
    s2 = [P.sb([128, 1], F32, stack=lst) for _ in range(2)]
    junk = P.sb([128, 128], F32, stack=lst)
    sge = [P.sb([128, 128], F32, stack=lst) for _ in range(2)]
    kcs = [P.sb([128, 8, 128], F32, stack=lst) for _ in range(1)]
    vcs = [P.sb([128, 8, 128], F32, stack=lst) for _ in range(1)]
    kTc = P.sb([128, PAST], BF16, stack=lst)
    vca = P.sb([128, 32, 132], BF16, stack=lst)
    P.op("pool", lambda E: E.memset(vca[:, :, 128:129], 1.0), [], [vca])
    PTs = [P.sb([128, 512], BF16, stack=lst) for _ in range(2)]

    ctr = [0]
    cnt = [0]
    W = C.a_w_in[j]
    scale = 64 ** -0.5

    def epilogue(h, t, n, po, k):
        o, og_, r_, s_ = ot[k % 2], ogt[k % 2], rr[k % 2], s2[k % 2]
        pov = po[:].rearrange("p (a b) -> p a b", a=2)
        P.op("dve", lambda E: E.reciprocal(r_[:n], pov[:n, :, 128]), [po], [r_])
        P.tt("dve", r_[:n, 1:2], r_[:n, 1:2], neglam[:n], ALU.mult, [r_, neglam], [r_])
        P.ts("dve", o[:n], po[:n, 0:128], r_[:n, 0:1], None, ALU.mult, reads=[po, r_], writes=[o])
        P.stt("dve", o[:n], po[:n, 256:384], r_[:n, 1:2], o[:n], ALU.mult, ALU.add, [po, r_, o], [o])
        P.tt("dve", junk[:n], o[:n], o[:n], ALU.mult, [o], [junk])
        P.op("dve", lambda E: E.reduce_sum(s_[:n], junk[:n], AX.X), [junk], [s_])
        P.act(s_[:n], s_[:n], AF.Ln, [s_, C.eps5], [s_], bias=C.eps5[:n], scale=1.0 / 128)
        P.act(s_[:n], s_[:n], AF.Exp, [s_], [s_], scale=-0.5)
        P.stt("dve", o[:n], o[:n], s_[:n, 0:1], gsub[:n], ALU.mult, ALU.mult, [o, s_, gsub], [o])
        P.tt("dve", og_[:n], o[:n], sg[h % 2][:n, t, :], ALU.mult, [o, sg[h % 2]], [og_])
        P.dma(C.og[t * 128:t * 128 + n, h * 128:(h + 1) * 128], og_[:n], reads=[og_])

    NH = int(os.environ.get('MK_HEADS', '16'))
    NOATT = int(os.environ.get('MK_NOATT', '0'))
    NOSAMP = int(os.environ.get('MK_NOSAMP', '0'))
    loaded = set()

    def ensure_loaded(h):
        if h in loaded or h >= NH:
            return
        loaded.add(h)
        pieces = [(W[:, blk * D + h * 128: blk * D + (h + 1) * 128], blk * 128) for blk in range(4)]
        load_w_block(C, wf_bufs, wbs[h % 2], pieces, 512, ctr)

    def gen_proj(h):
        wb = wbs[h % 2]
        ensure_loaded(h)
        QK, V, SG = qkT[h % 2], vaug[h % 2], sg[h % 2]
        for t in range(NT):
            n = tn(t)
            ps = C.pp[t % 2]
            for c in range(16):
                P.mm(ps[:n, :], C.hT[:, c, t * 128:t * 128 + n], wb[:, c, :], c == 0, c == 15, [C.hT, wb], [ps])
            k = cntp[0]
            cntp[0] += 1
            a_, t_, kv_, qb = ra[k % 2], rt[k % 2], kv[k % 2], qkb[k % 2]
            xv = ps[:n, 0:256].rearrange("p (a b c) -> p a b c", a=4, b=2)
            cosb = rope[:n, t, 0, :].rearrange("p (b c) -> p b c", b=2).unsqueeze(1).to_broadcast([n, 4, 2, 32])
            sinb = rope[:n, t, 1, :].rearrange("p (b c) -> p b c", b=2).unsqueeze(1).to_broadcast([n, 4, 2, 32])
            av = a_[:n].rearrange("p (a b c) -> p a b c", a=4, b=2)
            tv = t_[:n].rearrange("p (a b c) -> p a b c", a=4, b=2)
            P.tt("dve", av, xv, cosb, ALU.mult, [ps, rope], [a_])
            P.tt("dve", tv[:, :, 0, :], xv[:, :, 1, :], sinb[:, :, 0, :], ALU.mult, [ps, rope], [t_])
            P.tt("dve", tv[:, :, 1, :], xv[:, :, 0, :], sinb[:, :, 1, :], ALU.mult, [ps, rope], [t_])
            P.tt("dve", qb[:n, 0:128], a_[:n, 0:128], t_[:n, 0:128], ALU.add, [a_, t_], [qb])
            P.tt("dve", kv_[:n, 0, :], a_[:n, 128:256], t_[:n, 128:256], ALU.add, [a_, t_], [kv_])
            P.cp("dve", qb[:n, 128:256], kv_[:n, 0, :], [kv_], [qb])
            P.cp("act", kv_[:n, 1, :], ps[:n, 256:384], [ps], [kv_])
            P.cp("act", V[:n, t, 0:128], ps[:n, 256:384], [ps], [V])
            e_ = sge[k % 2]
            P.act(e_[:n], ps[:n, 384:512], AF.Exp, [ps], [e_], scale=-1.0)
            P.ts("dve", e_[:n], e_[:n], 1.0, None, ALU.add, reads=[e_], writes=[e_])
            P.op("dve", lambda E, e_=e_, n=n: E.reciprocal(e_[:n], e_[:n]), [e_], [e_])
            P.tt("dve", SG[:n, t, :], ps[:n, 384:512], e_[:n], ALU.mult, [ps, e_], [SG])
            P.dma(C.o_ak[j, t * 128:t * 128 + n, h * 128:(h + 1) * 128], kv_[:n, 0, :], reads=[kv_])
            P.dma(C.o_av[j, t * 128:t * 128 + n, h * 128:(h + 1) * 128], kv_[:n, 1, :], reads=[kv_])
            for m in range(2):
                P.tr(C.ptb[:, m * 128:m * 128 + n], qb[:n, m * 128:(m + 1) * 128], C.identb[:n, :n],
                     [qb, C.identb], [C.ptb])
            P.cp("act", QK[:, :, t * 128:t * 128 + n],
                 C.ptb[:, 0:256].rearrange("p (a b) -> p a b", a=2)[:, :, :n], [C.ptb], [QK])
            yield
            if t == 1:
                ensure_loaded(h + 1)

    def gen_att(h):
        QK, V, SG = qkT[h % 2], vaug[h % 2], sg[h % 2]
        for i in range(0 if not NOATT else 16, 16):
            k = cnt[0]
            cnt[0] += 1
            po = C.po[k % 2]
            for m in range(2):
                pt = PT[m]
                nb = i + 1
                for g0 in range(0, nb, 4):
                    g1 = min(nb, g0 + 4)
                    sc = C.psc[(g0 // 4 + m) % 2]
                    for jb in range(g0, g1):
                        P.mm(sc[:, (jb - g0) * 128:(jb - g0 + 1) * 128],
                             QK[m * 64:(m + 1) * 64, 1, jb * 128:(jb + 1) * 128],
                             QK[m * 64:(m + 1) * 64, 0, i * 128:(i + 1) * 128], True, True, [QK], [sc])
                    P.act(pt[:, g0 * 128:g1 * 128], sc[:, 0:(g1 - g0) * 128], AF.Exp, [sc], [pt], scale=scale)
                P.op("pool", lambda E, pt=pt, i=i: E.memset(pt[64:128, i * 128:i * 128 + 64], 0.0), [], [pt])
                for jb in range(nb):
                    P.mm(po[:, m * 256:m * 256 + 129], pt[:, jb * 128:(jb + 1) * 128], V[:, jb, 0:129],
                         jb == 0, jb == nb - 1, [pt, V], [po])
                yield
            epilogue(h, i, 128, po, k)
            yield
        if NOSAMP:
            return
        for half in range(4):
            yield
            kc, vc = kcs[0], vcs[0]
            P.dma(kc[:], C.cak[j, half * 1024:(half + 1) * 1024, h * 128:(h + 1) * 128].rearrange(
                "(b p) d -> p b d", p=128), writes=[kc])
            P.dma(vc[:], C.cav[j, half * 1024:(half + 1) * 1024, h * 128:(h + 1) * 128].rearrange(
                "(b p) d -> p b d", p=128), writes=[vc])
            P.cp("pool", vca[:, half * 8:(half + 1) * 8, 0:128], vc[:], [vc], [vca])
            for g in range(2):
                for b in range(4):
                    P.tr(C.ptf[:, b * 128:(b + 1) * 128], kc[:, g * 4 + b, :], C.ident[:], [kc, C.ident], [C.ptf])
                P.cp("dve" if g % 2 else "act", kTc[:, half * 1024 + g * 512: half * 1024 + (g + 1) * 512], C.ptf[:],
                     [C.ptf], [kTc])
        k = cnt[0]
        cnt[0] += 1
        po = C.po[k % 2]
        for m in range(2):
            qs = QK[m * 64:(m + 1) * 64, 0, 2048:2112]
            for g in range(5):
                sc = C.psc[(g + m) % 2]
                pts = PTs[(g + m) % 2]
                nblk = 8 if g < 4 else 1
                for b in range(nblk):
                    jb = g * 8 + b
                    if jb < 32:
                        P.mm(sc[:, b * 64:(b + 1) * 64], kTc[m * 64:(m + 1) * 64, jb * 128:(jb + 1) * 128], qs,
                             True, True, [kTc, QK], [sc])
                    else:
                        P.mm(sc[:64, b * 64:(b + 1) * 64], QK[m * 64:(m + 1) * 64, 1, 2048:2112], qs,
                             True, True, [QK], [sc])
                rows = 128 if g < 4 else 64
                P.act(pts[:rows, 0:nblk * 64], sc[:rows, 0:nblk * 64], AF.Exp, [sc], [pts], scale=scale)
                for b in range(nblk):
                    jb = g * 8 + b
                    if jb < 32:
                        P.mm(po[:64, m * 256:m * 256 + 129], pts[:, b * 64:(b + 1) * 64], vca[:, jb, 0:129],
                             jb == 0, False, [pts, vca], [po])
                    else:
                        P.mm(po[:64, m * 256:m * 256 + 129], pts[:64, b * 64:(b + 1) * 64], V[:64, 16, 0:129],
                             False, True, [pts, V], [po])
        epilogue(h, 16, 64, po, k)


    def interleave(gens):
        gens = list(gens)
        while gens:
            for g_ in list(gens):
                try:
                    next(g_)
                except StopIteration:
                    gens.remove(g_)

    cntp = [0]
    interleave([gen_proj(0)])
    for h in range(NH):
        gs = [gen_att(h)]
        if h + 1 < NH:
            gs.append(gen_proj(h + 1))
        interleave(gs)


def rope_ops(C, n, xin, H, half, rope, t, a_, t_, rd):
    P = C.P
    Wd = H * 2 * half
    xv = xin.rearrange("p (a b c) -> p a b c", a=H, b=2)
    cosb = rope[:n, t, 0, :].rearrange("p (b c) -> p b c", b=2).unsqueeze(1).to_broadcast([n, H, 2, half])
    sinb = rope[:n, t, 1, :].rearrange("p (b c) -> p b c", b=2).unsqueeze(1).to_broadcast([n, H, 2, half])
    av = a_[:n, :Wd].rearrange("p (a b c) -> p a b c", a=H, b=2)
    tv = t_[:n, :Wd].rearrange("p (a b c) -> p a b c", a=H, b=2)
    P.tt("dve", av, xv, cosb, ALU.mult, rd + [rope], [a_])
    P.tt("dve", tv[:, :, 0, :], xv[:, :, 1, :], sinb[:, :, 0, :], ALU.mult, rd + [rope], [t_])
    P.tt("dve", tv[:, :, 1, :], xv[:, :, 0, :], sinb[:, :, 1, :], ALU.mult, rd + [rope], [t_])


def layer_b(C, li):
    P = C.P
    W = C.b_w_in
    OQ, OK_, OV, OQI, OKI, OG = 0, 2048, 2560, 3072, 4096, 4176
    NEG = -1.0e30
    with ExitStack() as bst:
        kT = P.sb([128, 4, NTOK], BF16, stack=bst, name="b_kT")
        vaug = P.sb([128, NT, 4, 132], BF16, stack=bst, name="b_vaug")
        kiT2 = P.sb([128, NTOK], BF16, stack=bst, name="b_kiT2")
        wis = P.sb([128, NT, 16], F32, stack=bst, name="b_wis")
        for g in range(4):
            P.op("pool", lambda E, g=g: E.memset(vaug[:, :, g, 128:129], 1.0), [], [vaug])
        with ExitStack() as lst:
            rope128 = P.sb([128, NT, 2, 128], F32, stack=lst)
            P.dma(rope128[:, 0:16], C.rope128[0:2048].rearrange("(t p) a b -> p t a b", p=128), writes=[rope128])
            P.dma(rope128[:64, 16], C.rope128[2048:2112], writes=[rope128])
            rope64 = P.sb([128, NT, 2, 64], F32, stack=lst)
            P.dma(rope64[:, 0:16], C.rope64[0:2048].rearrange("(t p) a b -> p t a b", p=128), writes=[rope64])
            P.dma(rope64[:64, 16], C.rope64[2048:2112], writes=[rope64])
            wf_bufs = [P.sb([128, 4, 512], F32, stack=lst) for _ in range(2)]
            wbs = [P.sb([128, 16, 512], BF16, stack=lst) for _ in range(2)]
            ra = [P.sb([128, 512], F32, stack=lst) for _ in range(2)]
            rt = [P.sb([128, 512], F32, stack=lst) for _ in range(2)]
            of = [P.sb([128, 512], F32, stack=lst) for _ in range(2)]
            ob = [P.sb([128, 512], BF16, stack=lst) for _ in range(2)]
            ctr = [0]
            blocks = [("k", OK_, 512), ("v", OV, 512), ("ki", OKI, 80), ("qi", OQI, 512), ("qi", OQI + 512, 512)]
            blocks += [("q", OQ + g * 512, 512) for g in range(4)] + [("g", OG + g * 512, 512) for g in range(4)]
            kk = 0
            blocks = blocks[:int(os.environ.get('MK_BK', '99'))]
            for bi, (kind, off, ncols) in enumerate(blocks):
                wb = wbs[bi % 2]
                load_w_block(C, wf_bufs, wb, [(W[:, off:off + ncols], 0)], ncols, ctr)
                for t in range(NT):
                    n = tn(t)
                    r0 = t * 128
                    ps = C.pp[t % 2]
                    for c in range(16):
                        P.mm(ps[:n, :ncols], C.hT[:, c, r0:r0 + n], wb[:, c, :ncols], c == 0, c == 15, [C.hT, wb], [ps])
                    a_, t_, f_, b_ = ra[kk % 2], rt[kk % 2], of[kk % 2], ob[kk % 2]
                    kk += 1
                    if kind == "k":
                        rope_ops(C, n, ps[:n, 0:512], 4, 64, rope128, t, a_, t_, [ps])
                        P.tt("dve", f_[:n], a_[:n], t_[:n], ALU.add, [a_, t_], [f_])
                        P.dma(C.o_bk[r0:r0 + n, :], f_[:n], reads=[f_])
                        P.cp("act", b_[:n], f_[:n], [f_], [b_])
                        for g in range(4):
                            P.tr(C.ptb[:, g * 128:g * 128 + n], b_[:n, g * 128:(g + 1) * 128], C.identb[:n, :n],
                                 [b_, C.identb], [C.ptb])
                        P.cp("act", kT[:, :, r0:r0 + n], C.ptb[:, 0:512].rearrange("p (a b) -> p a b", a=4)[:, :, :n],
                             [C.ptb], [kT])
                    elif kind == "v":
                        P.cp("act", f_[:n], ps[:n, :], [ps], [f_])
                        P.dma(C.o_bv[r0:r0 + n, :], f_[:n], reads=[f_])
                        for g in range(4):
                            P.cp("dve", vaug[:n, t, g, 0:128], f_[:n, g * 128:(g + 1) * 128], [f_], [vaug])
                    elif kind == "ki":
                        rope_ops(C, n, ps[:n, 0:64], 1, 32, rope64, t, a_, t_, [ps])
                        P.tt("dve", f_[:n, 0:64], a_[:n, 0:64], t_[:n, 0:64], ALU.add, [a_, t_], [f_])
                        P.dma(C.o_bi[r0:r0 + n, :], f_[:n, 0:64], reads=[f_])
                        P.cp("act", b_[:n, 0:64], f_[:n, 0:64], [f_], [b_])
                        P.cp("act", b_[:n, 64:128], f_[:n, 0:64], [f_], [b_])
                        P.tr(C.ptb[:, 0:n], b_[:n, 0:128], C.identb[:n, :n], [b_, C.identb], [C.ptb])
                        P.cp("act", kiT2[:, r0:r0 + n], C.ptb[:, 0:n], [C.ptb], [kiT2])
                        P.ts("dve", wis[:n, t, :], ps[:n, 64:80], 0.25, None, ALU.mult, reads=[ps], writes=[wis])
                    elif kind == "qi":
                        rope_ops(C, n, ps[:n, 0:512], 8, 32, rope64, t, a_, t_, [ps])
                        P.tt("dve", b_[:n], a_[:n], t_[:n], ALU.add, [a_, t_], [b_])
                        P.dma(C.qid[r0:r0 + n, off - OQI:off - OQI + 512], b_[:n], reads=[b_])
                    elif kind == "q":
                        rope_ops(C, n, ps[:n, 0:512], 4, 64, rope128, t, a_, t_, [ps])
                        P.tt("dve", b_[:n], a_[:n], t_[:n], ALU.add, [a_, t_], [b_])
                        P.dma(C.qd[r0:r0 + n, off - OQ:off - OQ + 512], b_[:n], reads=[b_])
                    else:
                        P.act(b_[:n], ps[:n, :], AF.Silu, [ps], [b_])
                        P.dma(C.sgd[r0:r0 + n, off - OG:off - OG + 512], b_[:n], reads=[b_])
            P.fence()
        if int(os.environ.get('MK_B', '3')) < 2:
            return False
        with ExitStack() as lst:
            acc = P.sb([128, 4160], F32, stack=lst)
            work = P.sb([128, 4160], F32, stack=lst)
            maskb = P.sb([128, 4160], BF16, stack=lst)
            kiS = P.sb([128, 4160], BF16, stack=lst)
            qi_t = [P.sb([128, 1024], BF16, stack=lst) for _ in range(2)]
            qiT = [P.sb([128, 8, 128], BF16, stack=lst) for _ in range(2)]
            rl = [P.sb([128, 512], F32, stack=lst) for _ in range(2)]
            m8 = P.sb([128, 8], F32, stack=lst)
            thr = P.sb([128, 1], F32, stack=lst)
            cst = [P.sb([128, 8, 128], F32, stack=lst) for _ in range(2)]
            for qd_ in range(4):
                cs = cst[qd_ % 2]
                src = C.cbi[qd_ * 1024:(qd_ + 1) * 1024, :].rearrange("(b p) d -> p b d", p=128)
                P.dma(cs[:, :, 0:64], src, writes=[cs])
                P.dma(cs[:, :, 64:128], src, writes=[cs])
                for g in range(2):
                    for b in range(4):
                        P.tr(C.ptf[:, b * 128:(b + 1) * 128], cs[:, g * 4 + b, :], C.ident[:], [cs, C.ident], [C.ptf])
                    P.cp("dve" if g % 2 else "act", kiS[:, qd_ * 1024 + g * 512: qd_ * 1024 + (g + 1) * 512], C.ptf[:],
                         [C.ptf], [kiS])
            P.cp("pool", kiS[:, 4096:4160], kiT2[:, 2048:2112], [kiT2], [kiS])
            for t in range(NT):
                n = tn(t)
                r0 = t * 128
                S = 128 * (t + 1) if t < 16 else 4160
                keys = kiT2 if t < 16 else kiS
                qt, qT_ = qi_t[t % 2], qiT[t % 2]
                P.dma(qt[:n], C.qid[r0:r0 + n, :], writes=[qt])
                for hp in range(8):
                    P.tr(C.ptb[:, hp * 128:hp * 128 + n], qt[:n, hp * 128:(hp + 1) * 128], C.identb[:n, :n],
                         [qt, C.identb], [C.ptb])
                P.cp("act", qT_[:, :, :n], C.ptb[:].rearrange("p (a b) -> p a b", a=8)[:, :, :n], [C.ptb], [qT_])
                kq = 0
                for hd in range(16):
                    hp, par = hd // 2, hd % 2
                    for c0 in range(0, S, 512):
                        w = min(512, S - c0)
                        sc = C.psc[kq % 2]
                        r_ = rl[kq % 2]
                        kq += 1
                        P.mm(sc[:n, :w], qT_[par * 64:(par + 1) * 64, hp, :n], keys[par * 64:(par + 1) * 64, c0:c0 + w],
                             True, True, [qT_, keys], [sc])
                        P.act(r_[:n, :w], sc[:n, :w], AF.Relu, [sc], [r_], scale=0.125)
                        if hd == 0:
                            P.ts("dve", acc[:n, c0:c0 + w], r_[:n, :w], wis[:n, t, 0:1], None, ALU.mult,
                                 reads=[r_, wis], writes=[acc])
                        else:
                            P.stt("dve", acc[:n, c0:c0 + w], r_[:n, :w], wis[:n, t, hd:hd + 1], acc[:n, c0:c0 + w],
                                  ALU.mult, ALU.add, [r_, wis, acc], [acc])
                if t < 16:
                    P.op("dve", lambda E, t=t: E.memset(acc[0:64, t * 128 + 64:(t + 1) * 128], NEG), [], [acc])
                if t >= 2:
                    for rnd in range(32):
                        srcw = acc if rnd == 0 else work
                        P.op("dve", lambda E, srcw=srcw, n=n, S=S: E.max(out=m8[:n], in_=srcw[:n, :S]), [srcw], [m8])
                        if rnd < 31:
                            P.op("dve", lambda E, srcw=srcw, n=n, S=S: E.match_replace(
                                out=work[:n, :S], in_to_replace=m8[:n], in_values=srcw[:n, :S], imm_value=NEG),
                                [srcw, m8], [work])
                    P.cp("dve", thr[:n], m8[:n, 7:8], [m8], [thr])
                else:
                    P.op("dve", lambda E, n=n: E.memset(thr[:n], -1.0e29), [], [thr])
                P.ts("dve", maskb[:n, :S], acc[:n, :S], thr[:n, 0:1], None, ALU.is_ge, reads=[acc, thr], writes=[maskb])
                P.dma(C.maskd[r0:r0 + n, 0:S], maskb[:n, :S], reads=[maskb])
            P.fence()
        if int(os.environ.get('MK_B', '3')) < 3:
            return False
        with ExitStack() as lst:
            q_t = [P.sb([128, D], BF16, stack=lst) for _ in range(2)]
            sg_t = [P.sb([128, D], BF16, stack=lst) for _ in range(2)]
            mk = [P.sb([128, 4160], BF16, stack=lst) for _ in range(1)]
            maskT = P.sb([128, 2112], BF16, stack=lst)
            qT = P.sb([128, 16, 128], BF16, stack=lst)
            PTt = P.sb([128, 8448], BF16, stack=lst)
            kTc = P.sb([128, PAST], BF16, stack=lst)
            vca = P.sb([128, 32, 132], BF16, stack=lst)
            P.op("pool", lambda E: E.memset(vca[:, :, 128:129], 1.0), [], [vca])
            kcs = [P.sb([128, 8, 128], F32, stack=lst) for _ in range(2)]
            vcs = [P.sb([128, 8, 128], F32, stack=lst) for _ in range(2)]
            rr = P.sb([128, 4], F32, stack=lst)
            of = P.sb([128, 2, 128], F32, stack=lst)
            ogt = [P.sb([128, D], BF16, stack=lst) for _ in range(2)]
            scale = 128 ** -0.5
            kq = 0
            for t in range(NT):
                n = tn(t)
                r0 = t * 128
                S = 128 * (t + 1) if t < 16 else 4160
                nb = (S + 127) // 128
                q_, s_, m_, og_ = q_t[t % 2], sg_t[t % 2], mk[0], ogt[t % 2]
                P.dma(q_[:n], C.qd[r0:r0 + n, :], writes=[q_])
                P.dma(s_[:n], C.sgd[r0:r0 + n, :], writes=[s_])
                P.dma(m_[:n, :S], C.maskd[r0:r0 + n, 0:S], writes=[m_])
                for j0 in range(0, nb, 8):
                    j1 = min(nb, j0 + 8)
                    rmax = 0
                    for jb in range(j0, j1):
                        rows = min(128, S - jb * 128)
                        rmax = max(rmax, rows)
                        P.tr(C.ptb[:rows, (jb - j0) * 128:(jb - j0) * 128 + n], m_[:n, jb * 128:jb * 128 + rows],
                             C.identb[:n, :n], [m_, C.identb], [C.ptb])
                    full = [jb for jb in range(j0, j1) if min(128, S - jb * 128) == 128]
                    if full:
                        P.cp("act", maskT[:, j0 * n:(j0 + len(full)) * n].rearrange("p (a b) -> p a b", b=n),
                             C.ptb[:].rearrange("p (a b) -> p a b", a=8)[:, 0:len(full), :n], [C.ptb], [maskT])
                    if len(full) < j1 - j0:
                        jb = j1 - 1
                        P.cp("act", maskT[:64, jb * n:(jb + 1) * n], C.ptb[:64, (jb - j0) * 128:(jb - j0) * 128 + n],
                             [C.ptb], [maskT])
                for hp in range(2):
                    for hh in range(8):
                        hd = hp * 8 + hh
                        P.tr(C.ptb[:, hh * 128:hh * 128 + n], q_[:n, hd * 128:(hd + 1) * 128], C.identb[:n, :n],
                             [q_, C.identb], [C.ptb])
                    P.cp("dve", qT[:, hp * 8:(hp + 1) * 8, :n], C.ptb[:].rearrange("p (a b) -> p a b", a=8)[:, :, :n],
                         [C.ptb], [qT])
                for g in range(4):
                    if t == 16:
                        for qd_ in range(4):
                            kc, vc = kcs[qd_ % 2], vcs[qd_ % 2]
                            P.dma(kc[:], C.cbk[qd_ * 1024:(qd_ + 1) * 1024, g * 128:(g + 1) * 128].rearrange(
                                "(b p) d -> p b d", p=128), writes=[kc])
                            P.dma(vc[:], C.cbv[qd_ * 1024:(qd_ + 1) * 1024, g * 128:(g + 1) * 128].rearrange(
                                "(b p) d -> p b d", p=128), writes=[vc])
                            P.cp("pool", vca[:, qd_ * 8:(qd_ + 1) * 8, 0:128], vc[:], [vc], [vca])
                            for gg in range(2):
                                for b in range(4):
                                    P.tr(C.ptf[:, b * 128:(b + 1) * 128], kc[:, gg * 4 + b, :], C.ident[:],
                                         [kc, C.ident], [C.ptf])
                                P.cp("dve" if gg % 2 else "act",
                                     kTc[:, qd_ * 1024 + gg * 512: qd_ * 1024 + (gg + 1) * 512], C.ptf[:], [C.ptf], [kTc])

                    def kblk(jb):
                        if t < 16:
                            return kT[:, g, jb * 128:(jb + 1) * 128], vaug[:, jb, g, 0:129], 128, [kT], [vaug]
                        if jb < 32:
                            return kTc[:, jb * 128:(jb + 1) * 128], vca[:, jb, 0:129], 128, [kTc], [vca]
                        return kT[:, g, 2048:2112], vaug[:64, 16, g, 0:129], 64, [kT], [vaug]

                    for jb in range(nb):
                        ka, va, rows, kr, vr = kblk(jb)
                        sc = C.psc[kq % 2]
                        kq += 1
                        P.mm(sc[:rows, 0:4 * n], ka, qT[:, g * 4:(g + 1) * 4, :n], True, True, kr + [qT], [sc])
                        pt = PTt[:rows, jb * 4 * n:(jb + 1) * 4 * n]
                        P.act(pt, sc[:rows, 0:4 * n], AF.Exp, [sc], [PTt], scale=scale)
                        pt3 = pt.rearrange("p (a b) -> p a b", a=4)
                        mT = maskT[:rows, jb * n:(jb + 1) * n].unsqueeze(1).to_broadcast([rows, 4, n])
                        P.tt("dve", pt3, pt3, mT, ALU.mult, [PTt, maskT], [PTt])
                    for r in range(4):
                        po = C.po[r // 2]
                        for jb in range(nb):
                            ka, va, rows, kr, vr = kblk(jb)
                            P.mm(po[:n, (r % 2) * 256:(r % 2) * 256 + 129],
                                 PTt[:rows, jb * 4 * n + r * n: jb * 4 * n + (r + 1) * n], va,
                                 jb == 0, jb == nb - 1, [PTt] + vr, [po])
                    for b in range(2):
                        po = C.po[b]
                        pov = po[:].rearrange("p (a b) -> p a b", a=2)
                        P.op("dve", lambda E, pov=pov, b=b, n=n: E.reciprocal(rr[:n, b * 2:b * 2 + 2], pov[:n, :, 128]),
                             [po], [rr])
                        P.tt("dve", of[:n], pov[:n, :, 0:128],
                             rr[:n, b * 2:b * 2 + 2].unsqueeze(2).to_broadcast([n, 2, 128]), ALU.mult, [po, rr], [of])
                        c0 = g * 512 + b * 256
                        P.tt("dve", og_[:n, c0:c0 + 256], of[:n].rearrange("p a b -> p (a b)"), s_[:n, c0:c0 + 256],
                             ALU.mult, [of, s_], [og_])
                P.dma(C.og[r0:r0 + n, :], og_[:n], reads=[og_])
            P.fence()
    return True


def layer_c1(C, li):
    P = C.P
    lst = C.lst
    hT = C.hT
    ms = P.sb([128, 128], F32, stack=lst)
    P.dma(ms[0:96, :], C.c_mu.rearrange("n (c p) -> (n c) p", p=128), writes=[ms])
    P.dma(ms[96:112, :], C.sshift.rearrange("o (c p) -> (o c) p", p=128), writes=[ms])
    P.tr(C.ptf[:, 0:112], ms[0:112, :], C.ident[:112, :112], [ms, C.ident], [C.ptf])
    mu = P.sb([128, 112], F32, stack=lst)
    om = P.sb([128, 96], F32, stack=lst)
    P.cp("act", mu[:], C.ptf[:, 0:112], [C.ptf], [mu])
    P.ts("dve", om[:], mu[:, 0:96], -1.0, 1.0, ALU.mult, ALU.add, reads=[mu], writes=[om])
    lerpT = P.sb([128, 16, NTOK], BF16, stack=lst)
    tmpb = [P.sb([128, NTOK], BF16, stack=lst) for _ in range(2)]
    wf_bufs = [P.sb([128, 4, 512], F32, stack=lst) for _ in range(2)]
    wbs = [P.sb([128, 16, 512], BF16, stack=lst) for _ in range(2)]
    ev = [P.sb([128, 512], F32, stack=lst) for _ in range(2)]
    ctr = [0]
    kk = 0
    dsts = [C.c_r, C.c_k, C.c_v, C.c_sg]
    c1_loaded = set()

    def c1_load(ix):
        if ix in c1_loaded or ix >= 16:
            return
        c1_loaded.add(ix)
        nn, bb = ix // 4, ix % 4
        load_w_block(C, wf_bufs, wbs[ix % 2], [(C.c_w_rkvg[nn][:, bb * 512:(bb + 1) * 512], 0)], 512, ctr)

    for nidx in range(6):
        for c in range(16):
            tm = tmpb[c % 2]
            col = nidx * 16 + c
            P.ts("dve", tm[:], hT[:, c, :], om[:, col:col + 1], None, ALU.mult, reads=[hT, om], writes=[tm])
            P.stt("dve", lerpT[:, c, 1:NTOK], hT[:, c, 0:NTOK - 1], mu[:, col:col + 1], tm[:, 1:NTOK],
                  ALU.mult, ALU.add, [hT, mu, tm], [lerpT])
            P.cp("dve", lerpT[:, c, 0:1], tm[:, 0:1], [tm], [lerpT])
            P.stt("dve", lerpT[:, c, 2048:2049], mu[:, 96 + c:97 + c], mu[:, col:col + 1], tm[:, 2048:2049],
                  ALU.mult, ALU.add, [mu, tm], [lerpT])
        if nidx < 4:
            for blk in range(4):
                wb = wbs[(nidx * 4 + blk) % 2]
                c1_load(nidx * 4 + blk)
                c1_load(nidx * 4 + blk + 1)
                for t in range(NT):
                    n = tn(t)
                    r0 = t * 128
                    ps = C.pp[t % 2]
                    for c in range(16):
                        P.mm(ps[:n, :], lerpT[:, c, r0:r0 + n], wb[:, c, :], c == 0, c == 15, [lerpT, wb], [ps])
                    e_ = ev[kk % 2]
                    kk += 1
                    if nidx < 3:
                        P.cp("act", e_[:n], ps[:n, :], [ps], [e_])
                    else:
                        P.act(e_[:n], ps[:n, :], AF.Silu, [ps], [e_])
                    P.dma(dsts[nidx][r0:r0 + n, blk * 512:(blk + 1) * 512], e_[:n], reads=[e_])
        else:
            wsrc = C.c_w_la if nidx == 4 else C.c_a_la
            dstT = C.tT if nidx == 4 else C.aT
            wf = wf_bufs[0]
            wb = wbs[0]
            for qtr in range(4):
                P.dma(wf[:, :, 0:96], wsrc[qtr * 512:(qtr + 1) * 512, :].rearrange("(c p) n -> p c n", p=128), writes=[wf])
                P.cp("dve", wb[:, qtr * 4:(qtr + 1) * 4, 0:96], wf[:, :, 0:96], [wf], [wb])
            for tb in range(0, NTOK, 512):
                w = min(512, NTOK - tb)
                ps = C.pp[(tb // 512) % 2]
                for c in range(16):
                    P.mm(ps[:96, :w], wb[:, c, 0:96], lerpT[:, c, tb:tb + w], c == 0, c == 15, [lerpT, wb], [ps])
                if nidx == 4:
                    P.act(dstT[:96, tb:tb + w], ps[:96, :w], AF.Tanh, [ps], [dstT])
                else:
                    P.cp("act", dstT[:96, tb:tb + w], ps[:96, :w], [ps], [dstT])
    return True


def layer_c2(C, li):
    P = C.P
    lst = C.lst
    cb = {}
    for nm in ("c_w0", "c_a0", "c_k_k", "c_k_a", "c_r_k"):
        cb[nm] = P.sb([128, D], F32, stack=lst, name="cb_" + nm)
        P.dma(cb[nm][:], getattr(C, nm)[0:1, :].partition_broadcast(128), writes=[cb[nm]])
    lbf = P.sb([128, D], F32, stack=lst)
    wlb = P.sb([128, D], BF16, stack=lst)
    alb = P.sb([128, D], BF16, stack=lst)
    P.dma(lbf[:96], C.c_w_lb[:, :], writes=[lbf])
    P.cp("dve", wlb[:96], lbf[:96], [lbf], [wlb])
    P.dma(lbf[:96], C.c_a_lb[:, :], writes=[lbf])
    P.cp("dve", alb[:96], lbf[:96], [lbf], [alb])
    tri = P.sb([128, 128], F32, stack=lst)
    P.dma(tri[:], C.cmask[4], writes=[tri])
    selc = P.sb([128, 2], F32, stack=lst)
    P.dma(selc[:], C.selc[:, :], writes=[selc])
    eps12 = P.sb([128, 1], F32, stack=lst)
    B = {nm: P.sb([128, D], F32, stack=lst, name="c2_" + nm) for nm in
         ("R", "K", "V", "A", "W", "KK", "T1", "T2", "CUM", "G")}
    ssq = P.sb([128, 32], F32, stack=lst)
    ob16 = [P.sb([128, D], BF16, stack=lst) for _ in range(2)]
    bs = P.sb([128, 32], F32, stack=lst)
    v3 = lambda tl, n: tl[:n].rearrange("p (a b) -> p a b", a=32)
    for t in range(NT):
        n = tn(t)
        r0 = t * 128
        nc_ = 2 if t < 16 else 1
        R, K, V, A, W, KK, T1, T2, CUM, G = (B[x] for x in ("R", "K", "V", "A", "W", "KK", "T1", "T2", "CUM", "G"))
        P.dma(R[:n], C.c_r[r0:r0 + n, :], writes=[R])
        P.dma(K[:n], C.c_k[r0:r0 + n, :], writes=[K])
        P.dma(V[:n], C.c_v[r0:r0 + n, :], writes=[V])
        for blk in range(4):
            cs = slice(blk * 512, (blk + 1) * 512)
            ps = C.pp[blk % 2]
            P.mm(ps[:n, :], C.tT[:96, r0:r0 + n], wlb[:96, cs], True, True, [C.tT, wlb], [ps])
            P.tt("dve", W[:n, cs], ps[:n, :], cb["c_w0"][:n, cs], ALU.add, [ps, cb["c_w0"]], [W])
            ps2 = C.psc[blk % 2]
            P.mm(ps2[:n, :], C.aT[:96, r0:r0 + n], alb[:96, cs], True, True, [C.aT, alb], [ps2])
            P.tt("dve", A[:n, cs], ps2[:n, :], cb["c_a0"][:n, cs], ALU.add, [ps2, cb["c_a0"]], [A])
        P.act(W[:n], W[:n], AF.Sigmoid, [W], [W])
        P.ts("dve", W[:n], W[:n], -math.exp(-0.5), None, ALU.mult, reads=[W], writes=[W])
        P.act(A[:n], A[:n], AF.Sigmoid, [A], [A])
        P.tt("dve", KK[:n], K[:n], cb["c_k_k"][:n], ALU.mult, [K, cb["c_k_k"]], [KK])
        P.tt("dve", T1[:n], KK[:n], KK[:n], ALU.mult, [KK], [T1])
        P.op("dve", lambda E, n=n, T1=T1: E.reduce_sum(ssq[:n], v3(T1, n), AX.X), [T1], [ssq])
        P.act(ssq[:n], ssq[:n], AF.Sqrt, [ssq], [ssq])
        P.ts("dve", ssq[:n], ssq[:n], 1e-12, None, ALU.max, reads=[ssq], writes=[ssq])
        P.op("dve", lambda E, n=n: E.reciprocal(ssq[:n], ssq[:n]), [ssq], [ssq])
        P.tt("dve", v3(KK, n), v3(KK, n), ssq[:n].unsqueeze(2).to_broadcast([n, 32, 64]), ALU.mult, [KK, ssq], [KK])
        P.stt("dve", T1[:n], A[:n], -1.0, cb["c_k_a"][:n], ALU.add, ALU.mult, [A, cb["c_k_a"]], [T1])
        P.stt("dve", T2[:n], T1[:n], 1.0, K[:n], ALU.add, ALU.mult, [T1, K], [T2])
        P.tt("dve", T1[:n], R[:n], T2[:n], ALU.mult, [R, T2], [T1])
        P.tt("dve", T1[:n], T1[:n], cb["c_r_k"][:n], ALU.mult, [T1, cb["c_r_k"]], [T1])
        P.op("dve", lambda E, n=n, T1=T1: E.reduce_sum(bs[:n], v3(T1, n), AX.X), [T1], [bs])
        P.tt("dve", v3(T1, n), v3(V, n), bs[:n].unsqueeze(2).to_broadcast([n, 32, 64]), ALU.mult, [V, bs], [T1])
        P.dma(C.c_bv[r0:r0 + n, :], T1[:n], reads=[T1])
        for blk in range(4):
            cs = slice(blk * 512, (blk + 1) * 512)
            ps = C.po[blk % 2]
            P.mm(ps[:n, :], tri[:n, :n], W[:n, cs], True, True, [tri, W], [ps])
            P.cp("act", CUM[:n, cs], ps[:n, :], [ps], [CUM])
        for fc in range(16):
            P.mm(C.ptf[:, fc * 2:fc * 2 + nc_], W[:n, fc * 128:(fc + 1) * 128], selc[:n, 0:nc_], True, True,
                 [W, selc], [C.ptf])
        P.act(C.gC[:, :, t * 2:t * 2 + nc_], C.ptf[:, 0:32].rearrange("p (a b) -> p a b", b=2)[:, :, 0:nc_], AF.Exp,
              [C.ptf], [C.gC])
        P.act(G[:n], CUM[:n], AF.Exp, [CUM], [G])
        P.tt("dve", ob16[0][:n], R[:n], G[:n], ALU.mult, [R, G], [ob16[0]])
        P.dma(C.c_Rt[r0:r0 + n, :], ob16[0][:n], reads=[ob16[0]])
        P.act(G[:n], CUM[:n], AF.Exp, [CUM], [G], scale=-1.0)
        P.tt("dve", ob16[1][:n], T2[:n], G[:n], ALU.mult, [T2, G], [ob16[1]])
        P.dma(C.c_Kt[r0:r0 + n, :], ob16[1][:n], reads=[ob16[1]])
        P.tt("dve", A[:n], A[:n], KK[:n], ALU.mult, [A, KK], [A])
        P.tt("dve", ob16[0][:n], A[:n], G[:n], ALU.mult, [A, G], [ob16[0]])
        P.dma(C.c_Bt[r0:r0 + n, :], ob16[0][:n], reads=[ob16[0]])
        P.tt("dve", CUM[:n], CUM[:n], W[:n], ALU.subtract, [CUM, W], [CUM])
        P.act(G[:n], CUM[:n], AF.Exp, [CUM], [G])
        P.stt("dve", ob16[1][:n], KK[:n], -1.0, G[:n], ALU.mult, ALU.mult, [KK, G], [ob16[1]])
        P.dma(C.c_At[r0:r0 + n, :], ob16[1][:n], reads=[ob16[1]])


def layer_c3(C, li):
    P = C.P
    lst = C.lst
    mk = [P.sb([128, 128], F32, stack=lst, name="c3m%d" % i) for i in range(5)]
    for i in range(5):
        P.dma(mk[i][:], C.cmask[i], writes=[mk[i]])
    SEL2f, BDM, MUS, MLS, MUI = mk
    SEL2 = P.sb([128, 128], BF16, stack=lst, name="c3sel2b")
    P.cp("dve", SEL2[:], SEL2f[:], [SEL2f], [SEL2])
    lnw2 = P.sb([128, 16, 64], F32, stack=lst)
    lnb2 = P.sb([128, 16, 64], F32, stack=lst)
    for h in range(2):
        for dst, src in ((lnw2, C.c_ln_w), (lnb2, C.c_ln_b)):
            sv = src[0:1, :].rearrange("o (a h v) -> o a h v", h=2, v=64)[:, :, h, :]
            P.dma(dst[h * 64:(h + 1) * 64], sv.partition_broadcast(64), writes=[dst])
    epsg = P.sb([128, 1], F32, stack=lst)
    P.op("dve", lambda E: E.memset(epsg[:], 64e-5), [], [epsg])
    banks = [C.pp[0], C.pp[1], C.psc[0], C.psc[1], C.po[0], C.po[1], C.ptf]
    bk = [0]

    def bank():
        b = banks[bk[0] % len(banks)]
        bk[0] += 1
        return b

    tok = {nm: [P.sb([128, NT, 128], BF16, stack=lst, name="c3_%s%d" % (nm, i)) for i in range(2)]
           for nm in ("At", "Bt", "Kt", "Rt")}
    hv = {nm: [P.sb([128, 33, 64], F32, stack=lst, name="c3_%s%d" % (nm, i)) for i in range(2)]
          for nm in ("V2", "BV2", "SG2")}
    srcs = {"At": C.c_At, "Bt": C.c_Bt, "Kt": C.c_Kt, "Rt": C.c_Rt, "V2": C.c_v, "BV2": C.c_bv, "SG2": C.c_sg}
    sq = lambda nm: [[P.sb([128, 128], F32, stack=lst, name="c3q_%s%d" % (nm, i)) for i in range(2)]]
    Q = {nm: [P.sb([128, 128], BF16, stack=lst, name="c3q_%s%d" % (nm, i)) for i in range(2)]
         for nm in ("BD_A", "BD_B", "BD_K", "BD_R", "BDT_B", "BDT_K", "N", "NT", "AakT", "WbT", "WkT", "Pm",
                    "Ma", "MTa", "Mb", "MTb")}
    S2s = [P.sb([128, 64], F32, stack=lst) for _ in range(2)]
    S2gs = [P.sb([128, 64], F32, stack=lst) for _ in range(2)]
    Xss = [P.sb([128, 64], BF16, stack=lst) for _ in range(2)]
    Uss = [P.sb([128, 64], BF16, stack=lst) for _ in range(2)]
    cen = [P.sb([128, 64], F32, stack=lst) for _ in range(4)]
    junk = P.sb([128, 64], F32, stack=lst)
    st1 = [P.sb([128, 1], F32, stack=lst) for _ in range(4)]
    st2 = [P.sb([128, 1], F32, stack=lst) for _ in range(4)]
    ogt = [P.sb([128, 64], BF16, stack=lst) for _ in range(4)]
    sws = [P.sb([128, 128], F32, stack=lst) for _ in range(2)]
    stos = [P.sb([128, 128], F32, stack=lst) for _ in range(2)]
    Q2 = [Q, {nm: [P.sb([128, 128], BF16, stack=lst, name="c3r_%s%d" % (nm, i)) for i in range(2)] for nm in Q}]
    S2bs = [P.sb([128, 64], BF16, stack=lst) for _ in range(2)]
    V2bs = [P.sb([128, 33, 64], BF16, stack=lst) for _ in range(2)]
    NHP = int(os.environ.get("MK_NHP", "16"))

    def stream(hp, sx):
        S2, S2g, Xs, Us, sw, sto = S2s[sx], S2gs[sx], Xss[sx], Uss[sx], sws[sx], stos[sx]
        QQ = Q2[sx]
        S2b, V2b = S2bs[sx], V2bs[sx]

        def save_state(which):
            b = bank()
            P.tr(b[:64, 0:128], S2[:, :], C.ident[:, :], [S2, C.ident], [b])
            P.cp("act", sto[:64, :], b[:64, 0:128], [b], [sto])
            P.dma(C.o_cw[which, hp * 128:(hp + 1) * 128, :].rearrange("(h v) k -> v h k", h=2),
                  sto[:64, :].rearrange("v (h k) -> v h k", h=2), reads=[sto])

        fs = slice(hp * 128, (hp + 1) * 128)
        for nm in ("At", "Bt", "Kt", "Rt"):
            tl = tok[nm][sx]
            P.dma(tl[:, 0:16, :], srcs[nm][0:2048, fs].rearrange("(t p) f -> p t f", p=128), writes=[tl])
            P.dma(tl[:64, 16, :], srcs[nm][2048:2112, fs], writes=[tl])
        for nm in ("V2", "BV2", "SG2"):
            tl = hv[nm][sx]
            for h in range(2):
                hs_ = slice(hp * 128 + h * 64, hp * 128 + (h + 1) * 64)
                P.dma(tl[h * 64:(h + 1) * 64, 0:32, :], srcs[nm][0:2048, hs_].rearrange("(c j) v -> j c v", j=64),
                      writes=[tl])
                P.dma(tl[h * 64:(h + 1) * 64, 32, :], srcs[nm][2048:2112, hs_], writes=[tl])
        At, Bt, Kt, Rt = (tok[x][sx] for x in ("At", "Bt", "Kt", "Rt"))
        V2, BV2, SG2 = (hv[x][sx] for x in ("V2", "BV2", "SG2"))
        P.op("dve", lambda E: E.memset(S2[:], 0.0), [], [S2])
        P.op("dve", lambda E: E.memset(S2b[:], 0.0), [], [S2b])
        P.cp("dve", V2b[:], V2[:], [V2], [V2b])
        yield

        def prod(dst, lhsT, rhs, mask, rd):
            b = bank()
            P.mm(b[:, 0:128], lhsT, rhs, True, True, rd, [b])
            P.tt("dve", dst[:], b[:, 0:128], mask[:], ALU.mult, [b, mask], [dst])

        def pre(c):
            k = c % 2
            t, cp = (c // 2, c % 2) if c < 32 else (16, 0)
            rows = slice(cp * 64, cp * 64 + 64)
            q = {nm: QQ[nm][k] for nm in QQ}
            prod(q["BD_A"], At[rows, t, :], SEL2[rows, :], BDM, [At, SEL2])
            prod(q["BD_B"], Bt[rows, t, :], SEL2[rows, :], BDM, [Bt, SEL2])
            yield
            prod(q["BD_K"], Kt[rows, t, :], SEL2[rows, :], BDM, [Kt, SEL2])
            prod(q["BD_R"], Rt[rows, t, :], SEL2[rows, :], BDM, [Rt, SEL2])
            yield
            prod(q["BDT_B"], SEL2[rows, :], Bt[rows, t, :], BDM, [Bt, SEL2])
            prod(q["BDT_K"], SEL2[rows, :], Kt[rows, t, :], BDM, [Kt, SEL2])
            yield
            prod(q["N"], q["BD_B"][:], q["BD_A"][:], MUS, [q["BD_B"], q["BD_A"]])
            prod(q["NT"], q["BD_A"][:], q["BD_B"][:], MLS, [q["BD_B"], q["BD_A"]])
            yield
            prod(q["AakT"], q["BD_K"][:], q["BD_A"][:], MUS, [q["BD_K"], q["BD_A"]])
            prod(q["WbT"], q["BD_B"][:], q["BD_R"][:], MUI, [q["BD_B"], q["BD_R"]])
            prod(q["WkT"], q["BD_K"][:], q["BD_R"][:], MUI, [q["BD_K"], q["BD_R"]])
            yield
            Pm = q["Pm"]
            P.tt("dve", Pm[:], q["N"][:], C.identb[:], ALU.add, [q["N"], C.identb], [Pm])
            M, MT = q["N"], q["NT"]
            alt = [(q["Ma"], q["MTa"]), (q["Mb"], q["MTb"])]
            for lvl in range(5):
                M2, M2T = alt[lvl % 2]
                if lvl < 4:
                    b = bank()
                    P.mm(b[:, 0:128], MT[:], M[:], True, True, [MT, M], [b])
                    P.cp("act", M2[:], b[:, 0:128], [b], [M2])
                b = bank()
                P.mm(b[:, 0:128], M[:], MT[:], True, True, [MT, M], [b])
                P.cp("act", M2T[:], b[:, 0:128], [b], [M2T])
                yield
                b = bank()
                P.mm(b[:, 0:128], M2T[:], Pm[:], True, True, [M2T, Pm], [b])
                P.tt("dve", Pm[:], b[:, 0:128], Pm[:], ALU.add, [b, Pm], [Pm])
                M, MT = M2, M2T
                yield

        def serial(c):
            k = c % 2
            q = {nm: QQ[nm][k] for nm in QQ}
            Pm = q["Pm"]
            Vc = V2b[:, c, :]
            b = bank()
            P.mm(b[:, 0:64], q["BD_A"][:], S2b[:], True, False, [q["BD_A"], S2b], [b])
            P.mm(b[:, 0:64], q["AakT"][:], Vc, False, True, [q["AakT"], V2b], [b])
            P.cp("act", Xs[:], b[:, 0:64], [b], [Xs])
            yield
            b = bank()
            P.mm(b[:, 0:64], Pm[:], Xs[:], True, True, [Pm, Xs], [b])
            P.cp("act", Us[:], b[:, 0:64], [b], [Us])
            yield
            bo = bank()
            P.mm(bo[:, 0:64], q["BD_R"][:], S2b[:], True, False, [q["BD_R"], S2b], [bo])
            P.mm(bo[:, 0:64], q["WbT"][:], Us[:], False, False, [q["WbT"], Us], [bo])
            P.mm(bo[:, 0:64], q["WkT"][:], Vc, False, True, [q["WkT"], V2b], [bo])
            bd = bank()
            P.mm(bd[:, 0:64], q["BDT_B"][:], Us[:], True, False, [q["BDT_B"], Us], [bd])
            P.mm(bd[:, 0:64], q["BDT_K"][:], Vc, False, True, [q["BDT_K"], V2b], [bd])
            gcol = C.gC[:, hp, c:c + 1]
            P.ts("dve", S2g[:], S2[:], gcol, None, ALU.mult, reads=[S2, C.gC], writes=[S2g])
            P.stt("dve", S2[:], bd[:, 0:64], gcol, S2g[:], ALU.mult, ALU.add, [bd, C.gC, S2g], [S2])
            P.cp("dve", S2b[:], S2[:], [S2], [S2b])
            yield
            e = sx * 2 + k
            ce, s1, s2_, og_ = cen[e], st1[e], st2[e], ogt[e]
            P.op("dve", lambda E, bo=bo, s1=s1: E.reduce_sum(s1[:], bo[:, 0:64], AX.X), [bo], [s1])
            P.ts("dve", s1[:], s1[:], -1.0 / 64, None, ALU.mult, reads=[s1], writes=[s1])
            P.ts("dve", ce[:], bo[:, 0:64], s1[:, 0:1], None, ALU.add, reads=[bo, s1], writes=[ce])
            P.tt("dve", junk[:], ce[:], ce[:], ALU.mult, [ce], [junk])
            P.op("dve", lambda E, s2_=s2_: E.reduce_sum(s2_[:], junk[:], AX.X), [junk], [s2_])
            P.act(s2_[:], s2_[:], AF.Ln, [s2_, epsg], [s2_], bias=epsg[:], scale=1.0 / 64)
            P.act(s2_[:], s2_[:], AF.Exp, [s2_], [s2_], scale=-0.5)
            yield
            P.stt("dve", ce[:], ce[:], s2_[:, 0:1], lnw2[:, hp, :], ALU.mult, ALU.mult, [ce, s2_, lnw2], [ce])
            P.tt("pool", ce[:], ce[:], lnb2[:, hp, :], ALU.add, [ce, lnb2], [ce])
            P.tt("pool", ce[:], ce[:], BV2[:, c, :], ALU.add, [ce, BV2], [ce])
            P.tt("pool", og_[:], ce[:], SG2[:, c, :], ALU.mult, [ce, SG2], [og_])
            for h in range(2):
                P.dma(C.og[c * 64:(c + 1) * 64, hp * 128 + h * 64: hp * 128 + (h + 1) * 64], og_[h * 64:(h + 1) * 64, :],
                      reads=[og_])
            yield

        yield from pre(0)
        for c in range(33):
            if c + 1 < 33:
                yield from pre(c + 1)
            if c == 32:
                save_state(0)
                P.dma(sw[:64, :].rearrange("v (h k) -> v h k", h=2),
                      C.swkv[hp * 128:(hp + 1) * 128, :].rearrange("(h v) k -> v h k", h=2), writes=[sw])
                b = bank()
                P.tr(b[:, 0:64], sw[:64, :], C.ident[:64, :64], [sw, C.ident], [b])
                P.cp("act", S2[:], b[:, 0:64], [b], [S2])
                P.cp("dve", S2b[:], S2[:], [S2], [S2b])
            yield from serial(c)
        save_state(1)

    def interleave(gens):
        gens = list(gens)
        while gens:
            for g_ in list(gens):
                try:
                    next(g_)
                except StopIteration:
                    gens.remove(g_)

    for hp0 in range(0, NHP, 2):
        interleave([stream(hp0 + d, d) for d in range(2) if hp0 + d < NHP])
    return True


def phase_out(C, x_src, li):
    P = C.P
    lst = C.lst
    ogT = P.sb([128, 16, NTOK], BF16, stack=lst, name="ogT%d" % li)
    ob = [P.sb([128, D], BF16, stack=lst) for _ in range(2)]
    for t in range(NT):
        n = tn(t)
        o = ob[t % 2]
        P.dma(o[:n], C.og[t * 128:t * 128 + n, :], writes=[o])
        for gi in range(2):
            for jj in range(8):
                c = gi * 8 + jj
                P.tr(C.ptb[:, jj * 128:jj * 128 + n], o[:n, c * 128:(c + 1) * 128], C.identb[:n, :n],
                     [o, C.identb], [C.ptb])
            src = C.ptb[:].rearrange("p (a b) -> p a b", a=8)[:, :, :n]
            P.cp("act" if gi == 0 else "dve", ogT[:, gi * 8:(gi + 1) * 8, t * 128:t * 128 + n], src, [C.ptb], [ogT])
    wf_bufs = [P.sb([128, 4, 512], F32, stack=lst) for _ in range(2)]
    wbs = [P.sb([128, 16, 512], BF16, stack=lst) for _ in range(2)]
    xb = [P.sb([128, 512], F32, stack=lst) for _ in range(3)]
    ctr = [0]
    k = 0
    for blk in range(4):
        wb = wbs[blk % 2]
        load_w_block(C, wf_bufs, wb, [(C.w_out[li][:, blk * 512:(blk + 1) * 512], 0)], 512, ctr)
        for t in range(NT):
            n = tn(t)
            ps = C.pp[t % 2]
            xt = xb[k % 3]
            k += 1
            P.dma(xt[:n], x_src[t * 128:t * 128 + n, blk * 512:(blk + 1) * 512], writes=[xt])
            for c in range(16):
                P.mm(ps[:n, :], ogT[:, c, t * 128:t * 128 + n], wb[:, c, :], c == 0, c == 15, [ogT, wb], [ps])
            P.tt("dve", xt[:n], xt[:n], ps[:n, :], ALU.add, [xt, ps], [xt])
            P.dma(C.xs[t * 128:t * 128 + n, blk * 512:(blk + 1) * 512], xt[:n], reads=[xt])


def phase_final(C, x_src):
    P = C.P
    lst = C.lst
    C.gb = P.sb([128, D], F32, stack=lst)
    P.dma(C.gb[:], C.final_g[0:1, :].partition_broadcast(128), writes=[C.gb])
    xb = [P.sb([128, D], F32, stack=lst) for _ in range(2)]
    junk = P.sb([128, D], BF16, stack=lst)
    ss = [P.sb([128, 1], F32, stack=lst) for _ in range(2)]
    rs = [P.sb([128, 1], F32, stack=lst) for _ in range(2)]
    for t in range(NT):
        n = tn(t)
        xt, s, r = xb[t % 2], ss[t % 2], rs[t % 2]
        P.dma(xt[:n], x_src[t * 128:t * 128 + n, :], writes=[xt])
        P.act(junk[:n], xt[:n], AF.Square, [xt], [junk, s], accum_out=s[:n])
        P.act(r[:n], s[:n], AF.Sqrt, [s, C.eps6], [r], bias=C.eps6[:n], scale=1.0 / D)
        P.op("dve", lambda E, r=r, n=n: E.reciprocal(r[:n], r[:n]), [r], [r])
        P.stt("dve", xt[:n], xt[:n], r[:n, 0:1], C.gb[:n], ALU.mult, ALU.mult, [xt, r, C.gb], [xt])
        P.dma(C.y[t * 128:t * 128 + n, :], xt[:n], reads=[xt])


def const_masks():
    i = np.arange(128)
    same = (i[:, None] // 64) == (i[None, :] // 64)
    s_, t_ = i[:, None] % 64, i[None, :] % 64
    sel2 = (s_ == t_)
    m = np.stack([sel2, same, same & (s_ < t_), same & (s_ > t_), same & (s_ <= t_)]).astype(np.float32)
    return np.ascontiguousarray(m)


def const_selc():
    i = np.arange(128)
    return np.ascontiguousarray(((i[:, None] // 64) == np.arange(2)[None, :]).astype(np.float32))


def rope_table(dh):
    half = dh // 2
    pos = np.concatenate([np.arange(2048), PAST + np.arange(64)]).astype(np.float32)
    inv = np.power(np.float32(10000.0), -np.arange(half, dtype=np.float32) * np.float32(2.0 / dh)).astype(np.float32)
    ang = pos[:, None] * inv[None, :]
    cos = np.cos(ang).astype(np.float32)
    sin = np.sin(ang).astype(np.float32)
    tab = np.stack([np.concatenate([cos, cos], 1), np.concatenate([-sin, sin], 1)], 1)
    return np.ascontiguousarray(tab.astype(np.float32))


_NC_CACHE = {}


def kernel(x_prompt, x_sample, cache_a_k, cache_a_v, cache_b_k, cache_b_v, cache_b_kidx, state_c_wkv, state_c_shift,
           norm_g, final_g, w_out, a_w_in, a_lam, a_subln_g, b_w_in, c_mu, c_w_rkvg, c_w0, c_w_la, c_w_lb, c_a0,
           c_a_la, c_a_lb, c_k_k, c_k_a, c_r_k, c_ln_w, c_ln_b):
    stage = int(os.environ.get("MK_STAGE", "4"))
    skey = (stage, os.environ.get("MK_HEADS"), os.environ.get("MK_NOATT"), os.environ.get("MK_NOSAMP"), os.environ.get("MK_B"), os.environ.get("MK_BK"), os.environ.get("MK_NHP"))
    ncores = int(os.environ.get("MK_CORES", "8"))
    f = lambda a: np.ascontiguousarray(np.asarray(a, dtype=np.float32))
    if skey not in _NC_CACHE:
        _NC_CACHE[skey] = build_nc(stage)
    nc = _NC_CACHE[skey]
    shared = {
        "norm_g": f(norm_g), "final_g": f(final_g).reshape(1, D), "w_out": f(w_out), "a_w_in": f(a_w_in),
        "a_lam": f(a_lam).reshape(2, 256), "a_subln_g": f(a_subln_g), "b_w_in": f(b_w_in)[0],
        "c_mu": f(c_mu)[0], "c_w_rkvg": f(c_w_rkvg)[0], "c_w0": f(c_w0), "c_w_la": f(c_w_la)[0],
        "c_w_lb": f(c_w_lb)[0], "c_a0": f(c_a0), "c_a_la": f(c_a_la)[0], "c_a_lb": f(c_a_lb)[0],
        "c_k_k": f(c_k_k), "c_k_a": f(c_k_a), "c_r_k": f(c_r_k).reshape(1, D), "c_ln_w": f(c_ln_w),
        "c_ln_b": f(c_ln_b), "rope64": rope_table(64), "rope128": rope_table(128),
        "ident": np.eye(128, dtype=np.float32), "cmask": const_masks(), "selc": const_selc(),
    }
    in_maps = []
    for c in range(ncores):
        m = dict(shared)
        m["xin"] = np.concatenate([f(x_prompt[c // 2]), f(x_sample[c])], 0)
        m["cak"] = f(cache_a_k[:, c]).reshape(2, PAST, D)
        m["cav"] = f(cache_a_v[:, c]).reshape(2, PAST, D)
        m["cbk"] = f(cache_b_k[0, c]).reshape(PAST, 512)
        m["cbv"] = f(cache_b_v[0, c]).reshape(PAST, 512)
        m["cbi"] = f(cache_b_kidx[0, c])
        m["swkv"] = f(state_c_wkv[0, c]).reshape(2048, 64)
        m["sshift"] = f(state_c_shift[0, c]).reshape(1, D)
        in_maps.append(m)
    tr = bool(int(os.environ.get('MK_TRACE', '0')))
    res = run_bass_kernel_spmd(nc, in_maps, core_ids=list(range(ncores)), **({'trace': True} if tr else {}))
    if tr:
        print('EXEC_NS', res.exec_time_ns)
        try:
            import ast, collections
            insts = res.instructions_and_trace[0]
            src = open(__file__).read()
            funcs = [(n.lineno, n.end_lineno, n.name) for n in ast.parse(src).body if isinstance(n, ast.FunctionDef)]
            def fn_of(line):
                for a, b, nm in funcs:
                    if a <= line <= b:
                        return nm
                return '?'
            t0 = min(i.timestamp for i in insts)
            t1 = max(i.end_timestamp for i in insts)
            NBK = 48
            w = (t1 - t0) / NBK
            busy = collections.defaultdict(lambda: [0.0] * NBK)
            fnb = [collections.Counter() for _ in range(NBK)]
            for i in insts:
                if i.is_seq_only:
                    continue
                b = min(NBK - 1, int((i.timestamp - t0) / w))
                busy[str(i.engine)][b] += i.duration
                fnb[b][fn_of(i.source_line)] += i.duration
            print('TOTAL ms', (t1 - t0) / 1e6, 'bucket us', w / 1e3)
            for e, v in busy.items():
                print('%-10s tot %5.1f%% | ' % (e[:10], 100 * sum(v) / (t1 - t0)) + ' '.join('%2d' % min(99, int(100 * x / w)) for x in v))
            print('phase: ' + ' '.join((c.most_common(1)[0][0][-2:] if c else '--') for c in fnb))
        except Exception as ex:
            print('trace summary failed', ex)
    R = res.results
    nb = 4
    y_p = np.zeros((4, 2048, D), np.float32)
    y_s = np.zeros((8, 64, D), np.float32)
    akp = np.zeros((2, 4, 2048, 16, 128), np.float32)
    avp = np.zeros_like(akp)
    aks = np.zeros((2, 8, 64, 16, 128), np.float32)
    avs = np.zeros_like(aks)
    bkp = np.zeros((1, 4, 2048, 4, 128), np.float32)
    bvp = np.zeros_like(bkp)
    bip = np.zeros((1, 4, 2048, 64), np.float32)
    bks = np.zeros((1, 8, 64, 4, 128), np.float32)
    bvs = np.zeros_like(bks)
    bis = np.zeros((1, 8, 64, 64), np.float32)
    cwp = np.zeros((1, 4, 32, 64, 64), np.float32)
    csp = np.zeros((1, 4, D), np.float32)
    cws = np.zeros((1, 8, 32, 64, 64), np.float32)
    css = np.zeros((1, 8, D), np.float32)
    for c in range(ncores):
        r = R[c]
        p = c // 2
        if c % 2 == 0:
            y_p[p] = r["y"][:2048]
            akp[:, p] = r["o_ak"][:, :2048].reshape(2, 2048, 16, 128)
            avp[:, p] = r["o_av"][:, :2048].reshape(2, 2048, 16, 128)
            bkp[0, p] = r["o_bk"][:2048].reshape(2048, 4, 128)
            bvp[0, p] = r["o_bv"][:2048].reshape(2048, 4, 128)
            bip[0, p] = r["o_bi"][:2048]
            cwp[0, p] = r["o_cw"][0].reshape(32, 64, 64)
            csp[0, p] = r["o_cs"][0]
        y_s[c] = r["y"][2048:]
        aks[:, c] = r["o_ak"][:, 2048:].reshape(2, 64, 16, 128)
        avs[:, c] = r["o_av"][:, 2048:].reshape(2, 64, 16, 128)
        bks[0, c] = r["o_bk"][2048:].reshape(64, 4, 128)
        bvs[0, c] = r["o_bv"][2048:].reshape(64, 4, 128)
        bis[0, c] = r["o_bi"][2048:]
        cws[0, c] = r["o_cw"][1].reshape(32, 64, 64)
        css[0, c] = r["o_cs"][1]
    return (y_p, y_s, akp, avp, aks, avs, bkp, bvp, bip, bks, bvs, bis, cwp, csp, cws, css)
```
